# Optimizing a Trainium2 kernel written in Bass

```python
import math
import jax, jax.numpy as jnp
from jax import lax
import numpy as np

D_MODEL = 1024
BATCH = 8
SEQ = 2048
DEPTH = 1
DEC_BATCH = 16
DEC_SEQ = 2048
PAST_LEN = 128

MIX_WIDTH = D_MODEL
RET_WIDTH = MIX_WIDTH // 2
SSM_WIDTH = MIX_WIDTH - RET_WIDTH
RET_HEADS = 4
RET_HEAD_DIM = RET_WIDTH // RET_HEADS
RET_CHUNK = 128
ROPE_BASE = 10000.0
SSM_GROUP = 16
SSM_GROUPS = SSM_WIDTH // SSM_GROUP
SSM_STATE = 64
DT_MIN = 1e-3
DT_MAX = 1e-1
D_FF = 2816
CONV_WIDTH = 3
EPS = 1e-6
IN_WIDTH = 4 * RET_WIDTH + SSM_WIDTH

kernel_name = "hybrid_retention_s5_encoder"


def _rms_norm(x, g):
    xf = x.astype(jnp.float32)
    y = xf * lax.rsqrt(jnp.mean(xf * xf, axis=-1, keepdims=True) + EPS)
    return (y * g.astype(jnp.float32)).astype(x.dtype)


def _rotary(x, pos):
    half = x.shape[-1] // 2
    inv_freq = ROPE_BASE ** (-jnp.arange(half, dtype=jnp.float32) / half)
    ang = pos[:, None] * inv_freq[None, :]
    cos = jnp.cos(ang)[None, :, None, :]
    sin = jnp.sin(ang)[None, :, None, :]
    x1, x2 = x[..., :half], x[..., half:]
    return jnp.concatenate([x1 * cos - x2 * sin, x1 * sin + x2 * cos], axis=-1)


def _retention_one_direction(q, k, v, log_gamma, strict):
    b, h, l, dh = q.shape
    n = l // RET_CHUNK
    c = RET_CHUNK
    qc = q.reshape(b, h, n, c, dh)
    kc = k.reshape(b, h, n, c, dh)
    vc = v.reshape(b, h, n, c, dh)
    idx = jnp.arange(c, dtype=jnp.float32)
    diff = idx[:, None] - idx[None, :]
    mask = (diff > 0) if strict else (diff >= 0)
    lg = log_gamma[:, None, None]
    inner_decay = jnp.where(mask[None], jnp.exp(lg * jnp.maximum(diff, 0.0)[None]), 0.0)
    scores = jnp.einsum('bhncd,bhnmd->bhncm', qc, kc) * inner_decay[None, :, None]
    inner = jnp.einsum('bhncm,bhnme->bhnce', scores, vc)
    zeta = jnp.exp(log_gamma[:, None] * (c - 1.0 - idx)[None, :])
    kv = jnp.einsum('bhncd,bhnce->nbhde', kc * zeta[None, :, None, :, None], vc)
    chunk_decay = jnp.exp(log_gamma * c)[None, :, None, None]

    def step(r, kv_n):
        return chunk_decay * r + kv_n, r

    _, r_prev = lax.scan(step, jnp.zeros((b, h, dh, dh), jnp.float32), kv)
    xi = jnp.exp(log_gamma[:, None] * (idx + 1.0)[None, :])
    cross = jnp.einsum('bhncd,nbhde->bhnce', qc, r_prev) * xi[None, :, None, :, None]
    return (inner + cross).reshape(b, h, l, dh)


def _retention(q, k, v, g, gn_gain):
    b, l, _ = q.shape
    pos = jnp.arange(l, dtype=jnp.float32)
    qh = _rotary(q.astype(jnp.float32).reshape(b, l, RET_HEADS, RET_HEAD_DIM), pos)
    kh = _rotary(k.astype(jnp.float32).reshape(b, l, RET_HEADS, RET_HEAD_DIM), pos) * (RET_HEAD_DIM ** -0.5)
    vh = v.astype(jnp.float32).reshape(b, l, RET_HEADS, RET_HEAD_DIM)
    qh, kh, vh = (t.transpose(0, 2, 1, 3) for t in (qh, kh, vh))
    log_gamma = jnp.log(1.0 - 2.0 ** (-5.0 - jnp.arange(RET_HEADS, dtype=jnp.float32)))
    fwd = _retention_one_direction(qh, kh, vh, log_gamma, False)
    bwd = jnp.flip(_retention_one_direction(jnp.flip(qh, 2), jnp.flip(kh, 2), jnp.flip(vh, 2),
                                            log_gamma, True), 2)
    o = (fwd + bwd).transpose(0, 2, 1, 3)
    o = o * lax.rsqrt(jnp.mean(o * o, axis=-1, keepdims=True) + EPS)
    o = o.reshape(b, l, RET_WIDTH) * gn_gain.astype(jnp.float32)
    return (jax.nn.silu(g.astype(jnp.float32)) * o).astype(q.dtype)


def _complex_scan_combine(e1, e2):
    a1r, a1i, b1r, b1i = e1
    a2r, a2i, b2r, b2i = e2
    ar = a2r * a1r - a2i * a1i
    ai = a2r * a1i + a2i * a1r
    br = a2r * b1r - a2i * b1i + b2r
    bi = a2r * b1i + a2i * b1r + b2i
    return (ar, ai, br, bi)


def _s5(u, lam_re, lam_im, log_dt, b_re, b_im, c_re, c_im, d_skip, glu_w, glu_b):
    bsz, l, _ = u.shape
    uf = u.astype(jnp.float32)
    ug = uf.reshape(bsz, l, SSM_GROUPS, SSM_GROUP)
    y = d_skip.astype(jnp.float32) * uf
    for d in range(2):
        lr = jnp.minimum(lam_re[d].astype(jnp.float32), -1e-4)
        li = lam_im[d].astype(jnp.float32)
        dt = jnp.exp(log_dt[d].astype(jnp.float32))[:, None]
        mag = jnp.exp(lr * dt)
        ar = mag * jnp.cos(li * dt)
        ai = mag * jnp.sin(li * dt)
        den = lr * lr + li * li
        nr = ar - 1.0
        ni = ai
        cr = (nr * lr + ni * li) / den
        ci = (ni * lr - nr * li) / den
        br_ = b_re[d].astype(jnp.float32)
        bi_ = b_im[d].astype(jnp.float32)
        bbr = cr[..., None] * br_ - ci[..., None] * bi_
        bbi = cr[..., None] * bi_ + ci[..., None] * br_
        bu_r = jnp.einsum('blgc,gnc->blgn', ug, bbr)
        bu_i = jnp.einsum('blgc,gnc->blgn', ug, bbi)
        a_r = jnp.broadcast_to(ar, bu_r.shape)
        a_i = jnp.broadcast_to(ai, bu_i.shape)
        _, _, xr, xi = lax.associative_scan(_complex_scan_combine, (a_r, a_i, bu_r, bu_i),
                                            axis=1, reverse=(d == 1))
        yd = (jnp.einsum('blgn,gcn->blgc', xr, c_re[d].astype(jnp.float32))
              - jnp.einsum('blgn,gcn->blgc', xi, c_im[d].astype(jnp.float32)))
        y = y + yd.reshape(bsz, l, SSM_WIDTH)
    y = jax.nn.gelu(y)
    gate = jax.nn.sigmoid(y @ glu_w.astype(jnp.float32) + glu_b.astype(jnp.float32))
    return (y * gate).astype(u.dtype)


def _conv_ffn(h, w_up, conv_w, conv_b, w_down):
    z = h @ w_up
    zp = jnp.pad(z, ((0, 0), (1, 1), (0, 0)))
    z = zp[:, :-2] * conv_w[0] + zp[:, 1:-1] * conv_w[1] + zp[:, 2:] * conv_w[2] + conv_b
    val, gate = jnp.split(z, 2, axis=-1)
    return (jax.nn.gelu(gate) * val) @ w_down


def _trunk(x, norm_mix, w_in, ret_gn_gain, s5_lambda_re, s5_lambda_im, s5_log_dt,
           s5_B_re, s5_B_im, s5_C_re, s5_C_im, s5_D, s5_glu_w, s5_glu_b, w_out,
           norm_ffn, w_up, conv_w, conv_b, w_down, norm_final):
    for i in range(DEPTH):
        h = _rms_norm(x, norm_mix[i])
        proj = h @ w_in[i]
        q, k, v, g, u = jnp.split(proj, [RET_WIDTH, 2 * RET_WIDTH, 3 * RET_WIDTH, 4 * RET_WIDTH], axis=-1)
        ret = _retention(q, k, v, g, ret_gn_gain[i])
        ssm = _s5(u, s5_lambda_re[i], s5_lambda_im[i], s5_log_dt[i], s5_B_re[i], s5_B_im[i],
                  s5_C_re[i], s5_C_im[i], s5_D[i], s5_glu_w[i], s5_glu_b[i])
        x = x + jnp.concatenate([ret, ssm], axis=-1) @ w_out[i]
        h = _rms_norm(x, norm_ffn[i])
        x = x + _conv_ffn(h, w_up[i], conv_w[i], conv_b[i], w_down[i])
    return _rms_norm(x, norm_final)


def setup_inputs(seed: int = 0) -> dict:
    key = jax.random.key(seed)
    ks = jax.random.split(key, 24)
    f32 = jnp.float32
    nrm = lambda k, s, sc: jax.random.normal(k, s, f32) * sc
    n_idx = jnp.arange(SSM_STATE, dtype=f32)
    sdir = (DEPTH, 2, SSM_GROUPS)
    return {
        "x_prompt": jax.random.normal(ks[0], (BATCH, SEQ, D_MODEL), f32),
        "x_sample": jax.random.normal(ks[1], (DEC_BATCH, DEC_SEQ, D_MODEL), f32),
        "norm_mix": 1.0 + nrm(ks[2], (DEPTH, D_MODEL), 0.02),
        "w_in": nrm(ks[3], (DEPTH, D_MODEL, IN_WIDTH), D_MODEL ** -0.5),
        "ret_gn_gain": 1.0 + nrm(ks[4], (DEPTH, RET_WIDTH), 0.02),
        "s5_lambda_re": -0.5 + nrm(ks[5], sdir + (SSM_STATE,), 0.01),
        "s5_lambda_im": math.pi * n_idx + nrm(ks[6], sdir + (SSM_STATE,), 0.01),
        "s5_log_dt": jax.random.uniform(ks[7], sdir, f32, math.log(DT_MIN), math.log(DT_MAX)),
        "s5_B_re": nrm(ks[8], sdir + (SSM_STATE, SSM_GROUP), (2 * SSM_GROUP) ** -0.5),
        "s5_B_im": nrm(ks[9], sdir + (SSM_STATE, SSM_GROUP), (2 * SSM_GROUP) ** -0.5),
        "s5_C_re": nrm(ks[10], sdir + (SSM_GROUP, SSM_STATE), SSM_STATE ** -0.5),
        "s5_C_im": nrm(ks[11], sdir + (SSM_GROUP, SSM_STATE), SSM_STATE ** -0.5),
        "s5_D": nrm(ks[12], (DEPTH, SSM_WIDTH), 1.0),
        "s5_glu_w": nrm(ks[13], (DEPTH, SSM_WIDTH, SSM_WIDTH), SSM_WIDTH ** -0.5),
        "s5_glu_b": nrm(ks[14], (DEPTH, SSM_WIDTH), 0.01),
        "w_out": nrm(ks[15], (DEPTH, MIX_WIDTH, D_MODEL), MIX_WIDTH ** -0.5),
        "norm_ffn": 1.0 + nrm(ks[16], (DEPTH, D_MODEL), 0.02),
        "w_up": nrm(ks[17], (DEPTH, D_MODEL, 2 * D_FF), D_MODEL ** -0.5),
        "conv_w": nrm(ks[18], (DEPTH, CONV_WIDTH, 2 * D_FF), CONV_WIDTH ** -0.5),
        "conv_b": nrm(ks[19], (DEPTH, 2 * D_FF), 0.01),
        "w_down": nrm(ks[20], (DEPTH, D_FF, D_MODEL), D_FF ** -0.5),
        "norm_final": 1.0 + nrm(ks[21], (D_MODEL,), 0.02),
    }


def reference(x_prompt, x_sample, norm_mix, w_in, ret_gn_gain, s5_lambda_re, s5_lambda_im,
              s5_log_dt, s5_B_re, s5_B_im, s5_C_re, s5_C_im, s5_D, s5_glu_w, s5_glu_b,
              w_out, norm_ffn, w_up, conv_w, conv_b, w_down, norm_final):
    y_prompt = _trunk(x_prompt, norm_mix, w_in, ret_gn_gain, s5_lambda_re, s5_lambda_im,
                      s5_log_dt, s5_B_re, s5_B_im, s5_C_re, s5_C_im, s5_D, s5_glu_w, s5_glu_b,
                      w_out, norm_ffn, w_up, conv_w, conv_b, w_down, norm_final)
    y_sample = _trunk(x_sample, norm_mix, w_in, ret_gn_gain, s5_lambda_re, s5_lambda_im,
                      s5_log_dt, s5_B_re, s5_B_im, s5_C_re, s5_C_im, s5_D, s5_glu_w, s5_glu_b,
                      w_out, norm_ffn, w_up, conv_w, conv_b, w_down, norm_final)
    return (y_prompt, y_sample)
```

```python
import math
import contextlib
import numpy as np
import concourse.bass as bass
import concourse.mybir as mybir
from concourse.alu_op_type import AluOpType as ALU
from concourse.bass_utils import run_bass_kernel_spmd

F32 = mybir.dt.float32
BF16 = mybir.dt.bfloat16
AF = mybir.ActivationFunctionType

EPS = 1e-6
L = 2048
DM = 1024
NSEQ = 3
NCORES = 8
DFF = 2816
NCT = 22
LN_G = [math.log(1.0 - 2.0 ** (-5.0 - h)) for h in range(4)]
G128 = [math.exp(128.0 * lg) for lg in LN_G]
SC = 128.0 ** -0.5
TWO_PI = 2.0 * math.pi
MAGIC = 12582912.0
SIN_SCALE = 6.283185
WIN_COLS = 3584


class Sched:
    ENG = ("pe", "act", "dve", "pool", "sp")

    def __init__(self, nc):
        self.nc = nc
        self.streams = {e: [] for e in self.ENG}
        self.cnt = {}
        self.lastw = {}
        self.readers = {}
        self.waited = {e: {} for e in self.ENG}
        self.last_marker = {}
        self.pending = {e: [] for e in self.ENG}
        self.LIM = 32000

    def barrier(self):
        ms = list(self.last_marker.values())
        for e in self.ENG:
            self.pending[e] = list(ms)

    def op(self, eng, meth, kwargs, reads=(), writes=(), chan=None):
        base = chan if chan is not None else eng
        step = 16 if chan is not None else 1
        c = self.cnt.get(base, 0)
        epoch = c // self.LIM
        newc = c + step
        if newc > (epoch + 1) * self.LIM:
            epoch += 1
            c = epoch * self.LIM
            newc = c + step
        self.cnt[base] = newc
        sk = (base, epoch)
        marker = (sk, newc - epoch * self.LIM, eng if chan is None else None)
        deps = list(self.pending[eng])
        self.pending[eng] = []
        for t in reads:
            if t in self.lastw:
                deps.append(self.lastw[t])
        for t in writes:
            if t in self.lastw:
                deps.append(self.lastw[t])
            deps.extend(self.readers.get(t, ()))
        wd = self.waited[eng]
        m = {}
        for (k, v, de) in deps:
            if de == "pe" and eng == "pe" and chan is None:
                continue
            if wd.get(k, 0) >= v:
                continue
            m[k] = max(m.get(k, 0), v)
        for k, v in m.items():
            wd[k] = v
        self.streams[eng].append((meth, kwargs, list(m.items()), sk, step))
        for t in writes:
            self.lastw[t] = marker
            self.readers[t] = []
        for t in reads:
            self.readers.setdefault(t, []).append(marker)
        self.last_marker[base] = marker
        return marker

    def emit(self):
        nc = self.nc
        semkeys = set()
        for e in self.ENG:
            for (meth, kwargs, waits, sk, step) in self.streams[e]:
                semkeys.add(sk)
                for (k, v) in waits:
                    semkeys.add(k)
        semkeys = sorted(semkeys, key=str)
        with contextlib.ExitStack() as st:
            sems = {}
            for i, k in enumerate(semkeys):
                sems[k] = st.enter_context(nc.semaphore("s%d" % i))
            block = st.enter_context(nc.Block())
            engmap = {"pe": "tensor", "act": "scalar", "dve": "vector", "pool": "gpsimd", "sp": "sync"}

            def mk(e):
                def body(eng):
                    for (meth, kwargs, waits, sk, step) in self.streams[e]:
                        for (k, v) in waits:
                            eng.wait_ge(sems[k], v)
                        inst = getattr(eng, meth)(**kwargs)
                        inst.then_inc(sems[sk], step)
                return body

            for e in self.ENG:
                getattr(block, engmap[e])(mk(e))


class Mem:
    def __init__(self, big, cap):
        self.big = big
        self.cap = cap
        self.top = 0
        self.hw = 0

    def alloc(self, nbytes, dt=BF16, pattern=None, **kw):
        nbytes = (nbytes + 63) // 64 * 64
        off = self.top
        self.top += nbytes
        self.hw = max(self.hw, self.top)
        assert self.top <= self.cap, ("SBUF overflow", self.top, self.cap)
        ap = self.big[:, off // 2:(off + nbytes) // 2]
        if dt is F32:
            ap = ap.bitcast(F32)
        if pattern is not None:
            ap = ap.rearrange(pattern, **kw)
        return ap


def build_program(debug=None):
    nc = bass.Bass("TRN2", target_bir_lowering=False)
    DEBUG = {}

    def din(name, shape, dt=F32):
        return nc.dram_tensor(name, list(shape), dt, kind="ExternalInput").ap()

    x_d = din("x", [NSEQ, L, DM])
    y_d = nc.dram_tensor("y", [NSEQ, L, DM], F32, kind="ExternalOutput").ap()
    w_in_d = din("w_in_a", [DM, WIN_COLS])
    w_out_d = din("w_out", [DM, DM])
    w_up_d = din("w_up", [DM, 2 * DFF])
    w_down_d = din("w_down", [DFF, DM])
    glu_w_d = din("glu_w", [512, 512])
    nmix_d = din("nmix_c", [128, 8])
    nffn_d = din("nffn_c", [128, 8])
    nfin_d = din("nfin_r", [1, DM])
    gain_d = din("gain_c", [128, 4])
    glub_d = din("glub_c", [128, 4])
    cw_d = din("cw_c", [128, 44 * 3])
    cb_d = din("cb_c", [128, 44])
    lre2_d = din("lre2", [128, 32])
    lim2_d = din("lim2", [128, 32])
    ldt2_d = din("ldt2", [128, 32])
    bre2_d = din("bre2", [128, 512])
    bim2_d = din("bim2", [128, 512])
    cre2_d = din("cre2", [128, 512])
    cim2_d = din("cim2", [128, 512])
    drep_d = din("drep", [128, 32])

    wI = nc.dram_tensor("wI_s", [128, 8 * WIN_COLS], BF16).ap()
    wO = nc.dram_tensor("wO_s", [DM, DM], BF16).ap()
    wU = nc.dram_tensor("wU_s", [128, NCT * 2048], BF16).ap()
    wD = nc.dram_tensor("wD_s", [DFF, DM], BF16).ap()
    s5Wtoep = nc.dram_tensor("s5Wtoep_s", [128, 4096], BF16).ap()
    s5Wsum = nc.dram_tensor("s5Wsum_s", [128, 8192], BF16).ap()
    s5Wfar = nc.dram_tensor("s5Wfar_s", [128, 8192], BF16).ap()

    dbg_out = {}
    CAP = 207 * 1024
    with contextlib.ExitStack() as st:
        big = st.enter_context(nc.sbuf_tensor("big", [128, CAP // 2], BF16))
        pbs = [st.enter_context(nc.psum_tensor("pb%d" % i, [128, 512], F32)) for i in range(8)]
        S = Sched(nc)
        M = Mem(big, CAP)

        def pbf(i):
            return pbs[i][:]

        def pbb(i):
            return pbs[i][:].bitcast(BF16)

        def DMA(chan, out, in_, reads=(), writes=(), eng="sp"):
            return S.op(eng, "dma_start", dict(out=out, in_=in_), reads, writes, chan=chan)

        def ACT(out, in_, func, reads, writes, **kw):
            return S.op("act", "activation", dict(out=out, in_=in_, func=func, **kw), reads, writes)

        def STT(out, in0, scalar, in1, op0, op1, reads, writes):
            return S.op("dve", "scalar_tensor_tensor", dict(out=out, in0=in0, scalar=scalar, in1=in1, op0=op0, op1=op1), reads, writes)

        def TT(eng, out, in0, in1, op, reads, writes):
            return S.op(eng, "tensor_tensor", dict(out=out, in0=in0, in1=in1, op=op), reads, writes)

        def TS(eng, out, in0, s1, s2, op0, op1, reads, writes):
            if s2 is None:
                return S.op(eng, "tensor_scalar", dict(out=out, in0=in0, scalar1=s1, scalar2=None, op0=op0), reads, writes)
            return S.op(eng, "tensor_scalar", dict(out=out, in0=in0, scalar1=s1, scalar2=s2, op0=op0, op1=op1), reads, writes)

        def TSS(eng, out, in_, scalar, op, reads, writes):
            return S.op(eng, "tensor_single_scalar", dict(out=out, in_=in_, scalar=scalar, op=op), reads, writes)

        def CP(eng, out, in_, reads, writes):
            return S.op(eng, "tensor_copy", dict(out=out, in_=in_), reads, writes)

        def MM(out, lhsT, rhs, start, stop, reads, writes):
            return S.op("pe", "matmul", dict(out=out, lhsT=lhsT, rhs=rhs, start=start, stop=stop), reads, writes)

        def TR(out, in_, reads, writes):
            return S.op("pe", "transpose", dict(out=out, in_=in_, identity=ident), list(reads) + ["ident"], writes)

        def RCP(out, in_, reads, writes):
            return S.op("dve", "reciprocal", dict(out=out, in_=in_), reads, writes)

        def NOP(eng, reads, writes=()):
            return S.op(eng, "nop", dict(), reads, writes)

        ident = M.alloc(256)
        onesm = M.alloc(256)
        Pm = M.alloc(256)
        COS = M.alloc(4096)
        SINS = M.alloc(4096)
        Dm = M.alloc(2048, F32, "p (h n) -> p h n", h=4)
        XIF = M.alloc(2048, F32, "p (h n) -> p h n", h=4)
        XIB = M.alloc(2048, F32, "p (h n) -> p h n", h=4)
        ZF = M.alloc(64, F32)
        ZB = M.alloc(64, F32)
        cst = M.alloc(128, F32)
        gain = M.alloc(64, F32)
        glub = M.alloc(64, F32)
        gluw = M.alloc(4096, BF16, "p (k n) -> p k n", k=4)
        cw = M.alloc(44 * 3 * 4, F32)[:, 0:132].rearrange("p (c k) -> p c k", k=3)
        cb = M.alloc(44 * 4, F32)
        nfbc = M.alloc(4096, F32)
        MPr = M.alloc(20 * 128, F32, "p (q g) -> p q g", q=20)
        MPi = M.alloc(20 * 128, F32, "p (q g) -> p q g", q=20)
        MPB = M.alloc(20 * 256, F32, "p (q r g) -> p q r g", q=20, r=2)
        mixT_flat = M.alloc(32768, BF16)
        mixT = mixT_flat.rearrange("p (k t) -> p k t", k=8)
        stat = M.alloc(6 * 16 * 4, F32, "p (a i) -> p a i", a=6)
        stat3 = M.alloc(64, F32)
        PBASE = M.top

        CE, CLSC = 0, 1
        cvals = {CE: EPS, CLSC: math.log(SC)}
        for h in range(4):
            cvals[4 + h] = LN_G[h]
            cvals[8 + h] = 128.0 * LN_G[h]
            cvals[12 + h] = 127.0 * LN_G[h] + math.log(SC)
        for k, v in cvals.items():
            S.op("pool", "memset", dict(ap=cst[:, k:k + 1], constant=float(v)), (), [("cst", k)])

        M.top = PBASE
        identf = M.alloc(512, F32)
        pidx = M.alloc(64, F32)
        hi = M.alloc(64, F32)
        imod = M.alloc(64, F32)
        invf = M.alloc(64, F32)
        sgn = M.alloc(64, F32)
        nd = M.alloc(512, F32)
        nidx = M.alloc(512, F32)
        SMALL_END = M.top
        lidx = M.alloc(8192, F32)
        ta = M.alloc(8192, F32)
        tb_ = M.alloc(8192, F32)
        tc_ = M.alloc(8192, F32)

        S.op("pool", "memset", dict(ap=identf, constant=0.0), (), ["identf"])
        S.op("pool", "affine_select", dict(out=identf, in_=identf, pattern=[[-1, 128]], compare_op=ALU.not_equal,
                                           fill=1.0, base=0, channel_multiplier=1), ["identf"], ["identf"])
        CP("dve", ident, identf, ["identf"], ["ident"])
        Pmf = M.alloc(512, F32)
        S.op("pool", "memset", dict(ap=Pmf, constant=0.0), (), ["Pmf"])
        S.op("pool", "affine_select", dict(out=Pmf, in_=Pmf, pattern=[[-1, 128]], compare_op=ALU.not_equal,
                                           fill=1.0, base=-64, channel_multiplier=1), ["Pmf"], ["Pmf"])
        S.op("pool", "affine_select", dict(out=Pmf, in_=Pmf, pattern=[[-1, 128]], compare_op=ALU.not_equal,
                                           fill=1.0, base=64, channel_multiplier=1), ["Pmf"], ["Pmf"])
        CP("dve", Pm, Pmf, ["Pmf"], ["Pm"])
        S.op("pool", "memset", dict(ap=onesm, constant=1.0 / 128.0), (), ["onesm"])
        S.op("pool", "iota", dict(out=pidx[:, 0:1], pattern=[[0, 1]], base=0, channel_multiplier=1, allow_small_or_imprecise_dtypes=True), (), ["pidx"])
        S.op("pool", "iota", dict(out=lidx, pattern=[[1, 2048]], base=0, channel_multiplier=0, allow_small_or_imprecise_dtypes=True), (), ["lidx"])
        S.op("pool", "iota", dict(out=nd, pattern=[[1, 128]], base=0, channel_multiplier=-1, allow_small_or_imprecise_dtypes=True), (), ["nd"])
        S.op("pool", "iota", dict(out=nidx, pattern=[[1, 128]], base=0, channel_multiplier=0, allow_small_or_imprecise_dtypes=True), (), ["nidx"])
        TSS("dve", hi[:, 0:1], pidx[:, 0:1], 64.0, ALU.is_ge, ["pidx"], ["hi"])
        STT(imod[:, 0:1], hi[:, 0:1], -64.0, pidx[:, 0:1], ALU.mult, ALU.add, ["hi", "pidx"], ["imod"])
        ACT(invf[:, 0:1], imod[:, 0:1], AF.Exp, ["imod"], ["invf"], scale=-math.log(10000.0) / 64.0)
        TS("dve", sgn[:, 0:1], hi[:, 0:1], 2.0, -1.0, ALU.mult, ALU.add, ["hi"], ["sgn"])

        def sincos(a_ap, tA, tB, out_ap, tok_a, tok_A, tok_B, tok_out, quarter, post_scalar=None, deps=()):
            if quarter != 0.0:
                TSS("dve", tA, a_ap, quarter, ALU.add, [tok_a], [tok_A])
                src, stok = tA, tok_A
            else:
                src, stok = a_ap, tok_a
            TSS("dve", tB, src, MAGIC, ALU.add, [stok], [tok_B])
            TSS("dve", tB, tB, MAGIC, ALU.subtract, [tok_B], [tok_B])
            TT("dve", tA, src, tB, ALU.subtract, [stok, tok_B], [tok_A])
            ACT(tA, tA, AF.Sin, [tok_A], [tok_A], scale=SIN_SCALE)
            if post_scalar is None:
                CP("dve", out_ap, tA, [tok_A] + list(deps), [tok_out])
            else:
                TS("dve", out_ap, tA, post_scalar, None, ALU.mult, None, [tok_A] + list(deps), [tok_out])

        TS("dve", ta, lidx, invf[:, 0:1], 1.0 / TWO_PI, ALU.mult, ALU.mult, ["lidx", "invf"], ["ta"])
        sincos(ta, tb_, tc_, SINS, "ta", "tb", "tc", "SINS", 0.0, post_scalar=sgn[:, 0:1], deps=["sgn"])
        sincos(ta, tb_, tc_, COS, "ta", "tb", "tc", "COS", 0.25)
        ACT(nd, nd, AF.Abs, ["nd"], ["nd"])
        for h in range(4):
            ACT(Dm[:, h, :], nd, AF.Exp, ["nd", ("cst", CLSC)], [("Dm", h)], scale=LN_G[h], bias=cst[:, CLSC:CLSC + 1])
            ACT(XIF[:, h, :], nidx, AF.Exp, ["nidx", ("cst", 4 + h)], [("XIF", h)], scale=LN_G[h], bias=cst[:, 4 + h:5 + h])
            ACT(XIB[:, h, :], nidx, AF.Exp, ["nidx", ("cst", 8 + h)], [("XIB", h)], scale=-LN_G[h], bias=cst[:, 8 + h:9 + h])
            ACT(ZF[:, h:h + 1], pidx[:, 0:1], AF.Exp, ["pidx", ("cst", 12 + h)], [("ZF", h)], scale=-LN_G[h], bias=cst[:, 12 + h:13 + h])
            ACT(ZB[:, h:h + 1], pidx[:, 0:1], AF.Exp, ["pidx", ("cst", CLSC)], [("ZB", h)], scale=LN_G[h], bias=cst[:, CLSC:CLSC + 1])

        DMA("ld0", gain[:, 0:4], gain_d, (), ["gain"])
        DMA("ld1", glub[:, 0:4], glub_d, (), ["glub"])
        DMA("ld3", cw.rearrange("p c k -> p (c k)"), cw_d, (), ["cw"])
        DMA("ld4", cb[:, 0:44], cb_d, (), ["cb"])
        DMA("ld5", nfbc, nfin_d.partition_broadcast(128), (), ["nfbc"])
        DMA("ldc0", gluw, glu_w_d.rearrange("(k p) n -> p k n", p=128), (), ["gluw"], eng="pool")
        S.barrier()
        SBASE = M.top
        M.top = SMALL_END
        sm = {}
        for nm in ["lr", "li", "dt", "t0", "t1", "t2", "AR", "AI", "CR", "CI", "IR", "II", "u0", "u1", "u2", "u3"]:
            sm[nm] = M.alloc(128, F32)
        POSr = M.alloc(9 * 128, F32, "p (q g) -> p q g", q=9)
        POSi = M.alloc(9 * 128, F32, "p (q g) -> p q g", q=9)
        NEGr = M.alloc(8 * 128, F32, "p (q g) -> p q g", q=8)
        NEGi = M.alloc(8 * 128, F32, "p (q g) -> p q g", q=8)
        Et = {}
        for nm in ["q", "k", "s", "w"]:
            Et[nm] = (M.alloc(1024, F32, "p (g t) -> p g t", g=32), M.alloc(1024, F32, "p (g t) -> p g t", g=32))
        Braw_r = M.alloc(2048, F32, "p (g c) -> p g c", g=32)
        Braw_i = M.alloc(2048, F32, "p (g c) -> p g c", g=32)
        BBr = M.alloc(2048, F32, "p (g c) -> p g c", g=32)
        BBi = M.alloc(2048, F32, "p (g c) -> p g c", g=32)
        Cr_ = M.alloc(2048, F32, "p (g c) -> p g c", g=32)
        Ci_ = M.alloc(2048, F32, "p (g c) -> p g c", g=32)
        drep = M.alloc(128, F32)
        pq = M.alloc(64, F32)
        fq = M.alloc(512, F32)
        MF = M.alloc(512, F32)
        MB = M.alloc(512, F32)
        bigs = [M.alloc(16384, F32) for _ in range(5)]
        stg = [M.alloc(8192, BF16)]
        stg.append(stg[0])
        tp_f = [M.alloc(512, F32) for _ in range(2)]

        DMA("ld6", sm["lr"], lre2_d, (), ["v_lr"])
        DMA("ld7", sm["li"], lim2_d, (), ["v_li"])
        DMA("ld8", sm["dt"], ldt2_d, (), ["v_dt"])
        DMA("ld9", Braw_r.rearrange("p g c -> p (g c)"), bre2_d, (), ["Braw_r"])
        DMA("ld10", Braw_i.rearrange("p g c -> p (g c)"), bim2_d, (), ["Braw_i"])
        DMA("ld11", Cr_.rearrange("p g c -> p (g c)"), cre2_d, (), ["Cr"])
        DMA("ld12", Ci_.rearrange("p g c -> p (g c)"), cim2_d, (), ["Ci"])
        DMA("ld13", drep[:, 0:32], drep_d, (), ["drep"])

        def abar(lr, li, dt, t0, t1, t2, out_r, out_i, pfx):
            T = lambda n: pfx + n
            TSS("dve", lr, lr, -1e-4, ALU.min, [T("lr")], [T("lr")])
            ACT(dt, dt, AF.Exp, [T("dt")], [T("dt")])
            TT("dve", t0, lr, dt, ALU.mult, [T("lr"), T("dt")], [T("t0")])
            TS("dve", t0, t0, -8.0, 1.0 / 16.0, ALU.max, ALU.mult, [T("t0")], [T("t0")])
            TSS("dve", out_r, t0, 1.0 / math.factorial(8), ALU.mult, [T("t0")], [T("or")])
            for k in range(7, 0, -1):
                STT(out_r, out_r, 1.0 / math.factorial(k), t0, ALU.add, ALU.mult, [T("or"), T("t0")], [T("or")])
            TSS("dve", t0, out_r, 1.0, ALU.add, [T("or")], [T("t0")])
            for _ in range(4):
                TT("dve", t0, t0, t0, ALU.mult, [T("t0")], [T("t0")])
            STT(t1, li, 1.0 / TWO_PI, dt, ALU.mult, ALU.mult, [T("li"), T("dt")], [T("t1")])
            TSS("dve", t2, t1, MAGIC, ALU.add, [T("t1")], [T("t2")])
            TSS("dve", t2, t2, MAGIC, ALU.subtract, [T("t2")], [T("t2")])
            TT("dve", t1, t1, t2, ALU.subtract, [T("t1"), T("t2")], [T("t1")])
            TSS("dve", t1, t1, math.pi, ALU.mult, [T("t1")], [T("t1")])
            TT("dve", t2, t1, t1, ALU.mult, [T("t1")], [T("t2")])
            TSS("dve", out_i, t2, 1.0 / math.factorial(13), ALU.mult, [T("t2")], [T("oi")])
            for k in range(5, 0, -1):
                STT(out_i, out_i, ((-1.0) ** k) / math.factorial(2 * k + 1), t2, ALU.add, ALU.mult, [T("oi"), T("t2")], [T("oi")])
            STT(out_i, out_i, 1.0, t1, ALU.add, ALU.mult, [T("oi"), T("t1")], [T("oi")])
            TSS("dve", out_r, t2, -1.0 / math.factorial(14), ALU.mult, [T("t2"), T("t0")], [T("or")])
            for k in range(6, 0, -1):
                STT(out_r, out_r, ((-1.0) ** k) / math.factorial(2 * k), t2, ALU.add, ALU.mult, [T("or"), T("t2")], [T("or")])
            TSS("dve", out_r, out_r, 1.0, ALU.add, [T("or")], [T("or")])
            TT("dve", t2, out_i, out_i, ALU.mult, [T("oi"), T("or")], [T("t2")])
            TS("dve", t2, t2, -2.0, 1.0, ALU.mult, ALU.add, [T("t2")], [T("t2")])
            STT(out_i, out_i, 2.0, out_r, ALU.mult, ALU.mult, [T("oi"), T("or")], [T("oi")])
            TT("dve", out_r, t2, t0, ALU.mult, [T("t2"), T("t0"), T("oi")], [T("or")])
            TT("dve", out_i, out_i, t0, ALU.mult, [T("oi"), T("t0")], [T("oi")])

        AR, AI = sm["AR"], sm["AI"]
        abar(sm["lr"], sm["li"], sm["dt"], sm["t0"], sm["t1"], sm["t2"], AR, AI, "v_")
        nmix = M.alloc(64, F32)
        nffn = M.alloc(64, F32)
        PCH = 1536
        wtmp = [M.alloc(PCH * 4, F32) for _ in range(2)]
        wob = [M.alloc(PCH * 2, BF16) for _ in range(2)]
        DMA("ld0", nmix[:, 0:8], nmix_d, (), ["nmix"])
        DMA("ld1", nffn[:, 0:8], nffn_d, (), ["nffn"])
        prep_i = [0]

        def prep(src_, r0, c0, ncols, scale_ap, scale_tok, outs):
            i = prep_i[0] % 2
            prep_i[0] += 1
            DMA("wpi%d" % i, wtmp[i][:, 0:ncols], src_[r0:r0 + 128, c0:c0 + ncols], (), [("wtmp", i)])
            if scale_ap is None:
                ACT(wob[i][:, 0:ncols], wtmp[i][:, 0:ncols], AF.Copy, [("wtmp", i)], [("wob", i)])
            else:
                ACT(wob[i][:, 0:ncols], wtmp[i][:, 0:ncols], AF.Copy, [("wtmp", i), scale_tok], [("wob", i)], scale=scale_ap)
            for n_, (dst_ap, vf, tok) in enumerate(outs):
                DMA("wpo%d%s" % (i, "abc"[n_]), dst_ap, vf(wob[i]), [("wob", i)], [tok], eng="act")

        wI_toks, wO_toks, wU_toks, wD_toks = [], [], [], []
        wI_A = wI[:, 0:16384].rearrange("p (u k c) -> p u k c", u=8, k=8)
        wI_B = wI[:, 16384:20480].rearrange("p (u k c) -> p u k c", u=4, k=8)
        wI_C = wI[:, 20480:28672].rearrange("p (u k c) -> p u k c", u=2, k=8)
        for kt in range(8):
            prep(w_in_d, kt * 128, 0, 1536, nmix[:, kt:kt + 1], "nmix",
                 [(wI_A[:, 0:6, kt, :], lambda w: w[:, 0:1536].rearrange("p (u c) -> p u c", u=6), ("wI", kt, 0))])
            prep(w_in_d, kt * 128, 1536, 1536, nmix[:, kt:kt + 1], "nmix",
                 [(wI_A[:, 6:8, kt, :], lambda w: w[:, 0:512].rearrange("p (u c) -> p u c", u=2), ("wI", kt, 1)),
                  (wI_B[:, :, kt, :], lambda w: w[:, 512:1024].rearrange("p (u c) -> p u c", u=4), ("wI", kt, 2)),
                  (wI_C[:, 0, kt, :], lambda w: w[:, 1024:1536], ("wI", kt, 3))])
            prep(w_in_d, kt * 128, 3072, 512, nmix[:, kt:kt + 1], "nmix",
                 [(wI_C[:, 1, kt, :], lambda w: w[:, 0:512], ("wI", kt, 4))])
            wI_toks += [("wI", kt, n_) for n_ in range(5)]
        for kt in range(8):
            prep(w_out_d, kt * 128, 0, DM, None, None, [(wO[kt * 128:(kt + 1) * 128, :], lambda w: w[:, 0:DM], ("wO", kt))])
            wO_toks.append(("wO", kt))
        wU_v = wU.rearrange("p (c k v n) -> p c k v n", c=NCT, k=8, v=2)
        for kt in range(8):
            for hf in range(2):
                for ch_ in range(2):
                    prep(w_up_d, kt * 128, hf * DFF + ch_ * 1408, 1408, nffn[:, kt:kt + 1], "nffn",
                         [(wU_v[:, ch_ * 11:(ch_ + 1) * 11, kt, hf, :], lambda w: w[:, 0:1408].rearrange("p (c n) -> p c n", c=11), ("wU", kt, hf, ch_))])
                    wU_toks.append(("wU", kt, hf, ch_))
        for c in range(NCT):
            prep(w_down_d, c * 128, 0, DM, None, None, [(wD[c * 128:(c + 1) * 128, :], lambda w: w[:, 0:DM], ("wD", c))])
            wD_toks.append(("wD", c))

        ATOK = ["v_or", "v_oi"]
        lr, li = sm["lr"], sm["li"]
        u0, u1, u2, u3 = sm["u0"], sm["u1"], sm["u2"], sm["u3"]
        TT("dve", u0, lr, lr, ALU.mult, ["v_lr"] + ATOK, ["u0"])
        TT("dve", u1, li, li, ALU.mult, ["v_li"], ["u1"])
        TT("dve", u0, u0, u1, ALU.add, ["u0", "u1"], ["u0"])
        RCP(u0, u0, ["u0"], ["u0"])
        TSS("dve", u1, AR, -1.0, ALU.add, ATOK + ["u1"], ["u1"])
        TT("dve", u2, u1, lr, ALU.mult, ["u1", "v_lr"], ["u2"])
        TT("dve", u3, AI, li, ALU.mult, ATOK + ["v_li"], ["u3"])
        TT("dve", u2, u2, u3, ALU.add, ["u2", "u3"], ["u2"])
        TT("dve", sm["CR"], u2, u0, ALU.mult, ["u2", "u0"], ["CR"])
        TT("dve", u2, AI, lr, ALU.mult, ATOK + ["v_lr", "CR"], ["u2"])
        TT("dve", u3, u1, li, ALU.mult, ["u1", "v_li", "CR"], ["u3"])
        TT("dve", u2, u2, u3, ALU.subtract, ["u2", "u3"], ["u2"])
        TT("dve", sm["CI"], u2, u0, ALU.mult, ["u2", "u0"], ["CI"])
        TT("dve", u0, AR, AR, ALU.mult, ATOK + ["CI", "u0"], ["u0"])
        TT("dve", u1, AI, AI, ALU.mult, ATOK + ["CI", "u1"], ["u1"])
        TT("dve", u0, u0, u1, ALU.add, ["u0", "u1"], ["u0"])
        RCP(u0, u0, ["u0"], ["u0"])
        TT("dve", sm["IR"], AR, u0, ALU.mult, ATOK + ["u0"], ["IR"])
        STT(sm["II"], AI, -1.0, u0, ALU.mult, ALU.mult, ATOK + ["u0"], ["II"])

        def cmul(orr, oii, ar, ai, br, bi, toks_in, tok_out):
            TT("dve", u2, ar, br, ALU.mult, toks_in + ["u2"], ["u2"])
            TT("dve", u3, ai, bi, ALU.mult, toks_in + ["u3"], ["u3"])
            TT("dve", orr, u2, u3, ALU.subtract, ["u2", "u3"], [tok_out + "r"])
            TT("dve", u2, ar, bi, ALU.mult, toks_in + [tok_out + "r"], ["u2"])
            TT("dve", u3, ai, br, ALU.mult, toks_in + [tok_out + "r"], ["u3"])
            TT("dve", oii, u2, u3, ALU.add, ["u2", "u3"], [tok_out + "i"])

        S.op("dve", "memset", dict(ap=POSr[:, 0, :], constant=1.0), (), ["POS0r"])
        S.op("dve", "memset", dict(ap=POSi[:, 0, :], constant=0.0), (), ["POS0i"])
        S.op("dve", "memset", dict(ap=NEGr[:, 0, :], constant=1.0), (), ["NEG0r"])
        S.op("dve", "memset", dict(ap=NEGi[:, 0, :], constant=0.0), (), ["NEG0i"])
        CP("dve", POSr[:, 1, :], AR, ATOK, ["POS1r"])
        CP("dve", POSi[:, 1, :], AI, ATOK, ["POS1i"])
        CP("dve", NEGr[:, 1, :], sm["IR"], ["IR"], ["NEG1r"])
        CP("dve", NEGi[:, 1, :], sm["II"], ["II"], ["NEG1i"])
        for p in range(2, 9):
            cmul(POSr[:, p, :], POSi[:, p, :], POSr[:, p - 1, :], POSi[:, p - 1, :], POSr[:, 1, :], POSi[:, 1, :],
                 ["POS%dr" % (p - 1), "POS%di" % (p - 1), "POS1r", "POS1i"], "POS%d" % p)
        for p in range(2, 8):
            cmul(NEGr[:, p, :], NEGi[:, p, :], NEGr[:, p - 1, :], NEGi[:, p - 1, :], NEGr[:, 1, :], NEGi[:, 1, :],
                 ["NEG%dr" % (p - 1), "NEG%di" % (p - 1), "NEG1r", "NEG1i"], "NEG%d" % p)
        POST = [("POS%d" % p) + x for p in range(9) for x in "ri"]
        NEGT = [("NEG%d" % p) + x for p in range(8) for x in "ri"]
        CP("dve", MPr[:, 1, :], POSr[:, 8, :], POST, ["MP1r"])
        CP("dve", MPi[:, 1, :], POSi[:, 8, :], POST, ["MP1i"])
        for p in range(2, 17):
            cmul(MPr[:, p, :], MPi[:, p, :], MPr[:, p - 1, :], MPi[:, p - 1, :], MPr[:, 1, :], MPi[:, 1, :],
                 ["MP%dr" % (p - 1), "MP%di" % (p - 1), "MP1r", "MP1i"], "MP%d" % p)
        for p, q_ in ((17, 16), (18, 17), (19, 18)):
            cmul(MPr[:, p, :], MPi[:, p, :], MPr[:, q_, :], MPi[:, q_, :], MPr[:, q_, :], MPi[:, q_, :],
                 ["MP%dr" % q_, "MP%di" % q_], "MP%d" % p)
        MPT0 = [("MP%d" % p) + x for p in range(1, 20) for x in "ri"]
        TSS("dve", MPB[:, 1:20, 0, :], MPi[:, 1:20, :], -1.0, ALU.mult, MPT0, ["MPB0"])
        CP("dve", MPB[:, 1:20, 1, :], MPi[:, 1:20, :], MPT0, ["MPB1"])
        MPT = MPT0 + ["MPB0", "MPB1"]
        H0, H1 = slice(0, 64), slice(64, 128)
        for t in range(8):
            for (nm, src0, i0, src1, i1) in (("q", "POS", t, "NEG", t), ("k", "NEG", t, "POS", t), ("s", "POS", 7 - t, "POS", t), ("w", "POS", t + 1, "POS", 8 - t)):
                for ri in range(2):
                    tabs = {"POS": (POSr, POSi), "NEG": (NEGr, NEGi)}
                    CP("pool", Et[nm][ri][H0, :, t], tabs[src0][ri][H0, i0, :], POST + NEGT, [("E", nm, ri)])
                    CP("pool", Et[nm][ri][H1, :, t], tabs[src1][ri][H1, i1, :], POST + NEGT, [("E", nm, ri)])
        def bc_g(ap):
            return ap[:, 0:32].unsqueeze(2).to_broadcast([128, 32, 16])
        TT("dve", BBr, Braw_r, bc_g(sm["CR"]), ALU.mult, ["Braw_r", "CR"], ["BBr"])
        TT("dve", BBi, Braw_i, bc_g(sm["CI"]), ALU.mult, ["Braw_i", "CI"], ["BBi"])
        TT("dve", BBr, BBr, BBi, ALU.subtract, ["BBr", "BBi"], ["BBr"])
        TT("dve", BBi, Braw_i, bc_g(sm["CR"]), ALU.mult, ["Braw_i", "CR", "BBr"], ["BBi"])
        TT("dve", Braw_r, Braw_r, bc_g(sm["CI"]), ALU.mult, ["Braw_r", "CI", "BBr"], ["Braw_r"])
        TT("dve", BBi, BBi, Braw_r, ALU.add, ["BBi", "Braw_r"], ["BBi"])

        def v4(ap):
            return ap.rearrange("p (g t c) -> p g t c", g=32, t=8)

        def bX(ap):
            return ap.unsqueeze(2).to_broadcast([128, 32, 8, 16])

        def bE(ap):
            return ap.unsqueeze(3).to_broadcast([128, 32, 8, 16])

        def BG(k):
            return ("big", k)

        def cprod(ko_r, ko_i, Xr, Xi, xtoks, nm, k1, neg_im=False):
            Er, Ei = Et[nm]
            et = [("E", nm, 0), ("E", nm, 1)]
            outr, outi, t1_ = bigs[ko_r], bigs[ko_i], bigs[k1]
            TT("dve", v4(outr), bX(Xr), bE(Er), ALU.mult, xtoks + et, [BG(ko_r)])
            TT("dve", v4(t1_), bX(Xi), bE(Ei), ALU.mult, xtoks + et, [BG(k1)])
            TT("dve", outr, outr, t1_, ALU.subtract, [BG(ko_r), BG(k1)], [BG(ko_r)])
            TT("dve", v4(outi), bX(Xr), bE(Ei), ALU.mult, xtoks + et, [BG(ko_i)])
            TT("dve", v4(t1_), bX(Xi), bE(Er), ALU.mult, xtoks + et, [BG(k1)])
            if neg_im:
                STT(outi, outi, -1.0, t1_, ALU.mult, ALU.subtract, [BG(ko_i), BG(k1)], [BG(ko_i)])
            else:
                TT("dve", outi, outi, t1_, ALU.add, [BG(ko_i), BG(k1)], [BG(ko_i)])

        def TRF(out, in_, reads, writes):
            return S.op("pe", "transpose", dict(out=out, in_=in_, identity=identf), list(reads) + ["identf"], writes)

        CT = ["Cr", "Ci"]
        BT = ["BBr", "BBi"]
        cprod(0, 1, Cr_, Ci_, CT, "w", 2, neg_im=True)
        CP("dve", stg[0], bigs[0], [BG(0)], [("stg", 0)])
        DMA("st0", s5Wfar[:, 0:4096], stg[0], [("stg", 0)], ["s5Wfar0"], eng="pool")
        CP("dve", stg[1], bigs[1], [BG(1)], [("stg", 0)])
        DMA("st0", s5Wfar[:, 4096:8192], stg[1], [("stg", 0)], ["s5Wfar1"], eng="pool")
        cprod(0, 1, BBr, BBi, BT, "s", 2)
        nb = 0
        for ri in range(2):
            stv = stg[ri].rearrange("p (g m) -> p g m", g=32)
            for g4 in range(8):
                bk = 2 + nb % 4
                nb += 1
                for gg in range(4):
                    g = g4 * 4 + gg
                    TRF(pbf(bk)[:, gg * 128:(gg + 1) * 128], bigs[ri][:, g * 128:(g + 1) * 128], [BG(ri)], [("pb", bk)])
                CP("dve", stv[:, g4 * 4:(g4 + 1) * 4, :], pbf(bk).rearrange("p (g m) -> p g m", g=4), [("pb", bk)], [("stg", 0)])
            DMA("st0", s5Wsum[:, ri * 4096:(ri + 1) * 4096], stg[ri], [("stg", 0)], ["s5Wsum%d" % ri], eng="pool")
        cprod(0, 1, Cr_, Ci_, CT, "q", 4, neg_im=True)
        cprod(2, 3, BBr, BBi, BT, "k", 4)
        TS("dve", pq[:, 0:1], pidx[:, 0:1], 1.0 / 16.0, -15.0 / 32.0, ALU.mult, ALU.add, ["pidx"], ["pq"])
        TSS("dve", pq[:, 0:1], pq[:, 0:1], MAGIC, ALU.add, ["pq"], ["pq"])
        TSS("dve", pq[:, 0:1], pq[:, 0:1], MAGIC, ALU.subtract, ["pq"], ["pq"])
        TS("dve", fq[:, 0:128], nidx[:, 0:128], 1.0 / 16.0, -15.0 / 32.0, ALU.mult, ALU.add, ["nidx"], ["fq"])
        TSS("dve", fq[:, 0:128], fq[:, 0:128], MAGIC, ALU.add, ["fq"], ["fq"])
        TSS("dve", fq[:, 0:128], fq[:, 0:128], MAGIC, ALU.subtract, ["fq"], ["fq"])
        TS("dve", MF[:, 0:128], fq[:, 0:128], pq[:, 0:1], None, ALU.is_ge, None, ["fq", "pq"], ["MF"])
        TS("dve", MB[:, 0:128], fq[:, 0:128], pq[:, 0:1], None, ALU.is_le, None, ["fq", "pq"], ["MB"])
        stgT = stg[0].rearrange("p (g m) -> p g m", g=32)
        QK = [BG(0), BG(1), BG(2), BG(3)]
        for g in range(32):
            gsl = slice(g * 128, (g + 1) * 128)
            bf_, bb_ = (0, 1) if g % 2 == 0 else (6, 7)
            k_ = g % 2
            MM(pbf(bf_)[:, 0:128], bigs[2][H0, gsl], bigs[0][H0, gsl], True, False, QK, [("pb", bf_)])
            MM(pbf(bf_)[:, 0:128], bigs[3][H0, gsl], bigs[1][H0, gsl], False, True, QK, [("pb", bf_)])
            MM(pbf(bb_)[:, 0:128], bigs[2][H1, gsl], bigs[0][H1, gsl], True, False, QK, [("pb", bb_)])
            MM(pbf(bb_)[:, 0:128], bigs[3][H1, gsl], bigs[1][H1, gsl], False, True, QK, [("pb", bb_)])
            TT("dve", tp_f[k_][:, 0:128], pbf(bf_)[:, 0:128], MF[:, 0:128], ALU.mult, [("pb", bf_), "MF"], [("tpf", k_)])
            TT("dve", bigs[4][:, gsl], pbf(bb_)[:, 0:128], MB[:, 0:128], ALU.mult, [("pb", bb_), "MB"], [BG(4)])
            TT("dve", tp_f[k_][:, 0:128], tp_f[k_][:, 0:128], bigs[4][:, gsl], ALU.add, [("tpf", k_), BG(4)], [("tpf", k_)])
            STT(stgT[:, g, :], identf, drep[:, g:g + 1], tp_f[k_][:, 0:128], ALU.mult, ALU.add, ["identf", "drep", ("tpf", k_)], [("stg", 0)])
        DMA("st0", s5Wtoep, stg[0], [("stg", 0)], ["s5Wtoep"], eng="pool")
        S5W = ["s5Wfar0", "s5Wfar1", "s5Wsum0", "s5Wsum1", "s5Wtoep"]

        if debug == "setup":
            S.barrier()
            M.top = PBASE
            d1_ = M.alloc(8192)
            d2_ = M.alloc(16384)
            d3_ = M.alloc(16384)
            DMA("lb", d1_, s5Wtoep, S5W, ["d1"])
            DMA("lc", d2_, s5Wsum, S5W, ["d2"])
            DMA("ld", d3_, s5Wfar, S5W, ["d3"])
            DEBUG["Wtoep"] = (d1_, [128, 4096], BF16, ["d1"])
            DEBUG["Wsum"] = (d2_, [128, 8192], BF16, ["d2"])
            DEBUG["Wfar"] = (d3_, [128, 8192], BF16, ["d3"])
            DEBUG["MPr"] = (MPr[:, 1:17, :].rearrange("p q g -> p (q g)"), [128, 16 * 32], F32, MPT)
            DEBUG["MPi"] = (MPi[:, 1:17, :].rearrange("p q g -> p (q g)"), [128, 16 * 32], F32, MPT)

        nseq_run = 0 if debug == "setup" else (1 if debug else NSEQ)
        for s in range(nseq_run):
            S.barrier()
            M.top = PBASE
            hT = M.alloc(32768, BF16, "p (k t) -> p k t", k=8)
            V = M.alloc(16384, BF16, "p (i c) -> p i c", i=16)
            qT = M.alloc(4096)
            kT = M.alloc(4096)
            qf = M.alloc(4096)
            qb = M.alloc(4096)
            Kf = M.alloc(4096, BF16, "p (j d) -> p j d", j=16)
            Kb = M.alloc(4096, BF16, "p (j d) -> p j d", j=16)
            Rbf = M.alloc(8192, BF16, "p (a j e) -> p a j e", a=2, j=16)
            R32 = M.alloc(1024, F32, "p (a e) -> p a e", a=2)
            gs = M.alloc(4096)
            xt = [M.alloc(4096, F32) for _ in range(2)]
            hbs = [M.alloc(2048) for _ in range(2)]
            junk = M.alloc(2048)
            wv = M.alloc(8192, BF16, "p (k n) -> p k n", k=8)
            wq = [M.alloc(4096, BF16, "p (k n) -> p k n", k=8) for _ in range(2)]
            wg = M.alloc(2048, BF16, "p (k n) -> p k n", k=8)
            qs = [M.alloc(1024) for _ in range(2)]
            r1 = [M.alloc(2048, F32) for _ in range(2)]
            r2 = [M.alloc(2048, F32) for _ in range(2)]
            Sm = M.alloc(1024, BF16, "p (j n) -> p j n", j=4)
            sq = [M.alloc(1024) for _ in range(2)]
            sd = [M.alloc(2048, F32) for _ in range(2)]
            on = [M.alloc(2048, F32) for _ in range(2)]
            def wI_unit(off, ncols):
                return wI[:, off:off + 8 * ncols].rearrange("p (k c) -> p k c", k=8)

            DMA("wv", wv, wI_unit(20480, 512), wI_toks, ["wv"])

            def vproj(i_):
                pk_ = 2 + (i_ % 2)
                for kt in range(8):
                    MM(pbf(pk_), hT[:, kt, i_ * 128:(i_ + 1) * 128], wv[:, kt, :], kt == 0, kt == 7, [("hT", i_), "wv"], [("pb", pk_)])
                CP("dve", V[:, i_, :], pbf(pk_), [("pb", pk_)], [("V", i_)])

            pend_ev = None
            for i in range(16):
                b = i % 2
                DMA("xt%d" % b, xt[b], x_d[s, i * 128:(i + 1) * 128, :], (), [("xt", b)])
                ACT(junk, xt[b], AF.Square, [("xt", b)], ["junk", ("ss", i)], accum_out=stat[:, 0, i:i + 1])
                ACT(stat[:, 1, i:i + 1], stat[:, 0, i:i + 1], AF.Sqrt, [("ss", i), ("cst", CE)], [("sd", i)], scale=1.0 / DM, bias=cst[:, CE:CE + 1])
                RCP(stat[:, 2, i:i + 1], stat[:, 1, i:i + 1], [("sd", i)], [("rs", i)])
                hb = hbs[b]
                TS("dve", hb, xt[b], stat[:, 2, i:i + 1], None, ALU.mult, None, [("xt", b), ("rs", i)], [("hb", b)])
                pk = i % 2
                for kt in range(8):
                    TR(pbb(pk)[:, kt * 128:(kt + 1) * 128], hb[:, kt * 128:(kt + 1) * 128], [("hb", b)], [("pb", pk)])
                if pend_ev is not None:
                    ACT(hT[:, :, pend_ev[0] * 128:(pend_ev[0] + 1) * 128], pbb(pend_ev[1]).rearrange("p (k t) -> p k t", k=8), AF.Copy, [("pb", pend_ev[1])], [("hT", pend_ev[0])])
                pend_ev = (i, pk)
                if i >= 2:
                    vproj(i - 2)
            ACT(hT[:, :, pend_ev[0] * 128:(pend_ev[0] + 1) * 128], pbb(pend_ev[1]).rearrange("p (k t) -> p k t", k=8), AF.Copy, [("pb", pend_ev[1])], [("hT", pend_ev[0])])
            vproj(14)
            vproj(15)
            hT_all = [("hT", i) for i in range(16)]

            for h in range(4):
                hs = slice(h * 128, (h + 1) * 128)
                DMA("wq0", wq[0], wI_unit(h * 2048, 256), wI_toks, [("wq", 0, 0), ("wq", 0, 1)])
                DMA("wq1", wq[1], wI_unit((4 + h) * 2048, 256), wI_toks, [("wq", 1, 0), ("wq", 1, 1)])
                DMA("wg", wg, wI_unit(16384 + h * 1024, 128), wI_toks, ["wg"])
                ui = 0
                for which in range(2):
                    dst = qT if which == 0 else kT
                    dtok = "qT" if which == 0 else "kT"
                    for tb in range(4):
                        pa = (ui % 2) * 2
                        pbk = pa + 1
                        rr = ui % 2
                        ui += 1
                        tsl = slice(tb * 512, (tb + 1) * 512)
                        for kt in range(8):
                            MM(pbf(pa), wq[which][:, kt, 0:128], hT[:, kt, tsl], kt == 0, kt == 7, hT_all + [("wq", which, 0)], [("pb", pa)])
                        for kt in range(8):
                            MM(pbf(pbk), wq[which][:, kt, 128:256], hT[:, kt, tsl], kt == 0, kt == 7, hT_all + [("wq", which, 1)], [("pb", pbk)])
                        TT("dve", r1[rr][:, 0:512], pbf(pa), COS[:, tsl], ALU.mult, [("pb", pa), "COS"], [("r1", rr)])
                        TT("dve", r2[rr][:, 0:512], pbf(pbk), SINS[:, tsl], ALU.mult, [("pb", pbk), "SINS"], [("r2", rr)])
                        TT("pool", dst[:, tsl], r1[rr][:, 0:512], r2[rr][:, 0:512], ALU.add, [("r1", rr), ("r2", rr)], [(dtok, tb)])
                        if which == 0:
                            for (dd, XI, nm, xn) in ((qf, XIF, "qf", "XIF"), (qb, XIB, "qb", "XIB")):
                                TT("pool", dd[:, tsl].rearrange("p (j n) -> p j n", j=4), qT[:, tsl].rearrange("p (j n) -> p j n", j=4),
                                   XI[:, h, :].unsqueeze(1).to_broadcast([128, 4, 128]), ALU.mult, [("qT", tb), (xn, h)], [(nm, tb)])
                for tb in range(4):
                    pk = 4 + (tb % 2)
                    tsl = slice(tb * 512, (tb + 1) * 512)
                    for kt in range(8):
                        MM(pbf(pk), wg[:, kt, :], hT[:, kt, tsl], kt == 0, kt == 7, hT_all + ["wg"], [("pb", pk)])
                    ACT(gs[:, tsl], pbf(pk), AF.Silu, [("pb", pk)], [("gs", tb)])
                for g4 in range(4):
                    for jj in range(4):
                        j = g4 * 4 + jj
                        TR(pbb(6)[:, jj * 128:(jj + 1) * 128], kT[:, j * 128:(j + 1) * 128], [("kT", g4)], [("pb", 6)])
                    ACT(Kf[:, g4 * 4:(g4 + 1) * 4, :].rearrange("p j d -> p (j d)"), pbb(6)[:, 0:512], AF.Copy, [("pb", 6), ("ZF", h)], [("Kf", g4)], scale=ZF[:, h:h + 1])
                    ACT(Kb[:, g4 * 4:(g4 + 1) * 4, :].rearrange("p j d -> p (j d)"), pbb(6)[:, 0:512], AF.Copy, [("pb", 6), ("ZB", h)], [("Kb", g4)], scale=ZB[:, h:h + 1])
                ring = [6, 7, 2, 3]
                slot = 0
                for n_ in range(16):
                    for a in range(2):
                        j = n_ if a == 0 else 15 - n_
                        KK = Kf if a == 0 else Kb
                        ktok = "Kf" if a == 0 else "Kb"
                        bk = ring[slot % 4]
                        slot += 1
                        MM(pbf(bk)[:, 0:128], KK[:, j, :], V[:, j, hs], True, True, [(ktok, j // 4), ("V", j)], [("pb", bk)])
                        if n_ == 0:
                            CP("dve", R32[:, a, :], pbf(bk)[:, 0:128], [("pb", bk)], [("R32", a)])
                        else:
                            STT(R32[:, a, :], R32[:, a, :], G128[h], pbf(bk)[:, 0:128], ALU.mult, ALU.add, [("pb", bk), ("R32", a)], [("R32", a)])
                        if a == 0:
                            ACT(Rbf[:, a, j, :], R32[:, a, :], AF.Copy, [("R32", a)], [("Rbf", a, j)])
                        else:
                            CP("pool", Rbf[:, a, j, :], R32[:, a, :], [("R32", a)], [("Rbf", a, j)])

                def scores(j):
                    bk = 6 + (j % 2)
                    jsl = slice(j * 128, (j + 1) * 128)
                    MM(pbf(bk)[:, 0:128], kT[:, jsl], qT[:, jsl], True, True, [("kT", j // 4), ("qT", j // 4)], [("pb", bk)])

                def norm_tail(b4):
                    po = 4 + (b4 % 2)
                    pm = b4 % 2
                    k_ = b4 % 2
                    MM(pbf(pm), onesm, sq[k_][:, 0:512], True, True, [("sq", k_), "onesm"], [("pb", pm)])
                    ACT(sd[k_][:, 0:512], pbf(pm), AF.Ln, [("pb", pm), ("cst", CE)], [("sd", k_)], bias=cst[:, CE:CE + 1])
                    ACT(sd[k_][:, 0:512], sd[k_][:, 0:512], AF.Exp, [("sd", k_)], [("sd", k_)], scale=-0.5)
                    STT(on[k_][:, 0:512], pbf(po), gain[:, h:h + 1], sd[k_][:, 0:512], ALU.mult, ALU.mult, [("pb", po), ("sd", k_), "gain"], [("on", k_)])
                    TT("pool", mixT[:, h, b4 * 512:(b4 + 1) * 512], on[k_][:, 0:512], gs[:, b4 * 512:(b4 + 1) * 512], ALU.mult, [("on", k_), ("gs", b4)], [("mixT", h, b4)])

                scores(0)
                pend = None
                for j in range(16):
                    b4, jj = divmod(j, 4)
                    po = 4 + (b4 % 2)
                    sl = j % 4
                    bk = 6 + (j % 2)
                    jsl = slice(j * 128, (j + 1) * 128)
                    osl = slice(jj * 128, (jj + 1) * 128)
                    if j < 15:
                        scores(j + 1)
                    TT("dve", Sm[:, sl, :], pbf(bk)[:, 0:128], Dm[:, h, :], ALU.mult, [("pb", bk), ("Dm", h)], [("Sm", sl)])
                    nmm = 1 + (1 if j > 0 else 0) + (1 if j < 15 else 0)
                    MM(pbf(po)[:, osl], V[:, j, hs], Sm[:, sl, :], True, nmm == 1, [("V", j), ("Sm", sl)], [("pb", po)])
                    if j > 0:
                        MM(pbf(po)[:, osl], Rbf[:, 0, j - 1, :], qf[:, jsl], False, j == 15, [("Rbf", 0, j - 1), ("qf", b4)], [("pb", po)])
                    if j < 15:
                        MM(pbf(po)[:, osl], Rbf[:, 1, j + 1, :], qb[:, jsl], False, True, [("Rbf", 1, j + 1), ("qb", b4)], [("pb", po)])
                    if pend is not None and j == pend * 4 + 5:
                        norm_tail(pend)
                        pend = None
                    if jj == 3:
                        ACT(sq[b4 % 2][:, 0:512], pbf(po), AF.Square, [("pb", po)], [("sq", b4 % 2)])
                        if pend is not None:
                            norm_tail(pend)
                        pend = b4
                norm_tail(pend)
                if debug == "p1" and h == 0:
                    DEBUG["qT"] = (qT, [128, 2048], BF16, [("qT", t_) for t_ in range(4)])
                    DEBUG["kT"] = (kT, [128, 2048], BF16, [("kT", t_) for t_ in range(4)])
                    DEBUG["Kf"] = (Kf.rearrange("p j d -> p (j d)"), [128, 2048], BF16, [("Kf", t_) for t_ in range(4)])
                    DEBUG["Rbf"] = (Rbf.rearrange("p a j e -> p (a j e)"), [128, 4096], BF16, [("Rbf", a_, j_) for a_ in range(2) for j_ in range(16)])
                    DEBUG["gs"] = (gs, [128, 2048], BF16, [("gs", t_) for t_ in range(4)])

            if debug == "p1":
                DEBUG["ret"] = (mixT[:, 0:4, :].rearrange("p k t -> p (k t)"), [128, 4 * 2048], BF16, [("mixT", h, b) for h in range(4) for b in range(4)])
                DEBUG["hT"] = (hT.rearrange("p k t -> p (k t)"), [128, 8 * 2048], BF16, hT_all)
                DEBUG["V"] = (V.rearrange("p i c -> p (i c)"), [128, 16 * 512], BF16, [("V", i) for i in range(16)])
                break

            S.barrier()
            M.top = PBASE + 32768
            Sx = big[:, PBASE // 2:(PBASE + 32768) // 2].bitcast(F32).rearrange("p (r g j) -> p r g j", r=2, g=16)
            Ub = M.alloc(16384, BF16, "p (g j) -> p g j", g=32)
            WT = M.alloc(8192, BF16, "p (g m) -> p g m", g=32)
            WSm = M.alloc(16384, BF16, "p (r g m) -> p r g m", r=2, g=32)
            WFr = M.alloc(16384, BF16, "p (r g m) -> p r g m", r=2, g=32)
            yg = M.alloc(16384, BF16, "p (k t) -> p k t", k=4)
            YJ = [M.alloc(4096, BF16, "p (t c) -> p t c", t=8) for _ in range(2)]
            R1 = M.top
            wu = M.alloc(8192, BF16, "p (k n) -> p k n", k=8)
            UJ = M.alloc(8192, BF16, "p (g m) -> p g m", g=32)
            M.top = R1
            Fs = M.alloc(16384, BF16, "p (r g j) -> p r g j", r=2, g=16)
            sctA0, sctA1, sctB0, sctB1 = (M.alloc(2048, F32, "p (g j) -> p g j", g=32) for _ in range(4))
            gt = [sctA0.rearrange("p g j -> p (g j)"), sctB0.rearrange("p g j -> p (g j)")]

            DMA("wu", wu, wI_unit(24576, 512), wI_toks, ["wu"])
            DMA("lws", WSm.rearrange("p r g m -> p (r g m)"), s5Wsum, S5W, ["WSm"])
            DMA("lwt", WT.rearrange("p g m -> p (g m)"), s5Wtoep, S5W, ["WT"])
            DMA("lwf", WFr.rearrange("p r g m -> p (r g m)"), s5Wfar, S5W, ["WFr"])
            ev = 0
            for jb in range(2):
                for t in range(8):
                    pk = t % 2
                    base = jb * 1024 + t
                    for kt in range(8):
                        MM(pbf(pk), hT[:, kt, base:(jb + 1) * 1024:8], wu[:, kt, :], kt == 0, kt == 7, hT_all + ["wu"], [("pb", pk)])
                    if ev % 2 == 0:
                        ACT(UJ[:, :, t * 16:(t + 1) * 16], pbf(pk).rearrange("p (g c) -> p g c", g=32), AF.Copy, [("pb", pk)], ["UJ"])
                    else:
                        CP("dve", UJ[:, :, t * 16:(t + 1) * 16], pbf(pk).rearrange("p (g c) -> p g c", g=32), [("pb", pk)], ["UJ"])
                    ev += 1
                for g8 in range(4):
                    pk = 2 + g8 % 2
                    for gg in range(8):
                        TR(pbb(pk)[:, gg * 128:(gg + 1) * 128], UJ[:, g8 * 8 + gg, :], ["UJ"], [("pb", pk)])
                    if g8 % 2 == 0:
                        ACT(Ub[:, g8 * 8:(g8 + 1) * 8, jb * 128:(jb + 1) * 128], pbb(pk).rearrange("p (g j) -> p g j", g=8), AF.Copy, [("pb", pk)], [("Ub", g8)])
                    else:
                        CP("dve", Ub[:, g8 * 8:(g8 + 1) * 8, jb * 128:(jb + 1) * 128], pbb(pk).rearrange("p (g j) -> p g j", g=8), [("pb", pk)], [("Ub", g8)])

            H0, H1 = slice(0, 64), slice(64, 128)
            SXT = [("Sx", "dve"), ("Sx", "pool")]
            tmpA = [x.rearrange("p (r g) j -> p r g j", r=2) for x in (sctA0, sctA1)]
            tmpB = [x.rearrange("p (r g) j -> p r g j", r=2) for x in (sctB0, sctB1)]

            LANES = (("dve", 0, 10), ("pool", 10, 16))

            def upd(jd, js, p_, gh, n, ts):
                for (eng, g0, g1) in LANES:
                    ng = g1 - g0
                    gsl = slice(gh * 16 + g0, gh * 16 + g1)
                    lsl = slice(g0, g1)
                    dst = Sx[:, :, lsl, jd]
                    srcv = Sx[:, :, lsl, js]
                    swp = Sx[:, ::-1, lsl, js]
                    if n is None:
                        Ka = MPr[:, p_, gsl].unsqueeze(1).to_broadcast([128, 2, ng])
                        Kb = MPB[:, p_, :, gsl]
                        t0 = tmpA[ts][:, :, lsl, 0]
                        t1 = tmpB[ts][:, :, lsl, 0]
                    else:
                        Ka = MPr[:, p_, gsl].unsqueeze(1).unsqueeze(3).to_broadcast([128, 2, ng, n])
                        Kb = MPB[:, p_, :, gsl].unsqueeze(3).to_broadcast([128, 2, ng, n])
                        t0 = tmpA[ts][:, :, lsl, 0:n]
                        t1 = tmpB[ts][:, :, lsl, 0:n]
                    sx = [("Sx", eng)]
                    TT(eng, t0, srcv, Ka, ALU.mult, sx + MPT, [("tA", ts, eng)])
                    TT(eng, t1, swp, Kb, ALU.mult, sx + MPT, [("tB", ts, eng)])
                    TT(eng, dst, dst, t0, ALU.add, [("tA", ts, eng)], sx)
                    TT(eng, dst, dst, t1, ALU.add, [("tB", ts, eng)], sx)

            def s5_scan(gh):
                for gl in range(16):
                    g = gh * 16 + gl
                    pk = 4 + gl % 2
                    MM(pbf(pk)[:, 0:256], WSm[:, 0, g, :], Ub[:, g, :], True, True, [("Ub", g // 8), "WSm"], [("pb", pk)])
                    MM(pbf(pk)[:, 256:512], WSm[:, 1, g, :], Ub[:, g, :], True, True, [("Ub", g // 8), "WSm"], [("pb", pk)])
                    ACT(Sx[H0, :, gl, :], pbf(pk)[H0, :].rearrange("p (r j) -> p r j", r=2), AF.Copy, [("pb", pk)], SXT + hT_all)
                    CP("dve", Sx[H1, :, gl, ::-1], pbf(pk)[H1, :].rearrange("p (r j) -> p r j", r=2), [("pb", pk)], SXT + hT_all)
                for j1 in range(1, 16):
                    upd(slice(j1, 256, 16), slice(j1 - 1, 256, 16), 1, gh, 16, j1 % 2)
                for k_, p_ in enumerate((16, 17, 18, 19)):
                    sft = 1 << k_
                    upd(slice(16 * sft + 15, 256, 16), slice(15, 256 - 16 * sft, 16), p_, gh, 16 - sft, k_ % 2)
                for j1 in range(15):
                    upd(slice(16 + j1, 256, 16), slice(15, 240, 16), j1 + 1, gh, 15, j1 % 2)

            def s5_fs(gh):
                FST = ["Fs", "wu", "UJ"]
                S.op("pool", "memset", dict(ap=Fs[H0, :, :, 0:1], constant=0.0), (), FST)
                S.op("pool", "memset", dict(ap=Fs[H1, :, :, 255:256], constant=0.0), (), FST)
                ACT(Fs[H0, :, :, 1:256], Sx[H0, :, :, 0:255], AF.Copy, SXT, FST)
                CP("dve", Fs[H1, :, :, 0:255], Sx[H1, :, :, 254::-1], SXT, FST)

            def s5_out(gh):
                for jb in range(2):
                    jsl = slice(jb * 128, (jb + 1) * 128)
                    k_ = jb
                    for gl4 in range(4):
                        pk = 6 + gl4 % 2
                        for gg in range(4):
                            gl = gl4 * 4 + gg
                            g = gh * 16 + gl
                            osl = slice(gg * 128, (gg + 1) * 128)
                            MM(pbf(pk)[:, osl], Ub[:, g, jsl], WT[:, g, :], True, False, [("Ub", g // 8), "WT"], [("pb", pk)])
                            MM(pbf(pk)[:, osl], Fs[:, 0, gl, jsl], WFr[:, 0, g, :], False, False, ["Fs", "WFr"], [("pb", pk)])
                            MM(pbf(pk)[:, osl], Fs[:, 1, gl, jsl], WFr[:, 1, g, :], False, True, ["Fs", "WFr"], [("pb", pk)])
                        if gl4 % 2 == 0:
                            ACT(YJ[k_][:, :, gl4 * 64:(gl4 + 1) * 64].rearrange("p t (g c) -> p g t c", g=4),
                                pbf(pk).rearrange("p (g t c) -> p g t c", g=4, t=8), AF.Copy, [("pb", pk)], [("YJ", k_)])
                        else:
                            CP("dve", YJ[k_][:, :, gl4 * 64:(gl4 + 1) * 64].rearrange("p t (g c) -> p g t c", g=4),
                               pbf(pk).rearrange("p (g t c) -> p g t c", g=4, t=8), [("pb", pk)], [("YJ", k_)])
                    for cth in range(2):
                        ct = gh * 2 + cth
                        pk = 2 + cth
                        for t in range(8):
                            TR(pbb(pk)[:, t * 128:(t + 1) * 128], YJ[k_][:, t, cth * 128:(cth + 1) * 128], [("YJ", k_)], [("pb", pk)])
                        ACT(yg[:, ct, jb * 1024:(jb + 1) * 1024], pbb(pk), AF.Gelu_apprx_tanh, [("pb", pk)], [("yg", ct)])

            s5_scan(0)
            s5_fs(0)
            s5_scan(1)
            s5_out(0)
            s5_fs(1)
            s5_out(1)
            for ct in range(4):
                for tb in range(4):
                    pk = [0, 1, 4, 5][(ct * 4 + tb) % 4]
                    gi = (ct * 4 + tb) % 2
                    tsl = slice(tb * 512, (tb + 1) * 512)
                    for kt in range(4):
                        MM(pbf(pk), gluw[:, kt, ct * 128:(ct + 1) * 128], yg[:, kt, tsl], kt == 0, kt == 3, [("yg", k_) for k_ in range(4)] + ["gluw"], [("pb", pk)])
                    ACT(gt[gi][:, 0:512], pbf(pk), AF.Sigmoid, [("pb", pk), "glub"], [("gt", gi), ("tA", 0, "dve"), ("tB", 0, "dve"), ("tA", 0, "pool"), ("tB", 0, "pool")], bias=glub[:, ct:ct + 1])
                    jb_, t0_ = tb // 2, 4 * (tb % 2)
                    TT("pool", mixT[:, 4 + ct, jb_ * 1024:(jb_ + 1) * 1024].rearrange("p (j t) -> p t j", t=8)[:, t0_:t0_ + 4, :],
                       gt[gi][:, 0:512].rearrange("p (t j) -> p t j", t=4), yg[:, ct, tsl].rearrange("p (t j) -> p t j", t=4), ALU.mult,
                       [("gt", gi), ("yg", ct)], [("mixT", 4 + ct, jb_ * 2), ("mixT", 4 + ct, jb_ * 2 + 1)])
            if debug == "p2":
                DEBUG["yg"] = (yg.rearrange("p k t -> p (k t)"), [128, 4 * 2048], BF16, [("yg", c_) for c_ in range(4)])
                DEBUG["mix"] = (mixT.rearrange("p k t -> p (k t)"), [128, 8 * 2048], BF16, [("mixT", k_, b_) for k_ in range(8) for b_ in range(4)])
                DEBUG["Ub"] = (Ub.rearrange("p g j -> p (g j)"), [128, 32 * 256], BF16, [("Ub", g_) for g_ in range(4)])
                DEBUG["Sx"] = (Sx.rearrange("p r g j -> p (r g j)"), [128, 2 * 16 * 256], F32, SXT)
                break

            S.barrier()
            M.top = PBASE
            h2T = M.alloc(32768, BF16, "p (k t) -> p k t", k=8)
            x1 = M.alloc(65536, F32, "p (i c) -> p i c", i=16)
            P3 = M.top
            xt = [M.alloc(4096, F32) for _ in range(2)]
            hbs = [M.alloc(2048) for _ in range(2)]
            junk = M.alloc(2048)
            wo = M.alloc(16384, BF16, "p (k n) -> p k n", k=8)
            DMA("wo", wo, wO.rearrange("(k p) c -> p k c", p=128), wO_toks, ["wo"])
            mix_all = [("mixT", k_, b_) for k_ in range(8) for b_ in range(4)]
            pend_ev = None
            pend_tr = None
            ev_box = [None]

            def tr3a(i_, b_):
                pk_ = 4 + (i_ % 2)
                for kt in range(8):
                    TR(pbb(pk_)[:, kt * 128:(kt + 1) * 128], hbs[b_][:, kt * 128:(kt + 1) * 128], [("hb", b_)], [("pb", pk_)])
                if ev_box[0] is not None:
                    pi_, pp_ = ev_box[0]
                    ACT(h2T[:, :, pi_ * 128:(pi_ + 1) * 128], pbb(pp_).rearrange("p (k t) -> p k t", k=8), AF.Copy, [("pb", pp_)], [("h2T", pi_)])
                ev_box[0] = (i_, pk_)

            for i in range(16):
                b = i % 2
                DMA("xt%d" % b, xt[b], x_d[s, i * 128:(i + 1) * 128, :], (), [("xt", b)])
                for hf in range(2):
                    pk = (i % 2) * 2 + hf
                    hsl = slice(hf * 512, (hf + 1) * 512)
                    for kt in range(8):
                        MM(pbf(pk), mixT[:, kt, i * 128:(i + 1) * 128], wo[:, kt, hsl], kt == 0, kt == 7, mix_all + ["wo"], [("pb", pk)])
                    TT("dve", x1[:, i, hsl], pbf(pk), xt[b][:, hsl], ALU.add, [("pb", pk), ("xt", b)], [("x1", i, hf)])
                ACT(junk, x1[:, i, :], AF.Square, [("x1", i, 0), ("x1", i, 1)], ["junk", ("ss2", i)], accum_out=stat[:, 3, i:i + 1])
                ACT(stat[:, 4, i:i + 1], stat[:, 3, i:i + 1], AF.Sqrt, [("ss2", i), ("cst", CE)], [("sd2", i)], scale=1.0 / DM, bias=cst[:, CE:CE + 1])
                RCP(stat[:, 5, i:i + 1], stat[:, 4, i:i + 1], [("sd2", i)], [("rs2", i)])
                hb = hbs[b]
                TS("dve", hb, x1[:, i, :], stat[:, 5, i:i + 1], None, ALU.mult, None, [("x1", i, 0), ("x1", i, 1), ("rs2", i)], [("hb", b)])
                if pend_tr is not None:
                    tr3a(*pend_tr)
                pend_tr = (i, b)
            tr3a(*pend_tr)
            ACT(h2T[:, :, ev_box[0][0] * 128:(ev_box[0][0] + 1) * 128], pbb(ev_box[0][1]).rearrange("p (k t) -> p k t", k=8), AF.Copy, [("pb", ev_box[0][1])], [("h2T", ev_box[0][0])])
            h2T_all = [("h2T", i) for i in range(16)]
            if debug == "p3a":
                DEBUG["x1"] = (x1.rearrange("p i c -> p (i c)"), [128, 16 * 1024], F32, [("x1", i, hf) for i in range(16) for hf in range(2)])
                break

            S.barrier()
            M.top = P3
            aT = mixT_flat[:, 0:NCT * 512].rearrange("p (c t) -> p c t", c=NCT)
            accf = mixT_flat[:, NCT * 512:NCT * 512 + 4096].bitcast(F32)
            acc = [[accf[:, (sl * 2 + vg) * 512:(sl * 2 + vg + 1) * 512] for vg in range(2)] for sl in range(2)]
            NWU = 3
            wup = [M.alloc(4096, BF16, "p (k n) -> p k n", k=8) for _ in range(NWU)]
            wdn = M.alloc(NCT * 1024, BF16, "p (c n) -> p c n", c=NCT)
            dcnt = 0
            glb = [M.alloc(2048, F32) for _ in range(2)]
            yo = [M.alloc(4096, F32)]
            yo.append(yo[0])
            junk = glb[1].bitcast(BF16)
            oi = 0
            wuc = 0
            def ffn_tail(sl, c):
                ACT(glb[sl][:, 0:512], acc[sl][1], AF.Gelu_apprx_tanh, [("acc", sl, 1)], [("glb", sl)])
                TT("pool", aT[:, c, :], glb[sl][:, 0:512], acc[sl][0], ALU.mult, [("glb", sl), ("acc", sl, 0)], [("aT", c)])

            ffn_pend = None
            for q4 in range(4):
                t0 = q4 * 512
                tsl = slice(t0, t0 + 512)
                for c in range(NCT):
                    sl = c % 2
                    ws = wuc % NWU
                    wuc += 1
                    DMA("wup%d" % ws, wup[ws], wU[:, c * 2048:(c + 1) * 2048].rearrange("p (k n) -> p k n", k=8), wU_toks, [("wup", ws)])
                    hb_ = 6 + sl
                    for vg in range(2):
                        pk = sl * 2 + vg
                        wsl = slice(vg * 128, (vg + 1) * 128)
                        for kt in range(8):
                            MM(pbf(pk), wup[ws][:, kt, wsl], h2T[:, kt, tsl], kt == 0, kt == 7, h2T_all + [("wup", ws)], [("pb", pk)])
                    for vg in range(2):
                        wsl = slice(vg * 128, (vg + 1) * 128)
                        hoff = vg * 2
                        if 0 < q4 < 3:
                            for kt in range(8):
                                MM(pbf(hb_)[:, hoff:hoff + 2], wup[ws][:, kt, wsl], h2T[:, kt, t0 - 1:t0 + 513:513], kt == 0, kt == 7, h2T_all + [("wup", ws)], [("pb", hb_)])
                        elif q4 > 0:
                            for kt in range(8):
                                MM(pbf(hb_)[:, hoff:hoff + 1], wup[ws][:, kt, wsl], h2T[:, kt, t0 - 1:t0], kt == 0, kt == 7, h2T_all + [("wup", ws)], [("pb", hb_)])
                        else:
                            for kt in range(8):
                                MM(pbf(hb_)[:, hoff + 1:hoff + 2], wup[ws][:, kt, wsl], h2T[:, kt, t0 + 512:t0 + 513], kt == 0, kt == 7, h2T_all + [("wup", ws)], [("pb", hb_)])
                    for vg in range(2):
                        pk = sl * 2 + vg
                        hoff = vg * 2
                        ch = vg * NCT + c
                        A_ = acc[sl][vg]
                        atok = ("acc", sl, vg)
                        ACT(A_, pbf(pk), AF.Identity, [("pb", pk), "cw", "cb"], [atok], scale=cw[:, ch, 1:2], bias=cb[:, ch:ch + 1])
                        if q4 > 0:
                            ACT(A_[:, 0:1], pbf(hb_)[:, hoff:hoff + 1], AF.Identity, [("pb", hb_), atok, "cw"], [atok], scale=cw[:, ch, 0:1], bias=A_[:, 0:1])
                        if q4 < 3:
                            ACT(A_[:, 511:512], pbf(hb_)[:, hoff + 1:hoff + 2], AF.Identity, [("pb", hb_), atok, "cw"], [atok], scale=cw[:, ch, 2:3], bias=A_[:, 511:512])
                    for vg in range(2):
                        pk = sl * 2 + vg
                        ch = vg * NCT + c
                        A_ = acc[sl][vg]
                        atok = ("acc", sl, vg)
                        STT(A_[:, 1:512], pbf(pk)[:, 0:511], cw[:, ch, 0:1], A_[:, 1:512], ALU.mult, ALU.add, [("pb", pk), atok, "cw"], [atok])
                        STT(A_[:, 0:511], pbf(pk)[:, 1:512], cw[:, ch, 2:3], A_[:, 0:511], ALU.mult, ALU.add, [("pb", pk), atok, "cw"], [atok])
                    if ffn_pend is not None:
                        ffn_tail(*ffn_pend)
                    ffn_pend = (sl, c)
                ffn_tail(*ffn_pend)
                ffn_pend = None
                for hf in range(2):
                    hsl = slice(hf * 512, (hf + 1) * 512)
                    for c in range(NCT):
                        DMA("wdn%d" % c, wdn[:, c, :], wD[c * 128:(c + 1) * 128, hsl], wD_toks, [("wdn", c)])
                    for tt in range(4):
                        i = q4 * 4 + tt
                        pk = 4 + dcnt % 2
                        dcnt += 1
                        for c in range(NCT):
                            MM(pbf(pk), aT[:, c, tt * 128:(tt + 1) * 128], wdn[:, c, :], c == 0, c == NCT - 1, [("aT", c), ("wdn", c)], [("pb", pk)])
                        TT("dve", x1[:, i, hsl], pbf(pk), x1[:, i, hsl], ALU.add, [("pb", pk), ("x1", i, hf)], [("x1", i, hf)])
                        if hf == 1:
                            o_ = oi % 2
                            oi += 1
                            ACT(junk, x1[:, i, :], AF.Square, [("x1", i, 0), ("x1", i, 1)], ["junk", "ss3", ("glb", 1)], accum_out=stat3[:, 0:1])
                            ACT(stat3[:, 1:2], stat3[:, 0:1], AF.Sqrt, ["ss3", ("cst", CE)], ["sd3"], scale=1.0 / DM, bias=cst[:, CE:CE + 1])
                            RCP(stat3[:, 2:3], stat3[:, 1:2], ["sd3"], ["rs3"])
                            STT(yo[o_], x1[:, i, :], stat3[:, 2:3], nfbc, ALU.mult, ALU.mult, [("x1", i, 0), ("x1", i, 1), "rs3", "nfbc"], [("yo", 0)])
                            DMA("yo0", y_d[s, i * 128:(i + 1) * 128, :], yo[o_], [("yo", 0)], [("y", s, i)])

        fin = []
        if debug == "p3b":
            fin = [("y", 0, i) for i in range(16)]
        elif debug:
            for nm, (ap, shape, dt, toks) in DEBUG.items():
                dd = nc.dram_tensor("dbg_" + nm, list(shape), dt, kind="ExternalOutput").ap()
                dbg_out[nm] = (shape, dt)
                DMA("dbg_" + nm, dd, ap, toks, [("dbg", nm)])
                fin.append(("dbg", nm))
        else:
            fin = [("y", s, i) for s in range(NSEQ) for i in range(16)]
        NOP("sp", fin)
        S.emit()
        print("SBUF high water", M.hw, "instr counts", {e: len(v) for e, v in S.streams.items()})
    return nc, dbg_out


def _prep_weights(inp):
    f = np.float32
    w_in = np.asarray(inp["w_in"], f)[0]
    q, k, v, g, u = (w_in[:, i * 512:(i + 1) * 512] for i in range(5))

    def swap(w):
        w4 = w.reshape(DM, 4, 2, 64)
        return np.ascontiguousarray(w4[:, :, ::-1, :]).reshape(DM, 512)

    def inter(w):
        a = w.reshape(DM, 4, 1, 128)
        b = swap(w).reshape(DM, 4, 1, 128)
        return np.concatenate([a, b], axis=2).reshape(DM, 1024)

    w_in_a = np.ascontiguousarray(np.concatenate([inter(q), inter(k), g, v, u], axis=1))

    def col(vec, n):
        return np.ascontiguousarray(np.asarray(vec, f).reshape(n, 128).T)

    lre = np.asarray(inp["s5_lambda_re"], f)[0]
    lim = np.asarray(inp["s5_lambda_im"], f)[0]
    ldt = np.asarray(inp["s5_log_dt"], f)[0]

    def st2(a):
        return np.ascontiguousarray(a.transpose(0, 2, 1).reshape(128, 32))

    ldt_e = np.ascontiguousarray(np.broadcast_to(ldt[:, :, None], (2, 32, 64)))
    Bre = np.asarray(inp["s5_B_re"], f)[0]
    Bim = np.asarray(inp["s5_B_im"], f)[0]
    Cre = np.asarray(inp["s5_C_re"], f)[0]
    Cim = np.asarray(inp["s5_C_im"], f)[0]

    def b2(B):
        return np.ascontiguousarray(B.transpose(0, 2, 1, 3).reshape(128, 512))

    def c2(C):
        return np.ascontiguousarray(C.transpose(0, 3, 1, 2).reshape(128, 512))

    dsk = np.asarray(inp["s5_D"], f)[0]
    drep = np.ascontiguousarray(np.tile(dsk.reshape(32, 16).T, (8, 1)))

    conv_w = np.asarray(inp["conv_w"], f)[0]
    cwc = np.ascontiguousarray(conv_w.reshape(3, 44, 128).transpose(2, 1, 0).reshape(128, 132))
    d = {
        "w_in_a": w_in_a,
        "w_out": np.ascontiguousarray(np.asarray(inp["w_out"], f)[0]),
        "w_up": np.ascontiguousarray(np.asarray(inp["w_up"], f)[0]),
        "w_down": np.ascontiguousarray(np.asarray(inp["w_down"], f)[0]),
        "glu_w": np.ascontiguousarray(np.asarray(inp["s5_glu_w"], f)[0]),
        "nmix_c": col(np.asarray(inp["norm_mix"])[0], 8),
        "nffn_c": col(np.asarray(inp["norm_ffn"])[0], 8),
        "nfin_r": np.ascontiguousarray(np.asarray(inp["norm_final"], f).reshape(1, DM)),
        "gain_c": col(np.asarray(inp["ret_gn_gain"])[0], 4),
        "glub_c": col(np.asarray(inp["s5_glu_b"])[0], 4),
        "cw_c": cwc,
        "cb_c": col(np.asarray(inp["conv_b"])[0], 44),
        "lre2": st2(lre), "lim2": st2(lim), "ldt2": st2(ldt_e),
        "bre2": b2(Bre), "bim2": b2(Bim), "cre2": c2(Cre), "cim2": c2(Cim),
        "drep": drep,
    }
    return d


_CACHE = {}


def kernel(**inputs):
    xp = np.asarray(inputs["x_prompt"], np.float32)
    xs = np.asarray(inputs["x_sample"], np.float32)
    xall = np.concatenate([xp, xs], axis=0)
    wd = _prep_weights(inputs)
    if "nc" not in _CACHE:
        _CACHE["nc"] = build_program(False)[0]
    nc = _CACHE["nc"]
    in_maps = []
    for c in range(NCORES):
        m = dict(wd)
        m["x"] = np.ascontiguousarray(xall[c * NSEQ:(c + 1) * NSEQ])
        in_maps.append(m)
    res = run_bass_kernel_spmd(nc, in_maps, core_ids=list(range(NCORES)))
    yall = np.concatenate([np.asarray(r["y"], np.float32) for r in res.results], axis=0)
    return (np.ascontiguousarray(yall[0:8]), np.ascontiguousarray(yall[8:24]))
```

```python
import math
import contextlib
import numpy as np
import concourse.bass as bass
import concourse.mybir as mybir
from concourse.alu_op_type import AluOpType as ALU
from concourse.bass_utils import run_bass_kernel_spmd

F32 = mybir.dt.float32
BF16 = mybir.dt.bfloat16
AF = mybir.ActivationFunctionType

EPS = 1e-6
L = 2048
DM = 1024
NSEQ = 3
NCORES = 8
DFF = 2816
NCT = 22
LN_G = [math.log(1.0 - 2.0 ** (-5.0 - h)) for h in range(4)]
G128 = [math.exp(128.0 * lg) for lg in LN_G]
SC = 128.0 ** -0.5
TWO_PI = 2.0 * math.pi
MAGIC = 12582912.0
SIN_SCALE = 6.283185
WIN_COLS = 3584


class Sched:
    ENG = ("pe", "act", "dve", "pool", "sp")

    def __init__(self, nc):
        self.nc = nc
        self.streams = {e: [] for e in self.ENG}
        self.cnt = {}
        self.lastw = {}
        self.readers = {}
        self.waited = {e: {} for e in self.ENG}
        self.last_marker = {}
        self.pending = {e: [] for e in self.ENG}
        self.LIM = 32000

    def barrier(self):
        ms = list(self.last_marker.values())
        for e in self.ENG:
            self.pending[e] = list(ms)

    def op(self, eng, meth, kwargs, reads=(), writes=(), chan=None):
        base = chan if chan is not None else eng
        step = 16 if chan is not None else 1
        c = self.cnt.get(base, 0)
        epoch = c // self.LIM
        newc = c + step
        if newc > (epoch + 1) * self.LIM:
            epoch += 1
            c = epoch * self.LIM
            newc = c + step
        self.cnt[base] = newc
        sk = (base, epoch)
        marker = (sk, newc - epoch * self.LIM, eng if chan is None else None)
        deps = list(self.pending[eng])
        self.pending[eng] = []
        for t in reads:
            if t in self.lastw:
                deps.append(self.lastw[t])
        for t in writes:
            if t in self.lastw:
                deps.append(self.lastw[t])
            deps.extend(self.readers.get(t, ()))
        wd = self.waited[eng]
        m = {}
        for (k, v, de) in deps:
            if de == "pe" and eng == "pe" and chan is None:
                continue
            if wd.get(k, 0) >= v:
                continue
            m[k] = max(m.get(k, 0), v)
        for k, v in m.items():
            wd[k] = v
        self.streams[eng].append((meth, kwargs, list(m.items()), sk, step))
        for t in writes:
            self.lastw[t] = marker
            self.readers[t] = []
        for t in reads:
            self.readers.setdefault(t, []).append(marker)
        self.last_marker[base] = marker
        return marker

    def emit(self):
        nc = self.nc
        semkeys = set()
        for e in self.ENG:
            for (meth, kwargs, waits, sk, step) in self.streams[e]:
                semkeys.add(sk)
                for (k, v) in waits:
                    semkeys.add(k)
        semkeys = sorted(semkeys, key=str)
        with contextlib.ExitStack() as st:
            sems = {}
            for i, k in enumerate(semkeys):
                sems[k] = st.enter_context(nc.semaphore("s%d" % i))
            block = st.enter_context(nc.Block())
            engmap = {"pe": "tensor", "act": "scalar", "dve": "vector", "pool": "gpsimd", "sp": "sync"}

            def mk(e):
                def body(eng):
                    for (meth, kwargs, waits, sk, step) in self.streams[e]:
                        for (k, v) in waits:
                            eng.wait_ge(sems[k], v)
                        inst = getattr(eng, meth)(**kwargs)
                        inst.then_inc(sems[sk], step)
                return body

            for e in self.ENG:
                getattr(block, engmap[e])(mk(e))


class Mem:
    def __init__(self, big, cap):
        self.big = big
        self.cap = cap
        self.top = 0
        self.hw = 0

    def alloc(self, nbytes, dt=BF16, pattern=None, **kw):
        nbytes = (nbytes + 63) // 64 * 64
        off = self.top
        self.top += nbytes
        self.hw = max(self.hw, self.top)
        assert self.top <= self.cap, ("SBUF overflow", self.top, self.cap)
        ap = self.big[:, off // 2:(off + nbytes) // 2]
        if dt is F32:
            ap = ap.bitcast(F32)
        if pattern is not None:
            ap = ap.rearrange(pattern, **kw)
        return ap


def build_program(debug=None):
    nc = bass.Bass("TRN2", target_bir_lowering=False)
    DEBUG = {}

    def din(name, shape, dt=F32):
        return nc.dram_tensor(name, list(shape), dt, kind="ExternalInput").ap()

    x_d = din("x", [NSEQ, L, DM])
    y_d = nc.dram_tensor("y", [NSEQ, L, DM], F32, kind="ExternalOutput").ap()
    w_in_d = din("w_in_a", [DM, WIN_COLS])
    w_out_d = din("w_out", [DM, DM])
    w_up_d = din("w_up", [DM, 2 * DFF])
    w_down_d = din("w_down", [DFF, DM])
    glu_w_d = din("glu_w", [512, 512])
    nmix_d = din("nmix_c", [128, 8])
    nffn_d = din("nffn_c", [128, 8])
    nfin_d = din("nfin_r", [1, DM])
    gain_d = din("gain_c", [128, 4])
    glub_d = din("glub_c", [128, 4])
    cw_d = din("cw_c", [128, 44 * 3])
    cb_d = din("cb_c", [128, 44])
    lre2_d = din("lre2", [128, 32])
    lim2_d = din("lim2", [128, 32])
    ldt2_d = din("ldt2", [128, 32])
    bre2_d = din("bre2", [128, 512])
    bim2_d = din("bim2", [128, 512])
    cre2_d = din("cre2", [128, 512])
    cim2_d = din("cim2", [128, 512])
    drep_d = din("drep", [128, 32])

    wI = nc.dram_tensor("wI_s", [128, 8 * WIN_COLS], BF16).ap()
    wO = nc.dram_tensor("wO_s", [DM, DM], BF16).ap()
    wU = nc.dram_tensor("wU_s", [128, NCT * 2048], BF16).ap()
    wD = nc.dram_tensor("wD_s", [DFF, DM], BF16).ap()
    s5Wtoep = nc.dram_tensor("s5Wtoep_s", [128, 4096], BF16).ap()
    s5Wsum = nc.dram_tensor("s5Wsum_s", [128, 8192], BF16).ap()
    s5Wfar = nc.dram_tensor("s5Wfar_s", [128, 8192], BF16).ap()

    dbg_out = {}
    CAP = 207 * 1024
    with contextlib.ExitStack() as st:
        big = st.enter_context(nc.sbuf_tensor("big", [128, CAP // 2], BF16))
        pbs = [st.enter_context(nc.psum_tensor("pb%d" % i, [128, 512], F32)) for i in range(8)]
        S = Sched(nc)
        M = Mem(big, CAP)

        def pbf(i):
            return pbs[i][:]

        def pbb(i):
            return pbs[i][:].bitcast(BF16)

        def DMA(chan, out, in_, reads=(), writes=(), eng="sp"):
            return S.op(eng, "dma_start", dict(out=out, in_=in_), reads, writes, chan=chan)

        def ACT(out, in_, func, reads, writes, **kw):
            return S.op("act", "activation", dict(out=out, in_=in_, func=func, **kw), reads, writes)

        def STT(out, in0, scalar, in1, op0, op1, reads, writes):
            return S.op("dve", "scalar_tensor_tensor", dict(out=out, in0=in0, scalar=scalar, in1=in1, op0=op0, op1=op1), reads, writes)

        def TT(eng, out, in0, in1, op, reads, writes):
            return S.op(eng, "tensor_tensor", dict(out=out, in0=in0, in1=in1, op=op), reads, writes)

        def TS(eng, out, in0, s1, s2, op0, op1, reads, writes):
            if s2 is None:
                return S.op(eng, "tensor_scalar", dict(out=out, in0=in0, scalar1=s1, scalar2=None, op0=op0), reads, writes)
            return S.op(eng, "tensor_scalar", dict(out=out, in0=in0, scalar1=s1, scalar2=s2, op0=op0, op1=op1), reads, writes)

        def TSS(eng, out, in_, scalar, op, reads, writes):
            return S.op(eng, "tensor_single_scalar", dict(out=out, in_=in_, scalar=scalar, op=op), reads, writes)

        def CP(eng, out, in_, reads, writes):
            return S.op(eng, "tensor_copy", dict(out=out, in_=in_), reads, writes)

        def MM(out, lhsT, rhs, start, stop, reads, writes):
            return S.op("pe", "matmul", dict(out=out, lhsT=lhsT, rhs=rhs, start=start, stop=stop), reads, writes)

        def TR(out, in_, reads, writes):
            return S.op("pe", "transpose", dict(out=out, in_=in_, identity=ident), list(reads) + ["ident"], writes)

        def RCP(out, in_, reads, writes):
            return S.op("dve", "reciprocal", dict(out=out, in_=in_), reads, writes)

        def NOP(eng, reads, writes=()):
            return S.op(eng, "nop", dict(), reads, writes)

        ident = M.alloc(256)
        onesm = M.alloc(256)
        Pm = M.alloc(256)
        COS = M.alloc(4096)
        SINS = M.alloc(4096)
        Dm = M.alloc(2048, F32, "p (h n) -> p h n", h=4)
        XIF = M.alloc(2048, F32, "p (h n) -> p h n", h=4)
        XIB = M.alloc(2048, F32, "p (h n) -> p h n", h=4)
        ZF = M.alloc(64, F32)
        ZB = M.alloc(64, F32)
        cst = M.alloc(128, F32)
        gain = M.alloc(64, F32)
        glub = M.alloc(64, F32)
        gluw = M.alloc(4096, BF16, "p (k n) -> p k n", k=4)
        cw = M.alloc(44 * 3 * 4, F32)[:, 0:132].rearrange("p (c k) -> p c k", k=3)
        cb = M.alloc(44 * 4, F32)
        nfbc = M.alloc(4096, F32)
        MPr = M.alloc(20 * 128, F32, "p (q g) -> p q g", q=20)
        MPi = M.alloc(20 * 128, F32, "p (q g) -> p q g", q=20)
        MPB = M.alloc(20 * 256, F32, "p (q r g) -> p q r g", q=20, r=2)
        mixT_flat = M.alloc(32768, BF16)
        mixT = mixT_flat.rearrange("p (k t) -> p k t", k=8)
        stat = M.alloc(6 * 16 * 4, F32, "p (a i) -> p a i", a=6)
        stat3 = M.alloc(64, F32)
        PBASE = M.top

        CE, CLSC = 0, 1
        cvals = {CE: EPS, CLSC: math.log(SC)}
        for h in range(4):
            cvals[4 + h] = LN_G[h]
            cvals[8 + h] = 128.0 * LN_G[h]
            cvals[12 + h] = 127.0 * LN_G[h] + math.log(SC)
        for k, v in cvals.items():
            S.op("pool", "memset", dict(ap=cst[:, k:k + 1], constant=float(v)), (), [("cst", k)])

        M.top = PBASE
        identf = M.alloc(512, F32)
        pidx = M.alloc(64, F32)
        hi = M.alloc(64, F32)
        imod = M.alloc(64, F32)
        invf = M.alloc(64, F32)
        sgn = M.alloc(64, F32)
        nd = M.alloc(512, F32)
        nidx = M.alloc(512, F32)
        SMALL_END = M.top
        lidx = M.alloc(8192, F32)
        ta = M.alloc(8192, F32)
        tb_ = M.alloc(8192, F32)
        tc_ = M.alloc(8192, F32)

        S.op("pool", "memset", dict(ap=identf, constant=0.0), (), ["identf"])
        S.op("pool", "affine_select", dict(out=identf, in_=identf, pattern=[[-1, 128]], compare_op=ALU.not_equal,
                                           fill=1.0, base=0, channel_multiplier=1), ["identf"], ["identf"])
        CP("dve", ident, identf, ["identf"], ["ident"])
        Pmf = M.alloc(512, F32)
        S.op("pool", "memset", dict(ap=Pmf, constant=0.0), (), ["Pmf"])
        S.op("pool", "affine_select", dict(out=Pmf, in_=Pmf, pattern=[[-1, 128]], compare_op=ALU.not_equal,
                                           fill=1.0, base=-64, channel_multiplier=1), ["Pmf"], ["Pmf"])
        S.op("pool", "affine_select", dict(out=Pmf, in_=Pmf, pattern=[[-1, 128]], compare_op=ALU.not_equal,
                                           fill=1.0, base=64, channel_multiplier=1), ["Pmf"], ["Pmf"])
        CP("dve", Pm, Pmf, ["Pmf"], ["Pm"])
        S.op("pool", "memset", dict(ap=onesm, constant=1.0 / 128.0), (), ["onesm"])
        S.op("pool", "iota", dict(out=pidx[:, 0:1], pattern=[[0, 1]], base=0, channel_multiplier=1, allow_small_or_imprecise_dtypes=True), (), ["pidx"])
        S.op("pool", "iota", dict(out=lidx, pattern=[[1, 2048]], base=0, channel_multiplier=0, allow_small_or_imprecise_dtypes=True), (), ["lidx"])
        S.op("pool", "iota", dict(out=nd, pattern=[[1, 128]], base=0, channel_multiplier=-1, allow_small_or_imprecise_dtypes=True), (), ["nd"])
        S.op("pool", "iota", dict(out=nidx, pattern=[[1, 128]], base=0, channel_multiplier=0, allow_small_or_imprecise_dtypes=True), (), ["nidx"])
        TSS("dve", hi[:, 0:1], pidx[:, 0:1], 64.0, ALU.is_ge, ["pidx"], ["hi"])
        STT(imod[:, 0:1], hi[:, 0:1], -64.0, pidx[:, 0:1], ALU.mult, ALU.add, ["hi", "pidx"], ["imod"])
        ACT(invf[:, 0:1], imod[:, 0:1], AF.Exp, ["imod"], ["invf"], scale=-math.log(10000.0) / 64.0)
        TS("dve", sgn[:, 0:1], hi[:, 0:1], 2.0, -1.0, ALU.mult, ALU.add, ["hi"], ["sgn"])

        def sincos(a_ap, tA, tB, out_ap, tok_a, tok_A, tok_B, tok_out, quarter, post_scalar=None, deps=()):
            if quarter != 0.0:
                TSS("dve", tA, a_ap, quarter, ALU.add, [tok_a], [tok_A])
                src, stok = tA, tok_A
            else:
                src, stok = a_ap, tok_a
            TSS("dve", tB, src, MAGIC, ALU.add, [stok], [tok_B])
            TSS("dve", tB, tB, MAGIC, ALU.subtract, [tok_B], [tok_B])
            TT("dve", tA, src, tB, ALU.subtract, [stok, tok_B], [tok_A])
            ACT(tA, tA, AF.Sin, [tok_A], [tok_A], scale=SIN_SCALE)
            if post_scalar is None:
                CP("dve", out_ap, tA, [tok_A] + list(deps), [tok_out])
            else:
                TS("dve", out_ap, tA, post_scalar, None, ALU.mult, None, [tok_A] + list(deps), [tok_out])

        TS("dve", ta, lidx, invf[:, 0:1], 1.0 / TWO_PI, ALU.mult, ALU.mult, ["lidx", "invf"], ["ta"])
        sincos(ta, tb_, tc_, SINS, "ta", "tb", "tc", "SINS", 0.0, post_scalar=sgn[:, 0:1], deps=["sgn"])
        sincos(ta, tb_, tc_, COS, "ta", "tb", "tc", "COS", 0.25)
        ACT(nd, nd, AF.Abs, ["nd"], ["nd"])
        for h in range(4):
            ACT(Dm[:, h, :], nd, AF.Exp, ["nd", ("cst", CLSC)], [("Dm", h)], scale=LN_G[h], bias=cst[:, CLSC:CLSC + 1])
            ACT(XIF[:, h, :], nidx, AF.Exp, ["nidx", ("cst", 4 + h)], [("XIF", h)], scale=LN_G[h], bias=cst[:, 4 + h:5 + h])
            ACT(XIB[:, h, :], nidx, AF.Exp, ["nidx", ("cst", 8 + h)], [("XIB", h)], scale=-LN_G[h], bias=cst[:, 8 + h:9 + h])
            ACT(ZF[:, h:h + 1], pidx[:, 0:1], AF.Exp, ["pidx", ("cst", 12 + h)], [("ZF", h)], scale=-LN_G[h], bias=cst[:, 12 + h:13 + h])
            ACT(ZB[:, h:h + 1], pidx[:, 0:1], AF.Exp, ["pidx", ("cst", CLSC)], [("ZB", h)], scale=LN_G[h], bias=cst[:, CLSC:CLSC + 1])

        DMA("ld0", gain[:, 0:4], gain_d, (), ["gain"])
        DMA("ld1", glub[:, 0:4], glub_d, (), ["glub"])
        DMA("ld3", cw.rearrange("p c k -> p (c k)"), cw_d, (), ["cw"])
        DMA("ld4", cb[:, 0:44], cb_d, (), ["cb"])
        DMA("ld5", nfbc, nfin_d.partition_broadcast(128), (), ["nfbc"])
        DMA("ldc0", gluw, glu_w_d.rearrange("(k p) n -> p k n", p=128), (), ["gluw"], eng="pool")
        S.barrier()
        SBASE = M.top
        M.top = SMALL_END
        sm = {}
        for nm in ["lr", "li", "dt", "t0", "t1", "t2", "AR", "AI", "CR", "CI", "IR", "II", "u0", "u1", "u2", "u3"]:
            sm[nm] = M.alloc(128, F32)
        POSr = M.alloc(9 * 128, F32, "p (q g) -> p q g", q=9)
        POSi = M.alloc(9 * 128, F32, "p (q g) -> p q g", q=9)
        NEGr = M.alloc(8 * 128, F32, "p (q g) -> p q g", q=8)
        NEGi = M.alloc(8 * 128, F32, "p (q g) -> p q g", q=8)
        Et = {}
        for nm in ["q", "k", "s", "w"]:
            Et[nm] = (M.alloc(1024, F32, "p (g t) -> p g t", g=32), M.alloc(1024, F32, "p (g t) -> p g t", g=32))
        Braw_r = M.alloc(2048, F32, "p (g c) -> p g c", g=32)
        Braw_i = M.alloc(2048, F32, "p (g c) -> p g c", g=32)
        BBr = M.alloc(2048, F32, "p (g c) -> p g c", g=32)
        BBi = M.alloc(2048, F32, "p (g c) -> p g c", g=32)
        Cr_ = M.alloc(2048, F32, "p (g c) -> p g c", g=32)
        Ci_ = M.alloc(2048, F32, "p (g c) -> p g c", g=32)
        drep = M.alloc(128, F32)
        pq = M.alloc(64, F32)
        fq = M.alloc(512, F32)
        MF = M.alloc(512, F32)
        MB = M.alloc(512, F32)
        bigs = [M.alloc(16384, F32) for _ in range(5)]
        stg = [M.alloc(8192, BF16)]
        stg.append(stg[0])
        tp_f = [M.alloc(512, F32) for _ in range(2)]

        DMA("ld6", sm["lr"], lre2_d, (), ["v_lr"])
        DMA("ld7", sm["li"], lim2_d, (), ["v_li"])
        DMA("ld8", sm["dt"], ldt2_d, (), ["v_dt"])
        DMA("ld9", Braw_r.rearrange("p g c -> p (g c)"), bre2_d, (), ["Braw_r"])
        DMA("ld10", Braw_i.rearrange("p g c -> p (g c)"), bim2_d, (), ["Braw_i"])
        DMA("ld11", Cr_.rearrange("p g c -> p (g c)"), cre2_d, (), ["Cr"])
        DMA("ld12", Ci_.rearrange("p g c -> p (g c)"), cim2_d, (), ["Ci"])
        DMA("ld13", drep[:, 0:32], drep_d, (), ["drep"])

        def abar(lr, li, dt, t0, t1, t2, out_r, out_i, pfx):
            T = lambda n: pfx + n
            TSS("dve", lr, lr, -1e-4, ALU.min, [T("lr")], [T("lr")])
            ACT(dt, dt, AF.Exp, [T("dt")], [T("dt")])
            TT("dve", t0, lr, dt, ALU.mult, [T("lr"), T("dt")], [T("t0")])
            TS("dve", t0, t0, -8.0, 1.0 / 16.0, ALU.max, ALU.mult, [T("t0")], [T("t0")])
            TSS("dve", out_r, t0, 1.0 / math.factorial(8), ALU.mult, [T("t0")], [T("or")])
            for k in range(7, 0, -1):
                STT(out_r, out_r, 1.0 / math.factorial(k), t0, ALU.add, ALU.mult, [T("or"), T("t0")], [T("or")])
            TSS("dve", t0, out_r, 1.0, ALU.add, [T("or")], [T("t0")])
            for _ in range(4):
                TT("dve", t0, t0, t0, ALU.mult, [T("t0")], [T("t0")])
            STT(t1, li, 1.0 / TWO_PI, dt, ALU.mult, ALU.mult, [T("li"), T("dt")], [T("t1")])
            TSS("dve", t2, t1, MAGIC, ALU.add, [T("t1")], [T("t2")])
            TSS("dve", t2, t2, MAGIC, ALU.subtract, [T("t2")], [T("t2")])
            TT("dve", t1, t1, t2, ALU.subtract, [T("t1"), T("t2")], [T("t1")])
            TSS("dve", t1, t1, math.pi, ALU.mult, [T("t1")], [T("t1")])
            TT("dve", t2, t1, t1, ALU.mult, [T("t1")], [T("t2")])
            TSS("dve", out_i, t2, 1.0 / math.factorial(13), ALU.mult, [T("t2")], [T("oi")])
            for k in range(5, 0, -1):
                STT(out_i, out_i, ((-1.0) ** k) / math.factorial(2 * k + 1), t2, ALU.add, ALU.mult, [T("oi"), T("t2")], [T("oi")])
            STT(out_i, out_i, 1.0, t1, ALU.add, ALU.mult, [T("oi"), T("t1")], [T("oi")])
            TSS("dve", out_r, t2, -1.0 / math.factorial(14), ALU.mult, [T("t2"), T("t0")], [T("or")])
            for k in range(6, 0, -1):
                STT(out_r, out_r, ((-1.0) ** k) / math.factorial(2 * k), t2, ALU.add, ALU.mult, [T("or"), T("t2")], [T("or")])
            TSS("dve", out_r, out_r, 1.0, ALU.add, [T("or")], [T("or")])
            TT("dve", t2, out_i, out_i, ALU.mult, [T("oi"), T("or")], [T("t2")])
            TS("dve", t2, t2, -2.0, 1.0, ALU.mult, ALU.add, [T("t2")], [T("t2")])
            STT(out_i, out_i, 2.0, out_r, ALU.mult, ALU.mult, [T("oi"), T("or")], [T("oi")])
            TT("dve", out_r, t2, t0, ALU.mult, [T("t2"), T("t0"), T("oi")], [T("or")])
            TT("dve", out_i, out_i, t0, ALU.mult, [T("oi"), T("t0")], [T("oi")])

        AR, AI = sm["AR"], sm["AI"]
        abar(sm["lr"], sm["li"], sm["dt"], sm["t0"], sm["t1"], sm["t2"], AR, AI, "v_")
        nmix = M.alloc(64, F32)
        nffn = M.alloc(64, F32)
        PCH = 1536
        wtmp = [M.alloc(PCH * 4, F32) for _ in range(2)]
        wob = [M.alloc(PCH * 2, BF16) for _ in range(2)]
        DMA("ld0", nmix[:, 0:8], nmix_d, (), ["nmix"])
        DMA("ld1", nffn[:, 0:8], nffn_d, (), ["nffn"])
        prep_i = [0]

        def prep(src_, r0, c0, ncols, scale_ap, scale_tok, outs):
            i = prep_i[0] % 2
            prep_i[0] += 1
            DMA("wpi%d" % i, wtmp[i][:, 0:ncols], src_[r0:r0 + 128, c0:c0 + ncols], (), [("wtmp", i)])
            if scale_ap is None:
                ACT(wob[i][:, 0:ncols], wtmp[i][:, 0:ncols], AF.Copy, [("wtmp", i)], [("wob", i)])
            else:
                ACT(wob[i][:, 0:ncols], wtmp[i][:, 0:ncols], AF.Copy, [("wtmp", i), scale_tok], [("wob", i)], scale=scale_ap)
            for n_, (dst_ap, vf, tok) in enumerate(outs):
                DMA("wpo%d%s" % (i, "abc"[n_]), dst_ap, vf(wob[i]), [("wob", i)], [tok], eng="act")

        wI_toks, wO_toks, wU_toks, wD_toks = [], [], [], []
        wI_A = wI[:, 0:16384].rearrange("p (u k c) -> p u k c", u=8, k=8)
        wI_B = wI[:, 16384:20480].rearrange("p (u k c) -> p u k c", u=4, k=8)
        wI_C = wI[:, 20480:28672].rearrange("p (u k c) -> p u k c", u=2, k=8)
        for kt in range(8):
            prep(w_in_d, kt * 128, 0, 1536, nmix[:, kt:kt + 1], "nmix",
                 [(wI_A[:, 0:6, kt, :], lambda w: w[:, 0:1536].rearrange("p (u c) -> p u c", u=6), ("wI", kt, 0))])
            prep(w_in_d, kt * 128, 1536, 1536, nmix[:, kt:kt + 1], "nmix",
                 [(wI_A[:, 6:8, kt, :], lambda w: w[:, 0:512].rearrange("p (u c) -> p u c", u=2), ("wI", kt, 1)),
                  (wI_B[:, :, kt, :], lambda w: w[:, 512:1024].rearrange("p (u c) -> p u c", u=4), ("wI", kt, 2)),
                  (wI_C[:, 0, kt, :], lambda w: w[:, 1024:1536], ("wI", kt, 3))])
            prep(w_in_d, kt * 128, 3072, 512, nmix[:, kt:kt + 1], "nmix",
                 [(wI_C[:, 1, kt, :], lambda w: w[:, 0:512], ("wI", kt, 4))])
            wI_toks += [("wI", kt, n_) for n_ in range(5)]
        for kt in range(8):
            prep(w_out_d, kt * 128, 0, DM, None, None, [(wO[kt * 128:(kt + 1) * 128, :], lambda w: w[:, 0:DM], ("wO", kt))])
            wO_toks.append(("wO", kt))
        wU_v = wU.rearrange("p (c k v n) -> p c k v n", c=NCT, k=8, v=2)
        for kt in range(8):
            for hf in range(2):
                for ch_ in range(2):
                    prep(w_up_d, kt * 128, hf * DFF + ch_ * 1408, 1408, nffn[:, kt:kt + 1], "nffn",
                         [(wU_v[:, ch_ * 11:(ch_ + 1) * 11, kt, hf, :], lambda w: w[:, 0:1408].rearrange("p (c n) -> p c n", c=11), ("wU", kt, hf, ch_))])
                    wU_toks.append(("wU", kt, hf, ch_))
        for c in range(NCT):
            prep(w_down_d, c * 128, 0, DM, None, None, [(wD[c * 128:(c + 1) * 128, :], lambda w: w[:, 0:DM], ("wD", c))])
            wD_toks.append(("wD", c))

        ATOK = ["v_or", "v_oi"]
        lr, li = sm["lr"], sm["li"]
        u0, u1, u2, u3 = sm["u0"], sm["u1"], sm["u2"], sm["u3"]
        TT("dve", u0, lr, lr, ALU.mult, ["v_lr"] + ATOK, ["u0"])
        TT("dve", u1, li, li, ALU.mult, ["v_li"], ["u1"])
        TT("dve", u0, u0, u1, ALU.add, ["u0", "u1"], ["u0"])
        RCP(u0, u0, ["u0"], ["u0"])
        TSS("dve", u1, AR, -1.0, ALU.add, ATOK + ["u1"], ["u1"])
        TT("dve", u2, u1, lr, ALU.mult, ["u1", "v_lr"], ["u2"])
        TT("dve", u3, AI, li, ALU.mult, ATOK + ["v_li"], ["u3"])
        TT("dve", u2, u2, u3, ALU.add, ["u2", "u3"], ["u2"])
        TT("dve", sm["CR"], u2, u0, ALU.mult, ["u2", "u0"], ["CR"])
        TT("dve", u2, AI, lr, ALU.mult, ATOK + ["v_lr", "CR"], ["u2"])
        TT("dve", u3, u1, li, ALU.mult, ["u1", "v_li", "CR"], ["u3"])
        TT("dve", u2, u2, u3, ALU.subtract, ["u2", "u3"], ["u2"])
        TT("dve", sm["CI"], u2, u0, ALU.mult, ["u2", "u0"], ["CI"])
        TT("dve", u0, AR, AR, ALU.mult, ATOK + ["CI", "u0"], ["u0"])
        TT("dve", u1, AI, AI, ALU.mult, ATOK + ["CI", "u1"], ["u1"])
        TT("dve", u0, u0, u1, ALU.add, ["u0", "u1"], ["u0"])
        RCP(u0, u0, ["u0"], ["u0"])
        TT("dve", sm["IR"], AR, u0, ALU.mult, ATOK + ["u0"], ["IR"])
        STT(sm["II"], AI, -1.0, u0, ALU.mult, ALU.mult, ATOK + ["u0"], ["II"])

        def cmul(orr, oii, ar, ai, br, bi, toks_in, tok_out):
            TT("dve", u2, ar, br, ALU.mult, toks_in + ["u2"], ["u2"])
            TT("dve", u3, ai, bi, ALU.mult, toks_in + ["u3"], ["u3"])
            TT("dve", orr, u2, u3, ALU.subtract, ["u2", "u3"], [tok_out + "r"])
            TT("dve", u2, ar, bi, ALU.mult, toks_in + [tok_out + "r"], ["u2"])
            TT("dve", u3, ai, br, ALU.mult, toks_in + [tok_out + "r"], ["u3"])
            TT("dve", oii, u2, u3, ALU.add, ["u2", "u3"], [tok_out + "i"])

        S.op("dve", "memset", dict(ap=POSr[:, 0, :], constant=1.0), (), ["POS0r"])
        S.op("dve", "memset", dict(ap=POSi[:, 0, :], constant=0.0), (), ["POS0i"])
        S.op("dve", "memset", dict(ap=NEGr[:, 0, :], constant=1.0), (), ["NEG0r"])
        S.op("dve", "memset", dict(ap=NEGi[:, 0, :], constant=0.0), (), ["NEG0i"])
        CP("dve", POSr[:, 1, :], AR, ATOK, ["POS1r"])
        CP("dve", POSi[:, 1, :], AI, ATOK, ["POS1i"])
        CP("dve", NEGr[:, 1, :], sm["IR"], ["IR"], ["NEG1r"])
        CP("dve", NEGi[:, 1, :], sm["II"], ["II"], ["NEG1i"])
        for p in range(2, 9):
            cmul(POSr[:, p, :], POSi[:, p, :], POSr[:, p - 1, :], POSi[:, p - 1, :], POSr[:, 1, :], POSi[:, 1, :],
                 ["POS%dr" % (p - 1), "POS%di" % (p - 1), "POS1r", "POS1i"], "POS%d" % p)
        for p in range(2, 8):
            cmul(NEGr[:, p, :], NEGi[:, p, :], NEGr[:, p - 1, :], NEGi[:, p - 1, :], NEGr[:, 1, :], NEGi[:, 1, :],
                 ["NEG%dr" % (p - 1), "NEG%di" % (p - 1), "NEG1r", "NEG1i"], "NEG%d" % p)
        POST = [("POS%d" % p) + x for p in range(9) for x in "ri"]
        NEGT = [("NEG%d" % p) + x for p in range(8) for x in "ri"]
        CP("dve", MPr[:, 1, :], POSr[:, 8, :], POST, ["MP1r"])
        CP("dve", MPi[:, 1, :], POSi[:, 8, :], POST, ["MP1i"])
        for p in range(2, 17):
            cmul(MPr[:, p, :], MPi[:, p, :], MPr[:, p - 1, :], MPi[:, p - 1, :], MPr[:, 1, :], MPi[:, 1, :],
                 ["MP%dr" % (p - 1), "MP%di" % (p - 1), "MP1r", "MP1i"], "MP%d" % p)
        for p, q_ in ((17, 16), (18, 17), (19, 18)):
            cmul(MPr[:, p, :], MPi[:, p, :], MPr[:, q_, :], MPi[:, q_, :], MPr[:, q_, :], MPi[:, q_, :],
                 ["MP%dr" % q_, "MP%di" % q_], "MP%d" % p)
        MPT0 = [("MP%d" % p) + x for p in range(1, 20) for x in "ri"]
        TSS("dve", MPB[:, 1:20, 0, :], MPi[:, 1:20, :], -1.0, ALU.mult, MPT0, ["MPB0"])
        CP("dve", MPB[:, 1:20, 1, :], MPi[:, 1:20, :], MPT0, ["MPB1"])
        MPT = MPT0 + ["MPB0", "MPB1"]
        H0, H1 = slice(0, 64), slice(64, 128)
        for t in range(8):
            for (nm, src0, i0, src1, i1) in (("q", "POS", t, "NEG", t), ("k", "NEG", t, "POS", t), ("s", "POS", 7 - t, "POS", t), ("w", "POS", t + 1, "POS", 8 - t)):
                for ri in range(2):
                    tabs = {"POS": (POSr, POSi), "NEG": (NEGr, NEGi)}
                    CP("pool", Et[nm][ri][H0, :, t], tabs[src0][ri][H0, i0, :], POST + NEGT, [("E", nm, ri)])
                    CP("pool", Et[nm][ri][H1, :, t], tabs[src1][ri][H1, i1, :], POST + NEGT, [("E", nm, ri)])
        def bc_g(ap):
            return ap[:, 0:32].unsqueeze(2).to_broadcast([128, 32, 16])
        TT("dve", BBr, Braw_r, bc_g(sm["CR"]), ALU.mult, ["Braw_r", "CR"], ["BBr"])
        TT("dve", BBi, Braw_i, bc_g(sm["CI"]), ALU.mult, ["Braw_i", "CI"], ["BBi"])
        TT("dve", BBr, BBr, BBi, ALU.subtract, ["BBr", "BBi"], ["BBr"])
        TT("dve", BBi, Braw_i, bc_g(sm["CR"]), ALU.mult, ["Braw_i", "CR", "BBr"], ["BBi"])
        TT("dve", Braw_r, Braw_r, bc_g(sm["CI"]), ALU.mult, ["Braw_r", "CI", "BBr"], ["Braw_r"])
        TT("dve", BBi, BBi, Braw_r, ALU.add, ["BBi", "Braw_r"], ["BBi"])

        def v4(ap):
            return ap.rearrange("p (g t c) -> p g t c", g=32, t=8)

        def bX(ap):
            return ap.unsqueeze(2).to_broadcast([128, 32, 8, 16])

        def bE(ap):
            return ap.unsqueeze(3).to_broadcast([128, 32, 8, 16])

        def BG(k):
            return ("big", k)

        def cprod(ko_r, ko_i, Xr, Xi, xtoks, nm, k1, neg_im=False):
            Er, Ei = Et[nm]
            et = [("E", nm, 0), ("E", nm, 1)]
            outr, outi, t1_ = bigs[ko_r], bigs[ko_i], bigs[k1]
            TT("dve", v4(outr), bX(Xr), bE(Er), ALU.mult, xtoks + et, [BG(ko_r)])
            TT("dve", v4(t1_), bX(Xi), bE(Ei), ALU.mult, xtoks + et, [BG(k1)])
            TT("dve", outr, outr, t1_, ALU.subtract, [BG(ko_r), BG(k1)], [BG(ko_r)])
            TT("dve", v4(outi), bX(Xr), bE(Ei), ALU.mult, xtoks + et, [BG(ko_i)])
            TT("dve", v4(t1_), bX(Xi), bE(Er), ALU.mult, xtoks + et, [BG(k1)])
            if neg_im:
                STT(outi, outi, -1.0, t1_, ALU.mult, ALU.subtract, [BG(ko_i), BG(k1)], [BG(ko_i)])
            else:
                TT("dve", outi, outi, t1_, ALU.add, [BG(ko_i), BG(k1)], [BG(ko_i)])

        def TRF(out, in_, reads, writes):
            return S.op("pe", "transpose", dict(out=out, in_=in_, identity=identf), list(reads) + ["identf"], writes)

        CT = ["Cr", "Ci"]
        BT = ["BBr", "BBi"]
        cprod(0, 1, Cr_, Ci_, CT, "w", 2, neg_im=True)
        CP("dve", stg[0], bigs[0], [BG(0)], [("stg", 0)])
        DMA("st0", s5Wfar[:, 0:4096], stg[0], [("stg", 0)], ["s5Wfar0"], eng="pool")
        CP("dve", stg[1], bigs[1], [BG(1)], [("stg", 0)])
        DMA("st0", s5Wfar[:, 4096:8192], stg[1], [("stg", 0)], ["s5Wfar1"], eng="pool")
        cprod(0, 1, BBr, BBi, BT, "s", 2)
        nb = 0
        for ri in range(2):
            stv = stg[ri].rearrange("p (g m) -> p g m", g=32)
            for g4 in range(8):
                bk = 2 + nb % 4
                nb += 1
                for gg in range(4):
                    g = g4 * 4 + gg
                    TRF(pbf(bk)[:, gg * 128:(gg + 1) * 128], bigs[ri][:, g * 128:(g + 1) * 128], [BG(ri)], [("pb", bk)])
                CP("dve", stv[:, g4 * 4:(g4 + 1) * 4, :], pbf(bk).rearrange("p (g m) -> p g m", g=4), [("pb", bk)], [("stg", 0)])
            DMA("st0", s5Wsum[:, ri * 4096:(ri + 1) * 4096], stg[ri], [("stg", 0)], ["s5Wsum%d" % ri], eng="pool")
        cprod(0, 1, Cr_, Ci_, CT, "q", 4, neg_im=True)
        cprod(2, 3, BBr, BBi, BT, "k", 4)
        TS("dve", pq[:, 0:1], pidx[:, 0:1], 1.0 / 16.0, -15.0 / 32.0, ALU.mult, ALU.add, ["pidx"], ["pq"])
        TSS("dve", pq[:, 0:1], pq[:, 0:1], MAGIC, ALU.add, ["pq"], ["pq"])
        TSS("dve", pq[:, 0:1], pq[:, 0:1], MAGIC, ALU.subtract, ["pq"], ["pq"])
        TS("dve", fq[:, 0:128], nidx[:, 0:128], 1.0 / 16.0, -15.0 / 32.0, ALU.mult, ALU.add, ["nidx"], ["fq"])
        TSS("dve", fq[:, 0:128], fq[:, 0:128], MAGIC, ALU.add, ["fq"], ["fq"])
        TSS("dve", fq[:, 0:128], fq[:, 0:128], MAGIC, ALU.subtract, ["fq"], ["fq"])
        TS("dve", MF[:, 0:128], fq[:, 0:128], pq[:, 0:1], None, ALU.is_ge, None, ["fq", "pq"], ["MF"])
        TS("dve", MB[:, 0:128], fq[:, 0:128], pq[:, 0:1], None, ALU.is_le, None, ["fq", "pq"], ["MB"])
        stgT = stg[0].rearrange("p (g m) -> p g m", g=32)
        QK = [BG(0), BG(1), BG(2), BG(3)]
        for g in range(32):
            gsl = slice(g * 128, (g + 1) * 128)
            bf_, bb_ = (0, 1) if g % 2 == 0 else (6, 7)
            k_ = g % 2
            MM(pbf(bf_)[:, 0:128], bigs[2][H0, gsl], bigs[0][H0, gsl], True, False, QK, [("pb", bf_)])
            MM(pbf(bf_)[:, 0:128], bigs[3][H0, gsl], bigs[1][H0, gsl], False, True, QK, [("pb", bf_)])
            MM(pbf(bb_)[:, 0:128], bigs[2][H1, gsl], bigs[0][H1, gsl], True, False, QK, [("pb", bb_)])
            MM(pbf(bb_)[:, 0:128], bigs[3][H1, gsl], bigs[1][H1, gsl], False, True, QK, [("pb", bb_)])
            TT("dve", tp_f[k_][:, 0:128], pbf(bf_)[:, 0:128], MF[:, 0:128], ALU.mult, [("pb", bf_), "MF"], [("tpf", k_)])
            TT("dve", bigs[4][:, gsl], pbf(bb_)[:, 0:128], MB[:, 0:128], ALU.mult, [("pb", bb_), "MB"], [BG(4)])
            TT("dve", tp_f[k_][:, 0:128], tp_f[k_][:, 0:128], bigs[4][:, gsl], ALU.add, [("tpf", k_), BG(4)], [("tpf", k_)])
            STT(stgT[:, g, :], identf, drep[:, g:g + 1], tp_f[k_][:, 0:128], ALU.mult, ALU.add, ["identf", "drep", ("tpf", k_)], [("stg", 0)])
        DMA("st0", s5Wtoep, stg[0], [("stg", 0)], ["s5Wtoep"], eng="pool")
        S5W = ["s5Wfar0", "s5Wfar1", "s5Wsum0", "s5Wsum1", "s5Wtoep"]

        if debug == "setup":
            S.barrier()
            M.top = PBASE
            d1_ = M.alloc(8192)
            d2_ = M.alloc(16384)
            d3_ = M.alloc(16384)
            DMA("lb", d1_, s5Wtoep, S5W, ["d1"])
            DMA("lc", d2_, s5Wsum, S5W, ["d2"])
            DMA("ld", d3_, s5Wfar, S5W, ["d3"])
            DEBUG["Wtoep"] = (d1_, [128, 4096], BF16, ["d1"])
            DEBUG["Wsum"] = (d2_, [128, 8192], BF16, ["d2"])
            DEBUG["Wfar"] = (d3_, [128, 8192], BF16, ["d3"])
            DEBUG["MPr"] = (MPr[:, 1:17, :].rearrange("p q g -> p (q g)"), [128, 16 * 32], F32, MPT)
            DEBUG["MPi"] = (MPi[:, 1:17, :].rearrange("p q g -> p (q g)"), [128, 16 * 32], F32, MPT)

        nseq_run = 0 if debug == "setup" else (1 if debug else NSEQ)
        for s in range(nseq_run):
            S.barrier()
            M.top = PBASE
            hT = M.alloc(32768, BF16, "p (k t) -> p k t", k=8)
            V = M.alloc(16384, BF16, "p (i c) -> p i c", i=16)
            qT = M.alloc(4096)
            kT = M.alloc(4096)
            qf = M.alloc(4096)
            qb = M.alloc(4096)
            Kf = M.alloc(4096, BF16, "p (j d) -> p j d", j=16)
            Kb = M.alloc(4096, BF16, "p (j d) -> p j d", j=16)
            Rbf = M.alloc(8192, BF16, "p (a j e) -> p a j e", a=2, j=16)
            R32 = M.alloc(1024, F32, "p (a e) -> p a e", a=2)
            gs = M.alloc(4096)
            xt = [M.alloc(4096, F32) for _ in range(2)]
            hbs = [M.alloc(2048) for _ in range(2)]
            junk = M.alloc(2048)
            wv = M.alloc(8192, BF16, "p (k n) -> p k n", k=8)
            wq = [M.alloc(4096, BF16, "p (k n) -> p k n", k=8) for _ in range(2)]
            wg = M.alloc(2048, BF16, "p (k n) -> p k n", k=8)
            qs = [M.alloc(1024) for _ in range(2)]
            r1 = [M.alloc(2048, F32) for _ in range(2)]
            r2 = [M.alloc(2048, F32) for _ in range(2)]
            Sm = M.alloc(1024, BF16, "p (j n) -> p j n", j=4)
            sq = [M.alloc(1024) for _ in range(2)]
            sd = [M.alloc(2048, F32) for _ in range(2)]
            on = [M.alloc(2048, F32) for _ in range(2)]
            def wI_unit(off, ncols):
                return wI[:, off:off + 8 * ncols].rearrange("p (k c) -> p k c", k=8)

            pend_ev = None
            for i in range(16):
                b = i % 2
                DMA("xt%d" % b, xt[b], x_d[s, i * 128:(i + 1) * 128, :], (), [("xt", b)])
                ACT(junk, xt[b], AF.Square, [("xt", b)], ["junk", ("ss", i)], accum_out=stat[:, 0, i:i + 1])
                ACT(stat[:, 1, i:i + 1], stat[:, 0, i:i + 1], AF.Sqrt, [("ss", i), ("cst", CE)], [("sd", i)], scale=1.0 / DM, bias=cst[:, CE:CE + 1])
                RCP(stat[:, 2, i:i + 1], stat[:, 1, i:i + 1], [("sd", i)], [("rs", i)])
                hb = hbs[b]
                TS("dve", hb, xt[b], stat[:, 2, i:i + 1], None, ALU.mult, None, [("xt", b), ("rs", i)], [("hb", b)])
                pk = i % 2
                for kt in range(8):
                    TR(pbb(pk)[:, kt * 128:(kt + 1) * 128], hb[:, kt * 128:(kt + 1) * 128], [("hb", b)], [("pb", pk)])
                if pend_ev is not None:
                    ACT(hT[:, :, pend_ev[0] * 128:(pend_ev[0] + 1) * 128], pbb(pend_ev[1]).rearrange("p (k t) -> p k t", k=8), AF.Copy, [("pb", pend_ev[1])], [("hT", pend_ev[0])])
                pend_ev = (i, pk)
            ACT(hT[:, :, pend_ev[0] * 128:(pend_ev[0] + 1) * 128], pbb(pend_ev[1]).rearrange("p (k t) -> p k t", k=8), AF.Copy, [("pb", pend_ev[1])], [("hT", pend_ev[0])])
            hT_all = [("hT", i) for i in range(16)]

            DMA("wv", wv, wI_unit(20480, 512), wI_toks, ["wv"])
            for i in range(16):
                pk = 2 + (i % 2)
                for kt in range(8):
                    MM(pbf(pk), hT[:, kt, i * 128:(i + 1) * 128], wv[:, kt, :], kt == 0, kt == 7, [("hT", i), "wv"], [("pb", pk)])
                ACT(V[:, i, :], pbf(pk), AF.Copy, [("pb", pk)], [("V", i)])

            for h in range(4):
                hs = slice(h * 128, (h + 1) * 128)
                DMA("wq0", wq[0], wI_unit(h * 2048, 256), wI_toks, [("wq", 0, 0), ("wq", 0, 1)])
                DMA("wq1", wq[1], wI_unit((4 + h) * 2048, 256), wI_toks, [("wq", 1, 0), ("wq", 1, 1)])
                DMA("wg", wg, wI_unit(16384 + h * 1024, 128), wI_toks, ["wg"])
                ui = 0
                for which in range(2):
                    dst = qT if which == 0 else kT
                    dtok = "qT" if which == 0 else "kT"
                    for tb in range(4):
                        pa = (ui % 2) * 2
                        pbk = pa + 1
                        rr = ui % 2
                        ui += 1
                        tsl = slice(tb * 512, (tb + 1) * 512)
                        for kt in range(8):
                            MM(pbf(pa), wq[which][:, kt, 0:128], hT[:, kt, tsl], kt == 0, kt == 7, hT_all + [("wq", which, 0)], [("pb", pa)])
                        for kt in range(8):
                            MM(pbf(pbk), wq[which][:, kt, 128:256], hT[:, kt, tsl], kt == 0, kt == 7, hT_all + [("wq", which, 1)], [("pb", pbk)])
                        TT("dve", r1[rr][:, 0:512], pbf(pa), COS[:, tsl], ALU.mult, [("pb", pa), "COS"], [("r1", rr)])
                        TT("dve", r2[rr][:, 0:512], pbf(pbk), SINS[:, tsl], ALU.mult, [("pb", pbk), "SINS"], [("r2", rr)])
                        TT("pool", dst[:, tsl], r1[rr][:, 0:512], r2[rr][:, 0:512], ALU.add, [("r1", rr), ("r2", rr)], [(dtok, tb)])
                        if which == 0:
                            for (dd, XI, nm, xn) in ((qf, XIF, "qf", "XIF"), (qb, XIB, "qb", "XIB")):
                                TT("pool", dd[:, tsl].rearrange("p (j n) -> p j n", j=4), qT[:, tsl].rearrange("p (j n) -> p j n", j=4),
                                   XI[:, h, :].unsqueeze(1).to_broadcast([128, 4, 128]), ALU.mult, [("qT", tb), (xn, h)], [(nm, tb)])
                for tb in range(4):
                    pk = 4 + (tb % 2)
                    tsl = slice(tb * 512, (tb + 1) * 512)
                    for kt in range(8):
                        MM(pbf(pk), wg[:, kt, :], hT[:, kt, tsl], kt == 0, kt == 7, hT_all + ["wg"], [("pb", pk)])
                    ACT(gs[:, tsl], pbf(pk), AF.Silu, [("pb", pk)], [("gs", tb)])
                for g4 in range(4):
                    for jj in range(4):
                        j = g4 * 4 + jj
                        TR(pbb(6)[:, jj * 128:(jj + 1) * 128], kT[:, j * 128:(j + 1) * 128], [("kT", g4)], [("pb", 6)])
                    ACT(Kf[:, g4 * 4:(g4 + 1) * 4, :].rearrange("p j d -> p (j d)"), pbb(6)[:, 0:512], AF.Copy, [("pb", 6), ("ZF", h)], [("Kf", g4)], scale=ZF[:, h:h + 1])
                    ACT(Kb[:, g4 * 4:(g4 + 1) * 4, :].rearrange("p j d -> p (j d)"), pbb(6)[:, 0:512], AF.Copy, [("pb", 6), ("ZB", h)], [("Kb", g4)], scale=ZB[:, h:h + 1])
                ring = [6, 7, 2, 3]
                slot = 0
                for n_ in range(16):
                    for a in range(2):
                        j = n_ if a == 0 else 15 - n_
                        KK = Kf if a == 0 else Kb
                        ktok = "Kf" if a == 0 else "Kb"
                        bk = ring[slot % 4]
                        slot += 1
                        MM(pbf(bk)[:, 0:128], KK[:, j, :], V[:, j, hs], True, True, [(ktok, j // 4), ("V", j)], [("pb", bk)])
                        if n_ == 0:
                            CP("dve", R32[:, a, :], pbf(bk)[:, 0:128], [("pb", bk)], [("R32", a)])
                        else:
                            STT(R32[:, a, :], R32[:, a, :], G128[h], pbf(bk)[:, 0:128], ALU.mult, ALU.add, [("pb", bk), ("R32", a)], [("R32", a)])
                        if a == 0:
                            ACT(Rbf[:, a, j, :], R32[:, a, :], AF.Copy, [("R32", a)], [("Rbf", a, j)])
                        else:
                            CP("pool", Rbf[:, a, j, :], R32[:, a, :], [("R32", a)], [("Rbf", a, j)])

                def scores(j):
                    bk = 6 + (j % 2)
                    jsl = slice(j * 128, (j + 1) * 128)
                    MM(pbf(bk)[:, 0:128], kT[:, jsl], qT[:, jsl], True, True, [("kT", j // 4), ("qT", j // 4)], [("pb", bk)])

                def norm_tail(b4):
                    po = 4 + (b4 % 2)
                    pm = b4 % 2
                    k_ = b4 % 2
                    MM(pbf(pm), onesm, sq[k_][:, 0:512], True, True, [("sq", k_), "onesm"], [("pb", pm)])
                    ACT(sd[k_][:, 0:512], pbf(pm), AF.Ln, [("pb", pm), ("cst", CE)], [("sd", k_)], bias=cst[:, CE:CE + 1])
                    ACT(sd[k_][:, 0:512], sd[k_][:, 0:512], AF.Exp, [("sd", k_)], [("sd", k_)], scale=-0.5)
                    STT(on[k_][:, 0:512], pbf(po), gain[:, h:h + 1], sd[k_][:, 0:512], ALU.mult, ALU.mult, [("pb", po), ("sd", k_), "gain"], [("on", k_)])
                    TT("pool", mixT[:, h, b4 * 512:(b4 + 1) * 512], on[k_][:, 0:512], gs[:, b4 * 512:(b4 + 1) * 512], ALU.mult, [("on", k_), ("gs", b4)], [("mixT", h, b4)])

                scores(0)
                pend = None
                for j in range(16):
                    b4, jj = divmod(j, 4)
                    po = 4 + (b4 % 2)
                    sl = j % 4
                    bk = 6 + (j % 2)
                    jsl = slice(j * 128, (j + 1) * 128)
                    osl = slice(jj * 128, (jj + 1) * 128)
                    if j < 15:
                        scores(j + 1)
                    TT("dve", Sm[:, sl, :], pbf(bk)[:, 0:128], Dm[:, h, :], ALU.mult, [("pb", bk), ("Dm", h)], [("Sm", sl)])
                    nmm = 1 + (1 if j > 0 else 0) + (1 if j < 15 else 0)
                    MM(pbf(po)[:, osl], V[:, j, hs], Sm[:, sl, :], True, nmm == 1, [("V", j), ("Sm", sl)], [("pb", po)])
                    if j > 0:
                        MM(pbf(po)[:, osl], Rbf[:, 0, j - 1, :], qf[:, jsl], False, j == 15, [("Rbf", 0, j - 1), ("qf", b4)], [("pb", po)])
                    if j < 15:
                        MM(pbf(po)[:, osl], Rbf[:, 1, j + 1, :], qb[:, jsl], False, True, [("Rbf", 1, j + 1), ("qb", b4)], [("pb", po)])
                    if pend is not None and j == pend * 4 + 5:
                        norm_tail(pend)
                        pend = None
                    if jj == 3:
                        ACT(sq[b4 % 2][:, 0:512], pbf(po), AF.Square, [("pb", po)], [("sq", b4 % 2)])
                        if pend is not None:
                            norm_tail(pend)
                        pend = b4
                norm_tail(pend)
                if debug == "p1" and h == 0:
                    DEBUG["qT"] = (qT, [128, 2048], BF16, [("qT", t_) for t_ in range(4)])
                    DEBUG["kT"] = (kT, [128, 2048], BF16, [("kT", t_) for t_ in range(4)])
                    DEBUG["Kf"] = (Kf.rearrange("p j d -> p (j d)"), [128, 2048], BF16, [("Kf", t_) for t_ in range(4)])
                    DEBUG["Rbf"] = (Rbf.rearrange("p a j e -> p (a j e)"), [128, 4096], BF16, [("Rbf", a_, j_) for a_ in range(2) for j_ in range(16)])
                    DEBUG["gs"] = (gs, [128, 2048], BF16, [("gs", t_) for t_ in range(4)])

            if debug == "p1":
                DEBUG["ret"] = (mixT[:, 0:4, :].rearrange("p k t -> p (k t)"), [128, 4 * 2048], BF16, [("mixT", h, b) for h in range(4) for b in range(4)])
                DEBUG["hT"] = (hT.rearrange("p k t -> p (k t)"), [128, 8 * 2048], BF16, hT_all)
                DEBUG["V"] = (V.rearrange("p i c -> p (i c)"), [128, 16 * 512], BF16, [("V", i) for i in range(16)])
                break

            S.barrier()
            M.top = PBASE + 32768
            Sx = big[:, PBASE // 2:(PBASE + 32768) // 2].bitcast(F32).rearrange("p (r g j) -> p r g j", r=2, g=16)
            Ub = M.alloc(16384, BF16, "p (g j) -> p g j", g=32)
            WT = M.alloc(8192, BF16, "p (g m) -> p g m", g=32)
            WSm = M.alloc(16384, BF16, "p (r g m) -> p r g m", r=2, g=32)
            WFr = M.alloc(16384, BF16, "p (r g m) -> p r g m", r=2, g=32)
            yg = M.alloc(16384, BF16, "p (k t) -> p k t", k=4)
            YJ = [M.alloc(4096, BF16, "p (t c) -> p t c", t=8) for _ in range(2)]
            R1 = M.top
            wu = M.alloc(8192, BF16, "p (k n) -> p k n", k=8)
            UJ = M.alloc(8192, BF16, "p (g m) -> p g m", g=32)
            M.top = R1
            Fs = M.alloc(16384, BF16, "p (r g j) -> p r g j", r=2, g=16)
            sctA0, sctA1, sctB0, sctB1 = (M.alloc(2048, F32, "p (g j) -> p g j", g=32) for _ in range(4))
            gt = [sctA0.rearrange("p g j -> p (g j)"), sctB0.rearrange("p g j -> p (g j)")]

            DMA("wu", wu, wI_unit(24576, 512), wI_toks, ["wu"])
            DMA("lws", WSm.rearrange("p r g m -> p (r g m)"), s5Wsum, S5W, ["WSm"])
            DMA("lwt", WT.rearrange("p g m -> p (g m)"), s5Wtoep, S5W, ["WT"])
            DMA("lwf", WFr.rearrange("p r g m -> p (r g m)"), s5Wfar, S5W, ["WFr"])
            ev = 0
            for jb in range(2):
                for t in range(8):
                    pk = t % 2
                    base = jb * 1024 + t
                    for kt in range(8):
                        MM(pbf(pk), hT[:, kt, base:(jb + 1) * 1024:8], wu[:, kt, :], kt == 0, kt == 7, hT_all + ["wu"], [("pb", pk)])
                    if ev % 2 == 0:
                        ACT(UJ[:, :, t * 16:(t + 1) * 16], pbf(pk).rearrange("p (g c) -> p g c", g=32), AF.Copy, [("pb", pk)], ["UJ"])
                    else:
                        CP("dve", UJ[:, :, t * 16:(t + 1) * 16], pbf(pk).rearrange("p (g c) -> p g c", g=32), [("pb", pk)], ["UJ"])
                    ev += 1
                for g8 in range(4):
                    pk = 2 + g8 % 2
                    for gg in range(8):
                        TR(pbb(pk)[:, gg * 128:(gg + 1) * 128], UJ[:, g8 * 8 + gg, :], ["UJ"], [("pb", pk)])
                    if g8 % 2 == 0:
                        ACT(Ub[:, g8 * 8:(g8 + 1) * 8, jb * 128:(jb + 1) * 128], pbb(pk).rearrange("p (g j) -> p g j", g=8), AF.Copy, [("pb", pk)], [("Ub", g8)])
                    else:
                        CP("dve", Ub[:, g8 * 8:(g8 + 1) * 8, jb * 128:(jb + 1) * 128], pbb(pk).rearrange("p (g j) -> p g j", g=8), [("pb", pk)], [("Ub", g8)])

            H0, H1 = slice(0, 64), slice(64, 128)
            SXT = [("Sx", "dve"), ("Sx", "pool")]
            tmpA = [x.rearrange("p (r g) j -> p r g j", r=2) for x in (sctA0, sctA1)]
            tmpB = [x.rearrange("p (r g) j -> p r g j", r=2) for x in (sctB0, sctB1)]

            LANES = (("dve", 0, 10), ("pool", 10, 16))

            def upd(jd, js, p_, gh, n, ts):
                for (eng, g0, g1) in LANES:
                    ng = g1 - g0
                    gsl = slice(gh * 16 + g0, gh * 16 + g1)
                    lsl = slice(g0, g1)
                    dst = Sx[:, :, lsl, jd]
                    srcv = Sx[:, :, lsl, js]
                    swp = Sx[:, ::-1, lsl, js]
                    if n is None:
                        Ka = MPr[:, p_, gsl].unsqueeze(1).to_broadcast([128, 2, ng])
                        Kb = MPB[:, p_, :, gsl]
                        t0 = tmpA[ts][:, :, lsl, 0]
                        t1 = tmpB[ts][:, :, lsl, 0]
                    else:
                        Ka = MPr[:, p_, gsl].unsqueeze(1).unsqueeze(3).to_broadcast([128, 2, ng, n])
                        Kb = MPB[:, p_, :, gsl].unsqueeze(3).to_broadcast([128, 2, ng, n])
                        t0 = tmpA[ts][:, :, lsl, 0:n]
                        t1 = tmpB[ts][:, :, lsl, 0:n]
                    sx = [("Sx", eng)]
                    TT(eng, t0, srcv, Ka, ALU.mult, sx + MPT, [("tA", ts, eng)])
                    TT(eng, t1, swp, Kb, ALU.mult, sx + MPT, [("tB", ts, eng)])
                    TT(eng, dst, dst, t0, ALU.add, [("tA", ts, eng)], sx)
                    TT(eng, dst, dst, t1, ALU.add, [("tB", ts, eng)], sx)

            def s5_scan(gh):
                for gl in range(16):
                    g = gh * 16 + gl
                    pk = 4 + gl % 2
                    MM(pbf(pk)[:, 0:256], WSm[:, 0, g, :], Ub[:, g, :], True, True, [("Ub", g // 8), "WSm"], [("pb", pk)])
                    MM(pbf(pk)[:, 256:512], WSm[:, 1, g, :], Ub[:, g, :], True, True, [("Ub", g // 8), "WSm"], [("pb", pk)])
                    ACT(Sx[H0, :, gl, :], pbf(pk)[H0, :].rearrange("p (r j) -> p r j", r=2), AF.Copy, [("pb", pk)], SXT + hT_all)
                    CP("dve", Sx[H1, :, gl, ::-1], pbf(pk)[H1, :].rearrange("p (r j) -> p r j", r=2), [("pb", pk)], SXT + hT_all)
                for j1 in range(1, 16):
                    upd(slice(j1, 256, 16), slice(j1 - 1, 256, 16), 1, gh, 16, j1 % 2)
                for k_, p_ in enumerate((16, 17, 18, 19)):
                    sft = 1 << k_
                    upd(slice(16 * sft + 15, 256, 16), slice(15, 256 - 16 * sft, 16), p_, gh, 16 - sft, k_ % 2)
                for j1 in range(15):
                    upd(slice(16 + j1, 256, 16), slice(15, 240, 16), j1 + 1, gh, 15, j1 % 2)

            def s5_fs(gh):
                FST = ["Fs", "wu", "UJ"]
                S.op("pool", "memset", dict(ap=Fs[H0, :, :, 0:1], constant=0.0), (), FST)
                S.op("pool", "memset", dict(ap=Fs[H1, :, :, 255:256], constant=0.0), (), FST)
                ACT(Fs[H0, :, :, 1:256], Sx[H0, :, :, 0:255], AF.Copy, SXT, FST)
                CP("dve", Fs[H1, :, :, 0:255], Sx[H1, :, :, 254::-1], SXT, FST)

            def s5_out(gh):
                for jb in range(2):
                    jsl = slice(jb * 128, (jb + 1) * 128)
                    k_ = jb
                    for gl4 in range(4):
                        pk = 6 + gl4 % 2
                        for gg in range(4):
                            gl = gl4 * 4 + gg
                            g = gh * 16 + gl
                            osl = slice(gg * 128, (gg + 1) * 128)
                            MM(pbf(pk)[:, osl], Ub[:, g, jsl], WT[:, g, :], True, False, [("Ub", g // 8), "WT"], [("pb", pk)])
                            MM(pbf(pk)[:, osl], Fs[:, 0, gl, jsl], WFr[:, 0, g, :], False, False, ["Fs", "WFr"], [("pb", pk)])
                            MM(pbf(pk)[:, osl], Fs[:, 1, gl, jsl], WFr[:, 1, g, :], False, True, ["Fs", "WFr"], [("pb", pk)])
                        if gl4 % 2 == 0:
                            ACT(YJ[k_][:, :, gl4 * 64:(gl4 + 1) * 64].rearrange("p t (g c) -> p g t c", g=4),
                                pbf(pk).rearrange("p (g t c) -> p g t c", g=4, t=8), AF.Copy, [("pb", pk)], [("YJ", k_)])
                        else:
                            CP("dve", YJ[k_][:, :, gl4 * 64:(gl4 + 1) * 64].rearrange("p t (g c) -> p g t c", g=4),
                               pbf(pk).rearrange("p (g t c) -> p g t c", g=4, t=8), [("pb", pk)], [("YJ", k_)])
                    for cth in range(2):
                        ct = gh * 2 + cth
                        pk = 2 + cth
                        for t in range(8):
                            TR(pbb(pk)[:, t * 128:(t + 1) * 128], YJ[k_][:, t, cth * 128:(cth + 1) * 128], [("YJ", k_)], [("pb", pk)])
                        ACT(yg[:, ct, jb * 1024:(jb + 1) * 1024], pbb(pk), AF.Gelu_apprx_tanh, [("pb", pk)], [("yg", ct)])

            s5_scan(0)
            s5_fs(0)
            s5_scan(1)
            s5_out(0)
            s5_fs(1)
            s5_out(1)
            for ct in range(4):
                for tb in range(4):
                    pk = [0, 1, 4, 5][(ct * 4 + tb) % 4]
                    gi = (ct * 4 + tb) % 2
                    tsl = slice(tb * 512, (tb + 1) * 512)
                    for kt in range(4):
                        MM(pbf(pk), gluw[:, kt, ct * 128:(ct + 1) * 128], yg[:, kt, tsl], kt == 0, kt == 3, [("yg", k_) for k_ in range(4)] + ["gluw"], [("pb", pk)])
                    ACT(gt[gi][:, 0:512], pbf(pk), AF.Sigmoid, [("pb", pk), "glub"], [("gt", gi), ("tA", 0, "dve"), ("tB", 0, "dve"), ("tA", 0, "pool"), ("tB", 0, "pool")], bias=glub[:, ct:ct + 1])
                    jb_, t0_ = tb // 2, 4 * (tb % 2)
                    TT("pool" if tb % 2 == 0 else "dve", mixT[:, 4 + ct, jb_ * 1024:(jb_ + 1) * 1024].rearrange("p (j t) -> p t j", t=8)[:, t0_:t0_ + 4, :],
                       gt[gi][:, 0:512].rearrange("p (t j) -> p t j", t=4), yg[:, ct, tsl].rearrange("p (t j) -> p t j", t=4), ALU.mult,
                       [("gt", gi), ("yg", ct)], [("mixT", 4 + ct, jb_ * 2), ("mixT", 4 + ct, jb_ * 2 + 1)])
            if debug == "p2":
                DEBUG["yg"] = (yg.rearrange("p k t -> p (k t)"), [128, 4 * 2048], BF16, [("yg", c_) for c_ in range(4)])
                DEBUG["mix"] = (mixT.rearrange("p k t -> p (k t)"), [128, 8 * 2048], BF16, [("mixT", k_, b_) for k_ in range(8) for b_ in range(4)])
                DEBUG["Ub"] = (Ub.rearrange("p g j -> p (g j)"), [128, 32 * 256], BF16, [("Ub", g_) for g_ in range(4)])
                DEBUG["Sx"] = (Sx.rearrange("p r g j -> p (r g j)"), [128, 2 * 16 * 256], F32, SXT)
                break

            S.barrier()
            M.top = PBASE
            h2T = M.alloc(32768, BF16, "p (k t) -> p k t", k=8)
            x1 = M.alloc(65536, F32, "p (i c) -> p i c", i=16)
            P3 = M.top
            xt = [M.alloc(4096, F32) for _ in range(2)]
            hbs = [M.alloc(2048) for _ in range(2)]
            junk = M.alloc(2048)
            wo = M.alloc(16384, BF16, "p (k n) -> p k n", k=8)
            DMA("wo", wo, wO.rearrange("(k p) c -> p k c", p=128), wO_toks, ["wo"])
            mix_all = [("mixT", k_, b_) for k_ in range(8) for b_ in range(4)]
            pend_ev = None
            pend_tr = None
            ev_box = [None]

            def tr3a(i_, b_):
                pk_ = 4 + (i_ % 2)
                for kt in range(8):
                    TR(pbb(pk_)[:, kt * 128:(kt + 1) * 128], hbs[b_][:, kt * 128:(kt + 1) * 128], [("hb", b_)], [("pb", pk_)])
                if ev_box[0] is not None:
                    pi_, pp_ = ev_box[0]
                    ACT(h2T[:, :, pi_ * 128:(pi_ + 1) * 128], pbb(pp_).rearrange("p (k t) -> p k t", k=8), AF.Copy, [("pb", pp_)], [("h2T", pi_)])
                ev_box[0] = (i_, pk_)

            for i in range(16):
                b = i % 2
                DMA("xt%d" % b, xt[b], x_d[s, i * 128:(i + 1) * 128, :], (), [("xt", b)])
                for hf in range(2):
                    pk = (i % 2) * 2 + hf
                    hsl = slice(hf * 512, (hf + 1) * 512)
                    for kt in range(8):
                        MM(pbf(pk), mixT[:, kt, i * 128:(i + 1) * 128], wo[:, kt, hsl], kt == 0, kt == 7, mix_all + ["wo"], [("pb", pk)])
                    TT("dve", x1[:, i, hsl], pbf(pk), xt[b][:, hsl], ALU.add, [("pb", pk), ("xt", b)], [("x1", i, hf)])
                ACT(junk, x1[:, i, :], AF.Square, [("x1", i, 0), ("x1", i, 1)], ["junk", ("ss2", i)], accum_out=stat[:, 3, i:i + 1])
                ACT(stat[:, 4, i:i + 1], stat[:, 3, i:i + 1], AF.Sqrt, [("ss2", i), ("cst", CE)], [("sd2", i)], scale=1.0 / DM, bias=cst[:, CE:CE + 1])
                RCP(stat[:, 5, i:i + 1], stat[:, 4, i:i + 1], [("sd2", i)], [("rs2", i)])
                hb = hbs[b]
                TS("dve", hb, x1[:, i, :], stat[:, 5, i:i + 1], None, ALU.mult, None, [("x1", i, 0), ("x1", i, 1), ("rs2", i)], [("hb", b)])
                if pend_tr is not None:
                    tr3a(*pend_tr)
                pend_tr = (i, b)
            tr3a(*pend_tr)
            ACT(h2T[:, :, ev_box[0][0] * 128:(ev_box[0][0] + 1) * 128], pbb(ev_box[0][1]).rearrange("p (k t) -> p k t", k=8), AF.Copy, [("pb", ev_box[0][1])], [("h2T", ev_box[0][0])])
            h2T_all = [("h2T", i) for i in range(16)]
            if debug == "p3a":
                DEBUG["x1"] = (x1.rearrange("p i c -> p (i c)"), [128, 16 * 1024], F32, [("x1", i, hf) for i in range(16) for hf in range(2)])
                break

            S.barrier()
            M.top = P3
            aT = mixT_flat[:, 0:NCT * 512].rearrange("p (c t) -> p c t", c=NCT)
            accf = mixT_flat[:, NCT * 512:NCT * 512 + 4096].bitcast(F32)
            acc = [[accf[:, (sl * 2 + vg) * 512:(sl * 2 + vg + 1) * 512] for vg in range(2)] for sl in range(2)]
            NWU = 3
            wup = [M.alloc(4096, BF16, "p (k n) -> p k n", k=8) for _ in range(NWU)]
            wdn = M.alloc(NCT * 1024, BF16, "p (c n) -> p c n", c=NCT)
            dcnt = 0
            glb = [M.alloc(2048, F32) for _ in range(2)]
            yo = [M.alloc(4096, F32)]
            yo.append(yo[0])
            junk = glb[1].bitcast(BF16)
            oi = 0
            wuc = 0
            def ffn_tail(sl, c):
                ACT(glb[sl][:, 0:512], acc[sl][1], AF.Gelu_apprx_tanh, [("acc", sl, 1)], [("glb", sl)])
                TT("pool", aT[:, c, :], glb[sl][:, 0:512], acc[sl][0], ALU.mult, [("glb", sl), ("acc", sl, 0)], [("aT", c)])

            ffn_pend = None
            for q4 in range(4):
                t0 = q4 * 512
                tsl = slice(t0, t0 + 512)
                for c in range(NCT):
                    sl = c % 2
                    ws = wuc % NWU
                    wuc += 1
                    DMA("wup%d" % ws, wup[ws], wU[:, c * 2048:(c + 1) * 2048].rearrange("p (k n) -> p k n", k=8), wU_toks, [("wup", ws)])
                    hb_ = 6 + sl
                    for vg in range(2):
                        pk = sl * 2 + vg
                        wsl = slice(vg * 128, (vg + 1) * 128)
                        for kt in range(8):
                            MM(pbf(pk), wup[ws][:, kt, wsl], h2T[:, kt, tsl], kt == 0, kt == 7, h2T_all + [("wup", ws)], [("pb", pk)])
                    for vg in range(2):
                        wsl = slice(vg * 128, (vg + 1) * 128)
                        hoff = vg * 2
                        if 0 < q4 < 3:
                            for kt in range(8):
                                MM(pbf(hb_)[:, hoff:hoff + 2], wup[ws][:, kt, wsl], h2T[:, kt, t0 - 1:t0 + 513:513], kt == 0, kt == 7, h2T_all + [("wup", ws)], [("pb", hb_)])
                        elif q4 > 0:
                            for kt in range(8):
                                MM(pbf(hb_)[:, hoff:hoff + 1], wup[ws][:, kt, wsl], h2T[:, kt, t0 - 1:t0], kt == 0, kt == 7, h2T_all + [("wup", ws)], [("pb", hb_)])
                        else:
                            for kt in range(8):
                                MM(pbf(hb_)[:, hoff + 1:hoff + 2], wup[ws][:, kt, wsl], h2T[:, kt, t0 + 512:t0 + 513], kt == 0, kt == 7, h2T_all + [("wup", ws)], [("pb", hb_)])
                    for vg in range(2):
                        pk = sl * 2 + vg
                        hoff = vg * 2
                        ch = vg * NCT + c
                        A_ = acc[sl][vg]
                        atok = ("acc", sl, vg)
                        ACT(A_, pbf(pk), AF.Identity, [("pb", pk), "cw", "cb"], [atok], scale=cw[:, ch, 1:2], bias=cb[:, ch:ch + 1])
                        if q4 > 0:
                            ACT(A_[:, 0:1], pbf(hb_)[:, hoff:hoff + 1], AF.Identity, [("pb", hb_), atok, "cw"], [atok], scale=cw[:, ch, 0:1], bias=A_[:, 0:1])
                        if q4 < 3:
                            ACT(A_[:, 511:512], pbf(hb_)[:, hoff + 1:hoff + 2], AF.Identity, [("pb", hb_), atok, "cw"], [atok], scale=cw[:, ch, 2:3], bias=A_[:, 511:512])
                    for vg in range(2):
                        pk = sl * 2 + vg
                        ch = vg * NCT + c
                        A_ = acc[sl][vg]
                        atok = ("acc", sl, vg)
                        STT(A_[:, 1:512], pbf(pk)[:, 0:511], cw[:, ch, 0:1], A_[:, 1:512], ALU.mult, ALU.add, [("pb", pk), atok, "cw"], [atok])
                        STT(A_[:, 0:511], pbf(pk)[:, 1:512], cw[:, ch, 2:3], A_[:, 0:511], ALU.mult, ALU.add, [("pb", pk), atok, "cw"], [atok])
                    if ffn_pend is not None:
                        ffn_tail(*ffn_pend)
                    ffn_pend = (sl, c)
                ffn_tail(*ffn_pend)
                ffn_pend = None
                for hf in range(2):
                    hsl = slice(hf * 512, (hf + 1) * 512)
                    for c in range(NCT):
                        DMA("wdn%d" % c, wdn[:, c, :], wD[c * 128:(c + 1) * 128, hsl], wD_toks, [("wdn", c)])
                    for tt in range(4):
                        i = q4 * 4 + tt
                        pk = 4 + dcnt % 2
                        dcnt += 1
                        for c in range(NCT):
                            MM(pbf(pk), aT[:, c, tt * 128:(tt + 1) * 128], wdn[:, c, :], c == 0, c == NCT - 1, [("aT", c), ("wdn", c)], [("pb", pk)])
                        TT("dve", x1[:, i, hsl], pbf(pk), x1[:, i, hsl], ALU.add, [("pb", pk), ("x1", i, hf)], [("x1", i, hf)])
                        if hf == 1:
                            o_ = oi % 2
                            oi += 1
                            ACT(junk, x1[:, i, :], AF.Square, [("x1", i, 0), ("x1", i, 1)], ["junk", "ss3", ("glb", 1)], accum_out=stat3[:, 0:1])
                            ACT(stat3[:, 1:2], stat3[:, 0:1], AF.Sqrt, ["ss3", ("cst", CE)], ["sd3"], scale=1.0 / DM, bias=cst[:, CE:CE + 1])
                            RCP(stat3[:, 2:3], stat3[:, 1:2], ["sd3"], ["rs3"])
                            STT(yo[o_], x1[:, i, :], stat3[:, 2:3], nfbc, ALU.mult, ALU.mult, [("x1", i, 0), ("x1", i, 1), "rs3", "nfbc"], [("yo", 0)])
                            DMA("yo0", y_d[s, i * 128:(i + 1) * 128, :], yo[o_], [("yo", 0)], [("y", s, i)])

        fin = []
        if debug == "p3b":
            fin = [("y", 0, i) for i in range(16)]
        elif debug:
            for nm, (ap, shape, dt, toks) in DEBUG.items():
                dd = nc.dram_tensor("dbg_" + nm, list(shape), dt, kind="ExternalOutput").ap()
                dbg_out[nm] = (shape, dt)
                DMA("dbg_" + nm, dd, ap, toks, [("dbg", nm)])
                fin.append(("dbg", nm))
        else:
            fin = [("y", s, i) for s in range(NSEQ) for i in range(16)]
        NOP("sp", fin)
        S.emit()
        print("SBUF high water", M.hw, "instr counts", {e: len(v) for e, v in S.streams.items()})
    return nc, dbg_out


def _prep_weights(inp):
    f = np.float32
    w_in = np.asarray(inp["w_in"], f)[0]
    q, k, v, g, u = (w_in[:, i * 512:(i + 1) * 512] for i in range(5))

    def swap(w):
        w4 = w.reshape(DM, 4, 2, 64)
        return np.ascontiguousarray(w4[:, :, ::-1, :]).reshape(DM, 512)

    def inter(w):
        a = w.reshape(DM, 4, 1, 128)
        b = swap(w).reshape(DM, 4, 1, 128)
        return np.concatenate([a, b], axis=2).reshape(DM, 1024)

    w_in_a = np.ascontiguousarray(np.concatenate([inter(q), inter(k), g, v, u], axis=1))

    def col(vec, n):
        return np.ascontiguousarray(np.asarray(vec, f).reshape(n, 128).T)

    lre = np.asarray(inp["s5_lambda_re"], f)[0]
    lim = np.asarray(inp["s5_lambda_im"], f)[0]
    ldt = np.asarray(inp["s5_log_dt"], f)[0]

    def st2(a):
        return np.ascontiguousarray(a.transpose(0, 2, 1).reshape(128, 32))

    ldt_e = np.ascontiguousarray(np.broadcast_to(ldt[:, :, None], (2, 32, 64)))
    Bre = np.asarray(inp["s5_B_re"], f)[0]
    Bim = np.asarray(inp["s5_B_im"], f)[0]
    Cre = np.asarray(inp["s5_C_re"], f)[0]
    Cim = np.asarray(inp["s5_C_im"], f)[0]

    def b2(B):
        return np.ascontiguousarray(B.transpose(0, 2, 1, 3).reshape(128, 512))

    def c2(C):
        return np.ascontiguousarray(C.transpose(0, 3, 1, 2).reshape(128, 512))

    dsk = np.asarray(inp["s5_D"], f)[0]
    drep = np.ascontiguousarray(np.tile(dsk.reshape(32, 16).T, (8, 1)))

    conv_w = np.asarray(inp["conv_w"], f)[0]
    cwc = np.ascontiguousarray(conv_w.reshape(3, 44, 128).transpose(2, 1, 0).reshape(128, 132))
    d = {
        "w_in_a": w_in_a,
        "w_out": np.ascontiguousarray(np.asarray(inp["w_out"], f)[0]),
        "w_up": np.ascontiguousarray(np.asarray(inp["w_up"], f)[0]),
        "w_down": np.ascontiguousarray(np.asarray(inp["w_down"], f)[0]),
        "glu_w": np.ascontiguousarray(np.asarray(inp["s5_glu_w"], f)[0]),
        "nmix_c": col(np.asarray(inp["norm_mix"])[0], 8),
        "nffn_c": col(np.asarray(inp["norm_ffn"])[0], 8),
        "nfin_r": np.ascontiguousarray(np.asarray(inp["norm_final"], f).reshape(1, DM)),
        "gain_c": col(np.asarray(inp["ret_gn_gain"])[0], 4),
        "glub_c": col(np.asarray(inp["s5_glu_b"])[0], 4),
        "cw_c": cwc,
        "cb_c": col(np.asarray(inp["conv_b"])[0], 44),
        "lre2": st2(lre), "lim2": st2(lim), "ldt2": st2(ldt_e),
        "bre2": b2(Bre), "bim2": b2(Bim), "cre2": c2(Cre), "cim2": c2(Cim),
        "drep": drep,
    }
    return d


_CACHE = {}


def kernel(**inputs):
    xp = np.asarray(inputs["x_prompt"], np.float32)
    xs = np.asarray(inputs["x_sample"], np.float32)
    xall = np.concatenate([xp, xs], axis=0)
    wd = _prep_weights(inputs)
    if "nc" not in _CACHE:
        _CACHE["nc"] = build_program(False)[0]
    nc = _CACHE["nc"]
    in_maps = []
    for c in range(NCORES):
        m = dict(wd)
        m["x"] = np.ascontiguousarray(xall[c * NSEQ:(c + 1) * NSEQ])
        in_maps.append(m)
    res = run_bass_kernel_spmd(nc, in_maps, core_ids=list(range(NCORES)))
    yall = np.concatenate([np.asarray(r["y"], np.float32) for r in res.results], axis=0)
    return (np.ascontiguousarray(yall[0:8]), np.ascontiguousarray(yall[8:24]))
```

```python
import math
import contextlib
import numpy as np
import concourse.bass as bass
import concourse.mybir as mybir
from concourse.alu_op_type import AluOpType as ALU
from concourse.bass_utils import run_bass_kernel_spmd

F32 = mybir.dt.float32
BF16 = mybir.dt.bfloat16
AF = mybir.ActivationFunctionType

EPS = 1e-6
L = 2048
DM = 1024
NSEQ = 3
NCORES = 8
DFF = 2816
NCT = 22
LN_G = [math.log(1.0 - 2.0 ** (-5.0 - h)) for h in range(4)]
G128 = [math.exp(128.0 * lg) for lg in LN_G]
SC = 128.0 ** -0.5
TWO_PI = 2.0 * math.pi
MAGIC = 12582912.0
SIN_SCALE = 6.283185
WIN_COLS = 3584


class Sched:
    ENG = ("pe", "act", "dve", "pool", "sp")

    def __init__(self, nc):
        self.nc = nc
        self.streams = {e: [] for e in self.ENG}
        self.cnt = {}
        self.lastw = {}
        self.readers = {}
        self.waited = {e: {} for e in self.ENG}
        self.last_marker = {}
        self.pending = {e: [] for e in self.ENG}
        self.LIM = 32000

    def barrier(self):
        ms = list(self.last_marker.values())
        for e in self.ENG:
            self.pending[e] = list(ms)

    def op(self, eng, meth, kwargs, reads=(), writes=(), chan=None):
        base = chan if chan is not None else eng
        step = 16 if chan is not None else 1
        c = self.cnt.get(base, 0)
        epoch = c // self.LIM
        newc = c + step
        if newc > (epoch + 1) * self.LIM:
            epoch += 1
            c = epoch * self.LIM
            newc = c + step
        self.cnt[base] = newc
        sk = (base, epoch)
        marker = (sk, newc - epoch * self.LIM, eng if chan is None else None)
        deps = list(self.pending[eng])
        self.pending[eng] = []
        for t in reads:
            if t in self.lastw:
                deps.append(self.lastw[t])
        for t in writes:
            if t in self.lastw:
                deps.append(self.lastw[t])
            deps.extend(self.readers.get(t, ()))
        wd = self.waited[eng]
        m = {}
        for (k, v, de) in deps:
            if de == "pe" and eng == "pe" and chan is None:
                continue
            if wd.get(k, 0) >= v:
                continue
            m[k] = max(m.get(k, 0), v)
        for k, v in m.items():
            wd[k] = v
        self.streams[eng].append((meth, kwargs, list(m.items()), sk, step))
        for t in writes:
            self.lastw[t] = marker
            self.readers[t] = []
        for t in reads:
            self.readers.setdefault(t, []).append(marker)
        self.last_marker[base] = marker
        return marker

    def emit(self):
        nc = self.nc
        semkeys = set()
        for e in self.ENG:
            for (meth, kwargs, waits, sk, step) in self.streams[e]:
                semkeys.add(sk)
                for (k, v) in waits:
                    semkeys.add(k)
        semkeys = sorted(semkeys, key=str)
        with contextlib.ExitStack() as st:
            sems = {}
            for i, k in enumerate(semkeys):
                sems[k] = st.enter_context(nc.semaphore("s%d" % i))
            block = st.enter_context(nc.Block())
            engmap = {"pe": "tensor", "act": "scalar", "dve": "vector", "pool": "gpsimd", "sp": "sync"}

            def mk(e):
                def body(eng):
                    for (meth, kwargs, waits, sk, step) in self.streams[e]:
                        for (k, v) in waits:
                            eng.wait_ge(sems[k], v)
                        inst = getattr(eng, meth)(**kwargs)
                        inst.then_inc(sems[sk], step)
                return body

            for e in self.ENG:
                getattr(block, engmap[e])(mk(e))


class Mem:
    def __init__(self, big, cap):
        self.big = big
        self.cap = cap
        self.top = 0
        self.hw = 0

    def alloc(self, nbytes, dt=BF16, pattern=None, **kw):
        nbytes = (nbytes + 63) // 64 * 64
        off = self.top
        self.top += nbytes
        self.hw = max(self.hw, self.top)
        assert self.top <= self.cap, ("SBUF overflow", self.top, self.cap)
        ap = self.big[:, off // 2:(off + nbytes) // 2]
        if dt is F32:
            ap = ap.bitcast(F32)
        if pattern is not None:
            ap = ap.rearrange(pattern, **kw)
        return ap


def build_program(debug=None):
    nc = bass.Bass("TRN2", target_bir_lowering=False)
    DEBUG = {}

    def din(name, shape, dt=F32):
        return nc.dram_tensor(name, list(shape), dt, kind="ExternalInput").ap()

    x_d = din("x", [NSEQ, L, DM])
    y_d = nc.dram_tensor("y", [NSEQ, L, DM], F32, kind="ExternalOutput").ap()
    w_in_d = din("w_in_a", [DM, WIN_COLS])
    w_out_d = din("w_out", [DM, DM])
    w_up_d = din("w_up", [DM, 2 * DFF])
    w_down_d = din("w_down", [DFF, DM])
    glu_w_d = din("glu_w", [512, 512])
    nmix_d = din("nmix_c", [128, 8])
    nffn_d = din("nffn_c", [128, 8])
    nfin_d = din("nfin_r", [1, DM])
    gain_d = din("gain_c", [128, 4])
    glub_d = din("glub_c", [128, 4])
    cw_d = din("cw_c", [128, 44 * 3])
    cb_d = din("cb_c", [128, 44])
    lre2_d = din("lre2", [128, 32])
    lim2_d = din("lim2", [128, 32])
    ldt2_d = din("ldt2", [128, 32])
    bre2_d = din("bre2", [128, 512])
    bim2_d = din("bim2", [128, 512])
    cre2_d = din("cre2", [128, 512])
    cim2_d = din("cim2", [128, 512])
    drep_d = din("drep", [128, 32])

    wI = nc.dram_tensor("wI_s", [128, 8 * WIN_COLS], BF16).ap()
    wO = nc.dram_tensor("wO_s", [DM, DM], BF16).ap()
    wU = nc.dram_tensor("wU_s", [128, NCT * 2048], BF16).ap()
    wD = nc.dram_tensor("wD_s", [DFF, DM], BF16).ap()
    s5Wtoep = nc.dram_tensor("s5Wtoep_s", [128, 4096], BF16).ap()
    s5Wsum = nc.dram_tensor("s5Wsum_s", [128, 8192], BF16).ap()
    s5Wfar = nc.dram_tensor("s5Wfar_s", [128, 8192], BF16).ap()

    dbg_out = {}
    CAP = 207 * 1024
    with contextlib.ExitStack() as st:
        big = st.enter_context(nc.sbuf_tensor("big", [128, CAP // 2], BF16))
        pbs = [st.enter_context(nc.psum_tensor("pb%d" % i, [128, 512], F32)) for i in range(8)]
        S = Sched(nc)
        M = Mem(big, CAP)

        def pbf(i):
            return pbs[i][:]

        def pbb(i):
            return pbs[i][:].bitcast(BF16)

        def DMA(chan, out, in_, reads=(), writes=(), eng="sp"):
            return S.op(eng, "dma_start", dict(out=out, in_=in_), reads, writes, chan=chan)

        def ACT(out, in_, func, reads, writes, **kw):
            return S.op("act", "activation", dict(out=out, in_=in_, func=func, **kw), reads, writes)

        def STT(out, in0, scalar, in1, op0, op1, reads, writes):
            return S.op("dve", "scalar_tensor_tensor", dict(out=out, in0=in0, scalar=scalar, in1=in1, op0=op0, op1=op1), reads, writes)

        def TT(eng, out, in0, in1, op, reads, writes):
            return S.op(eng, "tensor_tensor", dict(out=out, in0=in0, in1=in1, op=op), reads, writes)

        def TS(eng, out, in0, s1, s2, op0, op1, reads, writes):
            if s2 is None:
                return S.op(eng, "tensor_scalar", dict(out=out, in0=in0, scalar1=s1, scalar2=None, op0=op0), reads, writes)
            return S.op(eng, "tensor_scalar", dict(out=out, in0=in0, scalar1=s1, scalar2=s2, op0=op0, op1=op1), reads, writes)

        def TSS(eng, out, in_, scalar, op, reads, writes):
            return S.op(eng, "tensor_single_scalar", dict(out=out, in_=in_, scalar=scalar, op=op), reads, writes)

        def CP(eng, out, in_, reads, writes):
            return S.op(eng, "tensor_copy", dict(out=out, in_=in_), reads, writes)

        def MM(out, lhsT, rhs, start, stop, reads, writes):
            return S.op("pe", "matmul", dict(out=out, lhsT=lhsT, rhs=rhs, start=start, stop=stop), reads, writes)

        def TR(out, in_, reads, writes):
            return S.op("pe", "transpose", dict(out=out, in_=in_, identity=ident), list(reads) + ["ident"], writes)

        def RCP(out, in_, reads, writes):
            return S.op("dve", "reciprocal", dict(out=out, in_=in_), reads, writes)

        def NOP(eng, reads, writes=()):
            return S.op(eng, "nop", dict(), reads, writes)

        ident = M.alloc(256)
        onesm = M.alloc(256)
        Pm = M.alloc(256)
        Pm = M.alloc(256)
        COS = M.alloc(4096)
        SINS = M.alloc(4096)
        Dm = M.alloc(2048, F32, "p (h n) -> p h n", h=4)
        XIF = M.alloc(2048, F32, "p (h n) -> p h n", h=4)
        XIB = M.alloc(2048, F32, "p (h n) -> p h n", h=4)
        ZF = M.alloc(64, F32)
        ZB = M.alloc(64, F32)
        cst = M.alloc(128, F32)
        gain = M.alloc(64, F32)
        glub = M.alloc(64, F32)
        gluw = M.alloc(4096, BF16, "p (k n) -> p k n", k=4)
        cw = M.alloc(44 * 3 * 4, F32)[:, 0:132].rearrange("p (c k) -> p c k", k=3)
        cb = M.alloc(44 * 4, F32)
        nfbc = M.alloc(4096, F32)
        MPr = M.alloc(20 * 128, F32, "p (q g) -> p q g", q=20)
        MPi = M.alloc(20 * 128, F32, "p (q g) -> p q g", q=20)
        MPB = M.alloc(20 * 256, F32, "p (q r g) -> p q r g", q=20, r=2)
        mixT_flat = M.alloc(32768, BF16)
        mixT = mixT_flat.rearrange("p (k t) -> p k t", k=8)
        stat = M.alloc(6 * 16 * 4, F32, "p (a i) -> p a i", a=6)
        stat3 = M.alloc(64, F32)
        PBASE = M.top

        CE, CLSC = 0, 1
        cvals = {CE: EPS, CLSC: math.log(SC)}
        for h in range(4):
            cvals[4 + h] = LN_G[h]
            cvals[8 + h] = 128.0 * LN_G[h]
            cvals[12 + h] = 127.0 * LN_G[h] + math.log(SC)
        for k, v in cvals.items():
            S.op("pool", "memset", dict(ap=cst[:, k:k + 1], constant=float(v)), (), [("cst", k)])

        M.top = PBASE
        identf = M.alloc(512, F32)
        pidx = M.alloc(64, F32)
        hi = M.alloc(64, F32)
        imod = M.alloc(64, F32)
        invf = M.alloc(64, F32)
        sgn = M.alloc(64, F32)
        nd = M.alloc(512, F32)
        nidx = M.alloc(512, F32)
        SMALL_END = M.top
        lidx = M.alloc(8192, F32)
        ta = M.alloc(8192, F32)
        tb_ = M.alloc(8192, F32)
        tc_ = M.alloc(8192, F32)

        S.op("pool", "memset", dict(ap=identf, constant=0.0), (), ["identf"])
        S.op("pool", "affine_select", dict(out=identf, in_=identf, pattern=[[-1, 128]], compare_op=ALU.not_equal,
                                           fill=1.0, base=0, channel_multiplier=1), ["identf"], ["identf"])
        CP("dve", ident, identf, ["identf"], ["ident"])
        Pmf = M.alloc(512, F32)
        S.op("pool", "memset", dict(ap=Pmf, constant=0.0), (), ["Pmf"])
        S.op("pool", "affine_select", dict(out=Pmf, in_=Pmf, pattern=[[-1, 128]], compare_op=ALU.not_equal,
                                           fill=1.0, base=-64, channel_multiplier=1), ["Pmf"], ["Pmf"])
        S.op("pool", "affine_select", dict(out=Pmf, in_=Pmf, pattern=[[-1, 128]], compare_op=ALU.not_equal,
                                           fill=1.0, base=64, channel_multiplier=1), ["Pmf"], ["Pmf"])
        CP("dve", Pm, Pmf, ["Pmf"], ["Pm"])
        Pmf = M.alloc(512, F32)
        S.op("pool", "memset", dict(ap=Pmf, constant=0.0), (), ["Pmf"])
        S.op("pool", "affine_select", dict(out=Pmf, in_=Pmf, pattern=[[-1, 128]], compare_op=ALU.not_equal,
                                           fill=1.0, base=-64, channel_multiplier=1), ["Pmf"], ["Pmf"])
        S.op("pool", "affine_select", dict(out=Pmf, in_=Pmf, pattern=[[-1, 128]], compare_op=ALU.not_equal,
                                           fill=1.0, base=64, channel_multiplier=1), ["Pmf"], ["Pmf"])
        CP("dve", Pm, Pmf, ["Pmf"], ["Pm"])
        S.op("pool", "memset", dict(ap=onesm, constant=1.0 / 128.0), (), ["onesm"])
        S.op("pool", "iota", dict(out=pidx[:, 0:1], pattern=[[0, 1]], base=0, channel_multiplier=1, allow_small_or_imprecise_dtypes=True), (), ["pidx"])
        S.op("pool", "iota", dict(out=lidx, pattern=[[1, 2048]], base=0, channel_multiplier=0, allow_small_or_imprecise_dtypes=True), (), ["lidx"])
        S.op("pool", "iota", dict(out=nd, pattern=[[1, 128]], base=0, channel_multiplier=-1, allow_small_or_imprecise_dtypes=True), (), ["nd"])
        S.op("pool", "iota", dict(out=nidx, pattern=[[1, 128]], base=0, channel_multiplier=0, allow_small_or_imprecise_dtypes=True), (), ["nidx"])
        TSS("dve", hi[:, 0:1], pidx[:, 0:1], 64.0, ALU.is_ge, ["pidx"], ["hi"])
        STT(imod[:, 0:1], hi[:, 0:1], -64.0, pidx[:, 0:1], ALU.mult, ALU.add, ["hi", "pidx"], ["imod"])
        ACT(invf[:, 0:1], imod[:, 0:1], AF.Exp, ["imod"], ["invf"], scale=-math.log(10000.0) / 64.0)
        TS("dve", sgn[:, 0:1], hi[:, 0:1], 2.0, -1.0, ALU.mult, ALU.add, ["hi"], ["sgn"])

        def sincos(a_ap, tA, tB, out_ap, tok_a, tok_A, tok_B, tok_out, quarter, post_scalar=None, deps=()):
            if quarter != 0.0:
                TSS("dve", tA, a_ap, quarter, ALU.add, [tok_a], [tok_A])
                src, stok = tA, tok_A
            else:
                src, stok = a_ap, tok_a
            TSS("dve", tB, src, MAGIC, ALU.add, [stok], [tok_B])
            TSS("dve", tB, tB, MAGIC, ALU.subtract, [tok_B], [tok_B])
            TT("dve", tA, src, tB, ALU.subtract, [stok, tok_B], [tok_A])
            ACT(tA, tA, AF.Sin, [tok_A], [tok_A], scale=SIN_SCALE)
            if post_scalar is None:
                CP("dve", out_ap, tA, [tok_A] + list(deps), [tok_out])
            else:
                TS("dve", out_ap, tA, post_scalar, None, ALU.mult, None, [tok_A] + list(deps), [tok_out])

        TS("dve", ta, lidx, invf[:, 0:1], 1.0 / TWO_PI, ALU.mult, ALU.mult, ["lidx", "invf"], ["ta"])
        sincos(ta, tb_, tc_, SINS, "ta", "tb", "tc", "SINS", 0.0, post_scalar=sgn[:, 0:1], deps=["sgn"])
        sincos(ta, tb_, tc_, COS, "ta", "tb", "tc", "COS", 0.25)
        ACT(nd, nd, AF.Abs, ["nd"], ["nd"])
        for h in range(4):
            ACT(Dm[:, h, :], nd, AF.Exp, ["nd", ("cst", CLSC)], [("Dm", h)], scale=LN_G[h], bias=cst[:, CLSC:CLSC + 1])
            ACT(XIF[:, h, :], nidx, AF.Exp, ["nidx", ("cst", 4 + h)], [("XIF", h)], scale=LN_G[h], bias=cst[:, 4 + h:5 + h])
            ACT(XIB[:, h, :], nidx, AF.Exp, ["nidx", ("cst", 8 + h)], [("XIB", h)], scale=-LN_G[h], bias=cst[:, 8 + h:9 + h])
            ACT(ZF[:, h:h + 1], pidx[:, 0:1], AF.Exp, ["pidx", ("cst", 12 + h)], [("ZF", h)], scale=-LN_G[h], bias=cst[:, 12 + h:13 + h])
            ACT(ZB[:, h:h + 1], pidx[:, 0:1], AF.Exp, ["pidx", ("cst", CLSC)], [("ZB", h)], scale=LN_G[h], bias=cst[:, CLSC:CLSC + 1])

        DMA("ld0", gain[:, 0:4], gain_d, (), ["gain"])
        DMA("ld1", glub[:, 0:4], glub_d, (), ["glub"])
        DMA("ld3", cw.rearrange("p c k -> p (c k)"), cw_d, (), ["cw"])
        DMA("ld4", cb[:, 0:44], cb_d, (), ["cb"])
        DMA("ld5", nfbc, nfin_d.partition_broadcast(128), (), ["nfbc"])
        DMA("ldc0", gluw, glu_w_d.rearrange("(k p) n -> p k n", p=128), (), ["gluw"], eng="pool")
        S.barrier()
        SBASE = M.top
        M.top = SMALL_END
        sm = {}
        for nm in ["lr", "li", "dt", "t0", "t1", "t2", "AR", "AI", "CR", "CI", "IR", "II", "u0", "u1", "u2", "u3"]:
            sm[nm] = M.alloc(128, F32)
        POSr = M.alloc(9 * 128, F32, "p (q g) -> p q g", q=9)
        POSi = M.alloc(9 * 128, F32, "p (q g) -> p q g", q=9)
        NEGr = M.alloc(8 * 128, F32, "p (q g) -> p q g", q=8)
        NEGi = M.alloc(8 * 128, F32, "p (q g) -> p q g", q=8)
        Et = {}
        for nm in ["q", "k", "s", "w"]:
            Et[nm] = (M.alloc(1024, F32, "p (g t) -> p g t", g=32), M.alloc(1024, F32, "p (g t) -> p g t", g=32))
        Braw_r = M.alloc(2048, F32, "p (g c) -> p g c", g=32)
        Braw_i = M.alloc(2048, F32, "p (g c) -> p g c", g=32)
        BBr = M.alloc(2048, F32, "p (g c) -> p g c", g=32)
        BBi = M.alloc(2048, F32, "p (g c) -> p g c", g=32)
        Cr_ = M.alloc(2048, F32, "p (g c) -> p g c", g=32)
        Ci_ = M.alloc(2048, F32, "p (g c) -> p g c", g=32)
        drep = M.alloc(128, F32)
        pq = M.alloc(64, F32)
        fq = M.alloc(512, F32)
        MF = M.alloc(512, F32)
        MB = M.alloc(512, F32)
        bigs = [M.alloc(16384, F32) for _ in range(5)]
        stg = [M.alloc(8192, BF16)]
        stg.append(stg[0])
        tp_f = [M.alloc(512, F32) for _ in range(2)]

        DMA("ld6", sm["lr"], lre2_d, (), ["v_lr"])
        DMA("ld7", sm["li"], lim2_d, (), ["v_li"])
        DMA("ld8", sm["dt"], ldt2_d, (), ["v_dt"])
        DMA("ld9", Braw_r.rearrange("p g c -> p (g c)"), bre2_d, (), ["Braw_r"])
        DMA("ld10", Braw_i.rearrange("p g c -> p (g c)"), bim2_d, (), ["Braw_i"])
        DMA("ld11", Cr_.rearrange("p g c -> p (g c)"), cre2_d, (), ["Cr"])
        DMA("ld12", Ci_.rearrange("p g c -> p (g c)"), cim2_d, (), ["Ci"])
        DMA("ld13", drep[:, 0:32], drep_d, (), ["drep"])

        def abar(lr, li, dt, t0, t1, t2, out_r, out_i, pfx):
            T = lambda n: pfx + n
            TSS("dve", lr, lr, -1e-4, ALU.min, [T("lr")], [T("lr")])
            ACT(dt, dt, AF.Exp, [T("dt")], [T("dt")])
            TT("dve", t0, lr, dt, ALU.mult, [T("lr"), T("dt")], [T("t0")])
            TS("dve", t0, t0, -8.0, 1.0 / 16.0, ALU.max, ALU.mult, [T("t0")], [T("t0")])
            TSS("dve", out_r, t0, 1.0 / math.factorial(8), ALU.mult, [T("t0")], [T("or")])
            for k in range(7, 0, -1):
                STT(out_r, out_r, 1.0 / math.factorial(k), t0, ALU.add, ALU.mult, [T("or"), T("t0")], [T("or")])
            TSS("dve", t0, out_r, 1.0, ALU.add, [T("or")], [T("t0")])
            for _ in range(4):
                TT("dve", t0, t0, t0, ALU.mult, [T("t0")], [T("t0")])
            STT(t1, li, 1.0 / TWO_PI, dt, ALU.mult, ALU.mult, [T("li"), T("dt")], [T("t1")])
            TSS("dve", t2, t1, MAGIC, ALU.add, [T("t1")], [T("t2")])
            TSS("dve", t2, t2, MAGIC, ALU.subtract, [T("t2")], [T("t2")])
            TT("dve", t1, t1, t2, ALU.subtract, [T("t1"), T("t2")], [T("t1")])
            TSS("dve", t1, t1, math.pi, ALU.mult, [T("t1")], [T("t1")])
            TT("dve", t2, t1, t1, ALU.mult, [T("t1")], [T("t2")])
            TSS("dve", out_i, t2, 1.0 / math.factorial(13), ALU.mult, [T("t2")], [T("oi")])
            for k in range(5, 0, -1):
                STT(out_i, out_i, ((-1.0) ** k) / math.factorial(2 * k + 1), t2, ALU.add, ALU.mult, [T("oi"), T("t2")], [T("oi")])
            STT(out_i, out_i, 1.0, t1, ALU.add, ALU.mult, [T("oi"), T("t1")], [T("oi")])
            TSS("dve", out_r, t2, -1.0 / math.factorial(14), ALU.mult, [T("t2"), T("t0")], [T("or")])
            for k in range(6, 0, -1):
                STT(out_r, out_r, ((-1.0) ** k) / math.factorial(2 * k), t2, ALU.add, ALU.mult, [T("or"), T("t2")], [T("or")])
            TSS("dve", out_r, out_r, 1.0, ALU.add, [T("or")], [T("or")])
            TT("dve", t2, out_i, out_i, ALU.mult, [T("oi"), T("or")], [T("t2")])
            TS("dve", t2, t2, -2.0, 1.0, ALU.mult, ALU.add, [T("t2")], [T("t2")])
            STT(out_i, out_i, 2.0, out_r, ALU.mult, ALU.mult, [T("oi"), T("or")], [T("oi")])
            TT("dve", out_r, t2, t0, ALU.mult, [T("t2"), T("t0"), T("oi")], [T("or")])
            TT("dve", out_i, out_i, t0, ALU.mult, [T("oi"), T("t0")], [T("oi")])

        AR, AI = sm["AR"], sm["AI"]
        abar(sm["lr"], sm["li"], sm["dt"], sm["t0"], sm["t1"], sm["t2"], AR, AI, "v_")
        nmix = M.alloc(64, F32)
        nffn = M.alloc(64, F32)
        PCH = 1536
        wtmp = [M.alloc(PCH * 4, F32) for _ in range(2)]
        wob = [M.alloc(PCH * 2, BF16) for _ in range(2)]
        DMA("ld0", nmix[:, 0:8], nmix_d, (), ["nmix"])
        DMA("ld1", nffn[:, 0:8], nffn_d, (), ["nffn"])
        prep_i = [0]

        def prep(src_, r0, c0, ncols, scale_ap, scale_tok, outs):
            i = prep_i[0] % 2
            prep_i[0] += 1
            DMA("wpi%d" % i, wtmp[i][:, 0:ncols], src_[r0:r0 + 128, c0:c0 + ncols], (), [("wtmp", i)])
            if scale_ap is None:
                ACT(wob[i][:, 0:ncols], wtmp[i][:, 0:ncols], AF.Copy, [("wtmp", i)], [("wob", i)])
            else:
                ACT(wob[i][:, 0:ncols], wtmp[i][:, 0:ncols], AF.Copy, [("wtmp", i), scale_tok], [("wob", i)], scale=scale_ap)
            for n_, (dst_ap, vf, tok) in enumerate(outs):
                DMA("wpo%d%s" % (i, "abc"[n_]), dst_ap, vf(wob[i]), [("wob", i)], [tok], eng="act")

        wI_toks, wO_toks, wU_toks, wD_toks = [], [], [], []
        wI_A = wI[:, 0:16384].rearrange("p (u k c) -> p u k c", u=8, k=8)
        wI_B = wI[:, 16384:20480].rearrange("p (u k c) -> p u k c", u=4, k=8)
        wI_C = wI[:, 20480:28672].rearrange("p (u k c) -> p u k c", u=2, k=8)
        for kt in range(8):
            prep(w_in_d, kt * 128, 0, 1536, nmix[:, kt:kt + 1], "nmix",
                 [(wI_A[:, 0:6, kt, :], lambda w: w[:, 0:1536].rearrange("p (u c) -> p u c", u=6), ("wI", kt, 0))])
            prep(w_in_d, kt * 128, 1536, 1536, nmix[:, kt:kt + 1], "nmix",
                 [(wI_A[:, 6:8, kt, :], lambda w: w[:, 0:512].rearrange("p (u c) -> p u c", u=2), ("wI", kt, 1)),
                  (wI_B[:, :, kt, :], lambda w: w[:, 512:1024].rearrange("p (u c) -> p u c", u=4), ("wI", kt, 2)),
                  (wI_C[:, 0, kt, :], lambda w: w[:, 1024:1536], ("wI", kt, 3))])
            prep(w_in_d, kt * 128, 3072, 512, nmix[:, kt:kt + 1], "nmix",
                 [(wI_C[:, 1, kt, :], lambda w: w[:, 0:512], ("wI", kt, 4))])
            wI_toks += [("wI", kt, n_) for n_ in range(5)]
        for kt in range(8):
            prep(w_out_d, kt * 128, 0, DM, None, None, [(wO[kt * 128:(kt + 1) * 128, :], lambda w: w[:, 0:DM], ("wO", kt))])
            wO_toks.append(("wO", kt))
        wU_v = wU.rearrange("p (c k v n) -> p c k v n", c=NCT, k=8, v=2)
        for kt in range(8):
            for hf in range(2):
                for ch_ in range(2):
                    prep(w_up_d, kt * 128, hf * DFF + ch_ * 1408, 1408, nffn[:, kt:kt + 1], "nffn",
                         [(wU_v[:, ch_ * 11:(ch_ + 1) * 11, kt, hf, :], lambda w: w[:, 0:1408].rearrange("p (c n) -> p c n", c=11), ("wU", kt, hf, ch_))])
                    wU_toks.append(("wU", kt, hf, ch_))
        for c in range(NCT):
            prep(w_down_d, c * 128, 0, DM, None, None, [(wD[c * 128:(c + 1) * 128, :], lambda w: w[:, 0:DM], ("wD", c))])
            wD_toks.append(("wD", c))

        ATOK = ["v_or", "v_oi"]
        lr, li = sm["lr"], sm["li"]
        u0, u1, u2, u3 = sm["u0"], sm["u1"], sm["u2"], sm["u3"]
        TT("dve", u0, lr, lr, ALU.mult, ["v_lr"] + ATOK, ["u0"])
        TT("dve", u1, li, li, ALU.mult, ["v_li"], ["u1"])
        TT("dve", u0, u0, u1, ALU.add, ["u0", "u1"], ["u0"])
        RCP(u0, u0, ["u0"], ["u0"])
        TSS("dve", u1, AR, -1.0, ALU.add, ATOK + ["u1"], ["u1"])
        TT("dve", u2, u1, lr, ALU.mult, ["u1", "v_lr"], ["u2"])
        TT("dve", u3, AI, li, ALU.mult, ATOK + ["v_li"], ["u3"])
        TT("dve", u2, u2, u3, ALU.add, ["u2", "u3"], ["u2"])
        TT("dve", sm["CR"], u2, u0, ALU.mult, ["u2", "u0"], ["CR"])
        TT("dve", u2, AI, lr, ALU.mult, ATOK + ["v_lr", "CR"], ["u2"])
        TT("dve", u3, u1, li, ALU.mult, ["u1", "v_li", "CR"], ["u3"])
        TT("dve", u2, u2, u3, ALU.subtract, ["u2", "u3"], ["u2"])
        TT("dve", sm["CI"], u2, u0, ALU.mult, ["u2", "u0"], ["CI"])
        TT("dve", u0, AR, AR, ALU.mult, ATOK + ["CI", "u0"], ["u0"])
        TT("dve", u1, AI, AI, ALU.mult, ATOK + ["CI", "u1"], ["u1"])
        TT("dve", u0, u0, u1, ALU.add, ["u0", "u1"], ["u0"])
        RCP(u0, u0, ["u0"], ["u0"])
        TT("dve", sm["IR"], AR, u0, ALU.mult, ATOK + ["u0"], ["IR"])
        STT(sm["II"], AI, -1.0, u0, ALU.mult, ALU.mult, ATOK + ["u0"], ["II"])

        def cmul(orr, oii, ar, ai, br, bi, toks_in, tok_out):
            TT("dve", u2, ar, br, ALU.mult, toks_in + ["u2"], ["u2"])
            TT("dve", u3, ai, bi, ALU.mult, toks_in + ["u3"], ["u3"])
            TT("dve", orr, u2, u3, ALU.subtract, ["u2", "u3"], [tok_out + "r"])
            TT("dve", u2, ar, bi, ALU.mult, toks_in + [tok_out + "r"], ["u2"])
            TT("dve", u3, ai, br, ALU.mult, toks_in + [tok_out + "r"], ["u3"])
            TT("dve", oii, u2, u3, ALU.add, ["u2", "u3"], [tok_out + "i"])

        S.op("dve", "memset", dict(ap=POSr[:, 0, :], constant=1.0), (), ["POS0r"])
        S.op("dve", "memset", dict(ap=POSi[:, 0, :], constant=0.0), (), ["POS0i"])
        S.op("dve", "memset", dict(ap=NEGr[:, 0, :], constant=1.0), (), ["NEG0r"])
        S.op("dve", "memset", dict(ap=NEGi[:, 0, :], constant=0.0), (), ["NEG0i"])
        CP("dve", POSr[:, 1, :], AR, ATOK, ["POS1r"])
        CP("dve", POSi[:, 1, :], AI, ATOK, ["POS1i"])
        CP("dve", NEGr[:, 1, :], sm["IR"], ["IR"], ["NEG1r"])
        CP("dve", NEGi[:, 1, :], sm["II"], ["II"], ["NEG1i"])
        for p in range(2, 9):
            cmul(POSr[:, p, :], POSi[:, p, :], POSr[:, p - 1, :], POSi[:, p - 1, :], POSr[:, 1, :], POSi[:, 1, :],
                 ["POS%dr" % (p - 1), "POS%di" % (p - 1), "POS1r", "POS1i"], "POS%d" % p)
        for p in range(2, 8):
            cmul(NEGr[:, p, :], NEGi[:, p, :], NEGr[:, p - 1, :], NEGi[:, p - 1, :], NEGr[:, 1, :], NEGi[:, 1, :],
                 ["NEG%dr" % (p - 1), "NEG%di" % (p - 1), "NEG1r", "NEG1i"], "NEG%d" % p)
        POST = [("POS%d" % p) + x for p in range(9) for x in "ri"]
        NEGT = [("NEG%d" % p) + x for p in range(8) for x in "ri"]
        CP("dve", MPr[:, 1, :], POSr[:, 8, :], POST, ["MP1r"])
        CP("dve", MPi[:, 1, :], POSi[:, 8, :], POST, ["MP1i"])
        for p in range(2, 17):
            cmul(MPr[:, p, :], MPi[:, p, :], MPr[:, p - 1, :], MPi[:, p - 1, :], MPr[:, 1, :], MPi[:, 1, :],
                 ["MP%dr" % (p - 1), "MP%di" % (p - 1), "MP1r", "MP1i"], "MP%d" % p)
        for p, q_ in ((17, 16), (18, 17), (19, 18)):
            cmul(MPr[:, p, :], MPi[:, p, :], MPr[:, q_, :], MPi[:, q_, :], MPr[:, q_, :], MPi[:, q_, :],
                 ["MP%dr" % q_, "MP%di" % q_], "MP%d" % p)
        MPT0 = [("MP%d" % p) + x for p in range(1, 20) for x in "ri"]
        TSS("dve", MPB[:, 1:20, 0, :], MPi[:, 1:20, :], -1.0, ALU.mult, MPT0, ["MPB0"])
        CP("dve", MPB[:, 1:20, 1, :], MPi[:, 1:20, :], MPT0, ["MPB1"])
        MPT = MPT0 + ["MPB0", "MPB1"]
        H0, H1 = slice(0, 64), slice(64, 128)
        for t in range(8):
            for (nm, src0, i0, src1, i1) in (("q", "POS", t, "NEG", t), ("k", "NEG", t, "POS", t), ("s", "POS", 7 - t, "POS", t), ("w", "POS", t + 1, "POS", 8 - t)):
                for ri in range(2):
                    tabs = {"POS": (POSr, POSi), "NEG": (NEGr, NEGi)}
                    CP("pool", Et[nm][ri][H0, :, t], tabs[src0][ri][H0, i0, :], POST + NEGT, [("E", nm, ri)])
                    CP("pool", Et[nm][ri][H1, :, t], tabs[src1][ri][H1, i1, :], POST + NEGT, [("E", nm, ri)])
        def bc_g(ap):
            return ap[:, 0:32].unsqueeze(2).to_broadcast([128, 32, 16])
        TT("dve", BBr, Braw_r, bc_g(sm["CR"]), ALU.mult, ["Braw_r", "CR"], ["BBr"])
        TT("dve", BBi, Braw_i, bc_g(sm["CI"]), ALU.mult, ["Braw_i", "CI"], ["BBi"])
        TT("dve", BBr, BBr, BBi, ALU.subtract, ["BBr", "BBi"], ["BBr"])
        TT("dve", BBi, Braw_i, bc_g(sm["CR"]), ALU.mult, ["Braw_i", "CR", "BBr"], ["BBi"])
        TT("dve", Braw_r, Braw_r, bc_g(sm["CI"]), ALU.mult, ["Braw_r", "CI", "BBr"], ["Braw_r"])
        TT("dve", BBi, BBi, Braw_r, ALU.add, ["BBi", "Braw_r"], ["BBi"])

        def v4(ap):
            return ap.rearrange("p (g t c) -> p g t c", g=32, t=8)

        def bX(ap):
            return ap.unsqueeze(2).to_broadcast([128, 32, 8, 16])

        def bE(ap):
            return ap.unsqueeze(3).to_broadcast([128, 32, 8, 16])

        def BG(k):
            return ("big", k)

        def cprod(ko_r, ko_i, Xr, Xi, xtoks, nm, k1, neg_im=False):
            Er, Ei = Et[nm]
            et = [("E", nm, 0), ("E", nm, 1)]
            outr, outi, t1_ = bigs[ko_r], bigs[ko_i], bigs[k1]
            TT("dve", v4(outr), bX(Xr), bE(Er), ALU.mult, xtoks + et, [BG(ko_r)])
            TT("dve", v4(t1_), bX(Xi), bE(Ei), ALU.mult, xtoks + et, [BG(k1)])
            TT("dve", outr, outr, t1_, ALU.subtract, [BG(ko_r), BG(k1)], [BG(ko_r)])
            TT("dve", v4(outi), bX(Xr), bE(Ei), ALU.mult, xtoks + et, [BG(ko_i)])
            TT("dve", v4(t1_), bX(Xi), bE(Er), ALU.mult, xtoks + et, [BG(k1)])
            if neg_im:
                STT(outi, outi, -1.0, t1_, ALU.mult, ALU.subtract, [BG(ko_i), BG(k1)], [BG(ko_i)])
            else:
                TT("dve", outi, outi, t1_, ALU.add, [BG(ko_i), BG(k1)], [BG(ko_i)])

        def TRF(out, in_, reads, writes):
            return S.op("pe", "transpose", dict(out=out, in_=in_, identity=identf), list(reads) + ["identf"], writes)

        CT = ["Cr", "Ci"]
        BT = ["BBr", "BBi"]
        cprod(0, 1, Cr_, Ci_, CT, "w", 2, neg_im=True)
        CP("dve", stg[0], bigs[0], [BG(0)], [("stg", 0)])
        DMA("st0", s5Wfar[:, 0:4096], stg[0], [("stg", 0)], ["s5Wfar0"], eng="pool")
        CP("dve", stg[1], bigs[1], [BG(1)], [("stg", 0)])
        DMA("st0", s5Wfar[:, 4096:8192], stg[1], [("stg", 0)], ["s5Wfar1"], eng="pool")
        cprod(0, 1, BBr, BBi, BT, "s", 2)
        nb = 0
        for ri in range(2):
            stv = stg[ri].rearrange("p (g m) -> p g m", g=32)
            for g4 in range(8):
                bk = 2 + nb % 4
                nb += 1
                for gg in range(4):
                    g = g4 * 4 + gg
                    TRF(pbf(bk)[:, gg * 128:(gg + 1) * 128], bigs[ri][:, g * 128:(g + 1) * 128], [BG(ri)], [("pb", bk)])
                CP("dve", stv[:, g4 * 4:(g4 + 1) * 4, :], pbf(bk).rearrange("p (g m) -> p g m", g=4), [("pb", bk)], [("stg", 0)])
            DMA("st0", s5Wsum[:, ri * 4096:(ri + 1) * 4096], stg[ri], [("stg", 0)], ["s5Wsum%d" % ri], eng="pool")
        cprod(0, 1, Cr_, Ci_, CT, "q", 4, neg_im=True)
        cprod(2, 3, BBr, BBi, BT, "k", 4)
        TS("dve", pq[:, 0:1], pidx[:, 0:1], 1.0 / 16.0, -15.0 / 32.0, ALU.mult, ALU.add, ["pidx"], ["pq"])
        TSS("dve", pq[:, 0:1], pq[:, 0:1], MAGIC, ALU.add, ["pq"], ["pq"])
        TSS("dve", pq[:, 0:1], pq[:, 0:1], MAGIC, ALU.subtract, ["pq"], ["pq"])
        TS("dve", fq[:, 0:128], nidx[:, 0:128], 1.0 / 16.0, -15.0 / 32.0, ALU.mult, ALU.add, ["nidx"], ["fq"])
        TSS("dve", fq[:, 0:128], fq[:, 0:128], MAGIC, ALU.add, ["fq"], ["fq"])
        TSS("dve", fq[:, 0:128], fq[:, 0:128], MAGIC, ALU.subtract, ["fq"], ["fq"])
        TS("dve", MF[:, 0:128], fq[:, 0:128], pq[:, 0:1], None, ALU.is_ge, None, ["fq", "pq"], ["MF"])
        TS("dve", MB[:, 0:128], fq[:, 0:128], pq[:, 0:1], None, ALU.is_le, None, ["fq", "pq"], ["MB"])
        stgT = stg[0].rearrange("p (g m) -> p g m", g=32)
        QK = [BG(0), BG(1), BG(2), BG(3)]
        for g in range(32):
            gsl = slice(g * 128, (g + 1) * 128)
            bf_, bb_ = (0, 1) if g % 2 == 0 else (6, 7)
            k_ = g % 2
            MM(pbf(bf_)[:, 0:128], bigs[2][H0, gsl], bigs[0][H0, gsl], True, False, QK, [("pb", bf_)])
            MM(pbf(bf_)[:, 0:128], bigs[3][H0, gsl], bigs[1][H0, gsl], False, True, QK, [("pb", bf_)])
            MM(pbf(bb_)[:, 0:128], bigs[2][H1, gsl], bigs[0][H1, gsl], True, False, QK, [("pb", bb_)])
            MM(pbf(bb_)[:, 0:128], bigs[3][H1, gsl], bigs[1][H1, gsl], False, True, QK, [("pb", bb_)])
            TT("dve", tp_f[k_][:, 0:128], pbf(bf_)[:, 0:128], MF[:, 0:128], ALU.mult, [("pb", bf_), "MF"], [("tpf", k_)])
            TT("dve", bigs[4][:, gsl], pbf(bb_)[:, 0:128], MB[:, 0:128], ALU.mult, [("pb", bb_), "MB"], [BG(4)])
            TT("dve", tp_f[k_][:, 0:128], tp_f[k_][:, 0:128], bigs[4][:, gsl], ALU.add, [("tpf", k_), BG(4)], [("tpf", k_)])
            STT(stgT[:, g, :], identf, drep[:, g:g + 1], tp_f[k_][:, 0:128], ALU.mult, ALU.add, ["identf", "drep", ("tpf", k_)], [("stg", 0)])
        DMA("st0", s5Wtoep, stg[0], [("stg", 0)], ["s5Wtoep"], eng="pool")
        S5W = ["s5Wfar0", "s5Wfar1", "s5Wsum0", "s5Wsum1", "s5Wtoep"]

        if debug == "setup":
            S.barrier()
            M.top = PBASE
            d1_ = M.alloc(8192)
            d2_ = M.alloc(16384)
            d3_ = M.alloc(16384)
            DMA("lb", d1_, s5Wtoep, S5W, ["d1"])
            DMA("lc", d2_, s5Wsum, S5W, ["d2"])
            DMA("ld", d3_, s5Wfar, S5W, ["d3"])
            DEBUG["Wtoep"] = (d1_, [128, 4096], BF16, ["d1"])
            DEBUG["Wsum"] = (d2_, [128, 8192], BF16, ["d2"])
            DEBUG["Wfar"] = (d3_, [128, 8192], BF16, ["d3"])
            DEBUG["MPr"] = (MPr[:, 1:17, :].rearrange("p q g -> p (q g)"), [128, 16 * 32], F32, MPT)
            DEBUG["MPi"] = (MPi[:, 1:17, :].rearrange("p q g -> p (q g)"), [128, 16 * 32], F32, MPT)

        nseq_run = 0 if debug == "setup" else (1 if debug else NSEQ)
        for s in range(nseq_run):
            S.barrier()
            M.top = PBASE
            hT = M.alloc(32768, BF16, "p (k t) -> p k t", k=8)
            V = M.alloc(16384, BF16, "p (i c) -> p i c", i=16)
            qT = M.alloc(4096)
            kT = M.alloc(4096)
            qf = M.alloc(4096)
            qb = M.alloc(4096)
            Kf = M.alloc(4096, BF16, "p (j d) -> p j d", j=16)
            Kb = M.alloc(4096, BF16, "p (j d) -> p j d", j=16)
            Rbf = M.alloc(8192, BF16, "p (a j e) -> p a j e", a=2, j=16)
            R32 = M.alloc(1024, F32, "p (a e) -> p a e", a=2)
            gs = M.alloc(4096)
            xt = [M.alloc(4096, F32) for _ in range(2)]
            hbs = [M.alloc(2048) for _ in range(2)]
            junk = M.alloc(2048)
            wv = M.alloc(8192, BF16, "p (k n) -> p k n", k=8)
            wq = [M.alloc(4096, BF16, "p (k n) -> p k n", k=8) for _ in range(2)]
            wg = M.alloc(2048, BF16, "p (k n) -> p k n", k=8)
            qs = [M.alloc(1024) for _ in range(2)]
            qs = [M.alloc(1024) for _ in range(2)]
            r1 = [M.alloc(2048, F32) for _ in range(2)]
            r2 = [M.alloc(2048, F32) for _ in range(2)]
            Sm = M.alloc(1024, BF16, "p (j n) -> p j n", j=4)
            sq = [M.alloc(1024) for _ in range(2)]
            sd = [M.alloc(2048, F32) for _ in range(2)]
            on = [M.alloc(2048, F32) for _ in range(2)]
            def wI_unit(off, ncols):
                return wI[:, off:off + 8 * ncols].rearrange("p (k c) -> p k c", k=8)

            pend_ev = None
            for i in range(16):
                b = i % 2
                DMA("xt%d" % b, xt[b], x_d[s, i * 128:(i + 1) * 128, :], (), [("xt", b)])
                ACT(junk, xt[b], AF.Square, [("xt", b)], ["junk", ("ss", i)], accum_out=stat[:, 0, i:i + 1])
                ACT(stat[:, 1, i:i + 1], stat[:, 0, i:i + 1], AF.Sqrt, [("ss", i), ("cst", CE)], [("sd", i)], scale=1.0 / DM, bias=cst[:, CE:CE + 1])
                RCP(stat[:, 2, i:i + 1], stat[:, 1, i:i + 1], [("sd", i)], [("rs", i)])
                hb = hbs[b]
                TS("dve", hb, xt[b], stat[:, 2, i:i + 1], None, ALU.mult, None, [("xt", b), ("rs", i)], [("hb", b)])
                pk = i % 2
                for kt in range(8):
                    TR(pbb(pk)[:, kt * 128:(kt + 1) * 128], hb[:, kt * 128:(kt + 1) * 128], [("hb", b)], [("pb", pk)])
                if pend_ev is not None:
                    ACT(hT[:, :, pend_ev[0] * 128:(pend_ev[0] + 1) * 128], pbb(pend_ev[1]).rearrange("p (k t) -> p k t", k=8), AF.Copy, [("pb", pend_ev[1])], [("hT", pend_ev[0])])
                pend_ev = (i, pk)
            ACT(hT[:, :, pend_ev[0] * 128:(pend_ev[0] + 1) * 128], pbb(pend_ev[1]).rearrange("p (k t) -> p k t", k=8), AF.Copy, [("pb", pend_ev[1])], [("hT", pend_ev[0])])
            hT_all = [("hT", i) for i in range(16)]

            DMA("wv", wv, wI_unit(20480, 512), wI_toks, ["wv"])
            for i in range(16):
                pk = 2 + (i % 2)
                for kt in range(8):
                    MM(pbf(pk), hT[:, kt, i * 128:(i + 1) * 128], wv[:, kt, :], kt == 0, kt == 7, [("hT", i), "wv"], [("pb", pk)])
                ACT(V[:, i, :], pbf(pk), AF.Copy, [("pb", pk)], [("V", i)])

            for h in range(4):
                hs = slice(h * 128, (h + 1) * 128)
                DMA("wq0", wq[0], wI_unit(h * 2048, 256), wI_toks, [("wq", 0, 0), ("wq", 0, 1)])
                DMA("wq1", wq[1], wI_unit((4 + h) * 2048, 256), wI_toks, [("wq", 1, 0), ("wq", 1, 1)])
                DMA("wg", wg, wI_unit(16384 + h * 1024, 128), wI_toks, ["wg"])
                units = [(which, tb) for which in range(2) for tb in range(4)]

                def proj_unit(ui, which, tb):
                    pa = (ui % 2) * 2
                    tsl = slice(tb * 512, (tb + 1) * 512)
                    for kt in range(8):
                        MM(pbf(pa), wq[which][:, kt, 0:128], hT[:, kt, tsl], kt == 0, kt == 7, hT_all + [("wq", which, 0)], [("pb", pa)])
                    ACT(qs[ui % 2][:, 0:512], pbf(pa), AF.Copy, [("pb", pa)], [("qs", ui % 2)])

                def rot_unit(ui, which, tb):
                    pa = (ui % 2) * 2
                    pbk = pa + 1
                    rr = ui % 2
                    dst = qT if which == 0 else kT
                    dtok = "qT" if which == 0 else "kT"
                    tsl = slice(tb * 512, (tb + 1) * 512)
                    MM(pbf(pbk), Pm, qs[rr][:, 0:512], True, True, [("qs", rr), "Pm"], [("pb", pbk)])
                    TT("dve", r1[rr][:, 0:512], pbf(pa), COS[:, tsl], ALU.mult, [("pb", pa), ("qs", rr), "COS"], [("r1", rr)])
                    TT("dve", r2[rr][:, 0:512], pbf(pbk), SINS[:, tsl], ALU.mult, [("pb", pbk), "SINS"], [("r2", rr)])
                    TT("pool", dst[:, tsl], r1[rr][:, 0:512], r2[rr][:, 0:512], ALU.add, [("r1", rr), ("r2", rr)], [(dtok, tb)])
                    if which == 0:
                        for (dd, XI, nm, xn) in ((qf, XIF, "qf", "XIF"), (qb, XIB, "qb", "XIB")):
                            TT("pool", dd[:, tsl].rearrange("p (j n) -> p j n", j=4), qT[:, tsl].rearrange("p (j n) -> p j n", j=4),
                               XI[:, h, :].unsqueeze(1).to_broadcast([128, 4, 128]), ALU.mult, [("qT", tb), (xn, h)], [(nm, tb)])

                for ui, (which, tb) in enumerate(units):
                    proj_unit(ui, which, tb)
                    if ui > 0:
                        rot_unit(ui - 1, *units[ui - 1])
                rot_unit(len(units) - 1, *units[-1])
                for tb in range(4):
                    pk = 4 + (tb % 2)
                    tsl = slice(tb * 512, (tb + 1) * 512)
                    for kt in range(8):
                        MM(pbf(pk), wg[:, kt, :], hT[:, kt, tsl], kt == 0, kt == 7, hT_all + ["wg"], [("pb", pk)])
                    ACT(gs[:, tsl], pbf(pk), AF.Silu, [("pb", pk)], [("gs", tb)])
                for g4 in range(4):
                    for jj in range(4):
                        j = g4 * 4 + jj
                        TR(pbb(6)[:, jj * 128:(jj + 1) * 128], kT[:, j * 128:(j + 1) * 128], [("kT", g4)], [("pb", 6)])
                    ACT(Kf[:, g4 * 4:(g4 + 1) * 4, :].rearrange("p j d -> p (j d)"), pbb(6)[:, 0:512], AF.Copy, [("pb", 6), ("ZF", h)], [("Kf", g4)], scale=ZF[:, h:h + 1])
                    ACT(Kb[:, g4 * 4:(g4 + 1) * 4, :].rearrange("p j d -> p (j d)"), pbb(6)[:, 0:512], AF.Copy, [("pb", 6), ("ZB", h)], [("Kb", g4)], scale=ZB[:, h:h + 1])
                ring = [6, 7, 2, 3]
                slot = 0
                for n_ in range(16):
                    for a in range(2):
                        j = n_ if a == 0 else 15 - n_
                        KK = Kf if a == 0 else Kb
                        ktok = "Kf" if a == 0 else "Kb"
                        bk = ring[slot % 4]
                        slot += 1
                        MM(pbf(bk)[:, 0:128], KK[:, j, :], V[:, j, hs], True, True, [(ktok, j // 4), ("V", j)], [("pb", bk)])
                        if n_ == 0:
                            CP("dve", R32[:, a, :], pbf(bk)[:, 0:128], [("pb", bk)], [("R32", a)])
                        else:
                            STT(R32[:, a, :], R32[:, a, :], G128[h], pbf(bk)[:, 0:128], ALU.mult, ALU.add, [("pb", bk), ("R32", a)], [("R32", a)])
                        if a == 0:
                            ACT(Rbf[:, a, j, :], R32[:, a, :], AF.Copy, [("R32", a)], [("Rbf", a, j)])
                        else:
                            CP("pool", Rbf[:, a, j, :], R32[:, a, :], [("R32", a)], [("Rbf", a, j)])

                def scores(j):
                    bk = 6 + (j % 2)
                    jsl = slice(j * 128, (j + 1) * 128)
                    MM(pbf(bk)[:, 0:128], kT[:, jsl], qT[:, jsl], True, True, [("kT", j // 4), ("qT", j // 4)], [("pb", bk)])

                def norm_tail(b4):
                    po = 4 + (b4 % 2)
                    pm = b4 % 2
                    k_ = b4 % 2
                    MM(pbf(pm), onesm, sq[k_][:, 0:512], True, True, [("sq", k_), "onesm"], [("pb", pm)])
                    ACT(sd[k_][:, 0:512], pbf(pm), AF.Ln, [("pb", pm), ("cst", CE)], [("sd", k_)], bias=cst[:, CE:CE + 1])
                    ACT(sd[k_][:, 0:512], sd[k_][:, 0:512], AF.Exp, [("sd", k_)], [("sd", k_)], scale=-0.5)
                    STT(on[k_][:, 0:512], pbf(po), gain[:, h:h + 1], sd[k_][:, 0:512], ALU.mult, ALU.mult, [("pb", po), ("sd", k_), "gain"], [("on", k_)])
                    TT("pool", mixT[:, h, b4 * 512:(b4 + 1) * 512], on[k_][:, 0:512], gs[:, b4 * 512:(b4 + 1) * 512], ALU.mult, [("on", k_), ("gs", b4)], [("mixT", h, b4)])

                scores(0)
                pend = None
                for j in range(16):
                    b4, jj = divmod(j, 4)
                    po = 4 + (b4 % 2)
                    sl = j % 4
                    bk = 6 + (j % 2)
                    jsl = slice(j * 128, (j + 1) * 128)
                    osl = slice(jj * 128, (jj + 1) * 128)
                    if j < 15:
                        scores(j + 1)
                    TT("dve", Sm[:, sl, :], pbf(bk)[:, 0:128], Dm[:, h, :], ALU.mult, [("pb", bk), ("Dm", h)], [("Sm", sl)])
                    nmm = 1 + (1 if j > 0 else 0) + (1 if j < 15 else 0)
                    MM(pbf(po)[:, osl], V[:, j, hs], Sm[:, sl, :], True, nmm == 1, [("V", j), ("Sm", sl)], [("pb", po)])
                    if j > 0:
                        MM(pbf(po)[:, osl], Rbf[:, 0, j - 1, :], qf[:, jsl], False, j == 15, [("Rbf", 0, j - 1), ("qf", b4)], [("pb", po)])
                    if j < 15:
                        MM(pbf(po)[:, osl], Rbf[:, 1, j + 1, :], qb[:, jsl], False, True, [("Rbf", 1, j + 1), ("qb", b4)], [("pb", po)])
                    if pend is not None and j == pend * 4 + 5:
                        norm_tail(pend)
                        pend = None
                    if jj == 3:
                        ACT(sq[b4 % 2][:, 0:512], pbf(po), AF.Square, [("pb", po)], [("sq", b4 % 2)])
                        if pend is not None:
                            norm_tail(pend)
                        pend = b4
                norm_tail(pend)
                if debug == "p1" and h == 0:
                    DEBUG["qT"] = (qT, [128, 2048], BF16, [("qT", t_) for t_ in range(4)])
                    DEBUG["kT"] = (kT, [128, 2048], BF16, [("kT", t_) for t_ in range(4)])
                    DEBUG["Kf"] = (Kf.rearrange("p j d -> p (j d)"), [128, 2048], BF16, [("Kf", t_) for t_ in range(4)])
                    DEBUG["Rbf"] = (Rbf.rearrange("p a j e -> p (a j e)"), [128, 4096], BF16, [("Rbf", a_, j_) for a_ in range(2) for j_ in range(16)])
                    DEBUG["gs"] = (gs, [128, 2048], BF16, [("gs", t_) for t_ in range(4)])

            if debug == "p1":
                DEBUG["ret"] = (mixT[:, 0:4, :].rearrange("p k t -> p (k t)"), [128, 4 * 2048], BF16, [("mixT", h, b) for h in range(4) for b in range(4)])
                DEBUG["hT"] = (hT.rearrange("p k t -> p (k t)"), [128, 8 * 2048], BF16, hT_all)
                DEBUG["V"] = (V.rearrange("p i c -> p (i c)"), [128, 16 * 512], BF16, [("V", i) for i in range(16)])
                break

            S.barrier()
            M.top = PBASE + 32768
            Sx = big[:, PBASE // 2:(PBASE + 32768) // 2].bitcast(F32).rearrange("p (r g j) -> p r g j", r=2, g=16)
            Ub = M.alloc(16384, BF16, "p (g j) -> p g j", g=32)
            WT = M.alloc(8192, BF16, "p (g m) -> p g m", g=32)
            WSm = M.alloc(16384, BF16, "p (r g m) -> p r g m", r=2, g=32)
            WFr = M.alloc(16384, BF16, "p (r g m) -> p r g m", r=2, g=32)
            yg = M.alloc(16384, BF16, "p (k t) -> p k t", k=4)
            YJ = [M.alloc(4096, BF16, "p (t c) -> p t c", t=8) for _ in range(2)]
            R1 = M.top
            wu = M.alloc(8192, BF16, "p (k n) -> p k n", k=8)
            UJ = M.alloc(8192, BF16, "p (g m) -> p g m", g=32)
            M.top = R1
            Fs = M.alloc(16384, BF16, "p (r g j) -> p r g j", r=2, g=16)
            sctA0, sctA1, sctB0, sctB1 = (M.alloc(2048, F32, "p (g j) -> p g j", g=32) for _ in range(4))
            gt = [sctA0.rearrange("p g j -> p (g j)"), sctB0.rearrange("p g j -> p (g j)")]

            DMA("wu", wu, wI_unit(24576, 512), wI_toks, ["wu"])
            DMA("lws", WSm.rearrange("p r g m -> p (r g m)"), s5Wsum, S5W, ["WSm"])
            DMA("lwt", WT.rearrange("p g m -> p (g m)"), s5Wtoep, S5W, ["WT"])
            DMA("lwf", WFr.rearrange("p r g m -> p (r g m)"), s5Wfar, S5W, ["WFr"])
            ev = 0
            for jb in range(2):
                for t in range(8):
                    pk = t % 2
                    base = jb * 1024 + t
                    for kt in range(8):
                        MM(pbf(pk), hT[:, kt, base:(jb + 1) * 1024:8], wu[:, kt, :], kt == 0, kt == 7, hT_all + ["wu"], [("pb", pk)])
                    if ev % 2 == 0:
                        ACT(UJ[:, :, t * 16:(t + 1) * 16], pbf(pk).rearrange("p (g c) -> p g c", g=32), AF.Copy, [("pb", pk)], ["UJ"])
                    else:
                        CP("dve", UJ[:, :, t * 16:(t + 1) * 16], pbf(pk).rearrange("p (g c) -> p g c", g=32), [("pb", pk)], ["UJ"])
                    ev += 1
                for g8 in range(4):
                    pk = 2 + g8 % 2
                    for gg in range(8):
                        TR(pbb(pk)[:, gg * 128:(gg + 1) * 128], UJ[:, g8 * 8 + gg, :], ["UJ"], [("pb", pk)])
                    if g8 % 2 == 0:
                        ACT(Ub[:, g8 * 8:(g8 + 1) * 8, jb * 128:(jb + 1) * 128], pbb(pk).rearrange("p (g j) -> p g j", g=8), AF.Copy, [("pb", pk)], [("Ub", g8)])
                    else:
                        CP("dve", Ub[:, g8 * 8:(g8 + 1) * 8, jb * 128:(jb + 1) * 128], pbb(pk).rearrange("p (g j) -> p g j", g=8), [("pb", pk)], [("Ub", g8)])

            H0, H1 = slice(0, 64), slice(64, 128)
            SXT = [("Sx", "dve"), ("Sx", "pool")]
            tmpA = [x.rearrange("p (r g) j -> p r g j", r=2) for x in (sctA0, sctA1)]
            tmpB = [x.rearrange("p (r g) j -> p r g j", r=2) for x in (sctB0, sctB1)]

            LANES = (("dve", 0, 10), ("pool", 10, 16))

            def upd(jd, js, p_, gh, n, ts):
                for (eng, g0, g1) in LANES:
                    ng = g1 - g0
                    gsl = slice(gh * 16 + g0, gh * 16 + g1)
                    lsl = slice(g0, g1)
                    dst = Sx[:, :, lsl, jd]
                    srcv = Sx[:, :, lsl, js]
                    swp = Sx[:, ::-1, lsl, js]
                    if n is None:
                        Ka = MPr[:, p_, gsl].unsqueeze(1).to_broadcast([128, 2, ng])
                        Kb = MPB[:, p_, :, gsl]
                        t0 = tmpA[ts][:, :, lsl, 0]
                        t1 = tmpB[ts][:, :, lsl, 0]
                    else:
                        Ka = MPr[:, p_, gsl].unsqueeze(1).unsqueeze(3).to_broadcast([128, 2, ng, n])
                        Kb = MPB[:, p_, :, gsl].unsqueeze(3).to_broadcast([128, 2, ng, n])
                        t0 = tmpA[ts][:, :, lsl, 0:n]
                        t1 = tmpB[ts][:, :, lsl, 0:n]
                    sx = [("Sx", eng)]
                    TT(eng, t0, srcv, Ka, ALU.mult, sx + MPT, [("tA", ts, eng)])
                    TT(eng, t1, swp, Kb, ALU.mult, sx + MPT, [("tB", ts, eng)])
                    TT(eng, dst, dst, t0, ALU.add, [("tA", ts, eng)], sx)
                    TT(eng, dst, dst, t1, ALU.add, [("tB", ts, eng)], sx)

            def s5_scan(gh):
                for gl in range(16):
                    g = gh * 16 + gl
                    pk = 4 + gl % 2
                    MM(pbf(pk)[:, 0:256], WSm[:, 0, g, :], Ub[:, g, :], True, True, [("Ub", g // 8), "WSm"], [("pb", pk)])
                    MM(pbf(pk)[:, 256:512], WSm[:, 1, g, :], Ub[:, g, :], True, True, [("Ub", g // 8), "WSm"], [("pb", pk)])
                    ACT(Sx[H0, :, gl, :], pbf(pk)[H0, :].rearrange("p (r j) -> p r j", r=2), AF.Copy, [("pb", pk)], SXT + hT_all)
                    CP("dve", Sx[H1, :, gl, ::-1], pbf(pk)[H1, :].rearrange("p (r j) -> p r j", r=2), [("pb", pk)], SXT + hT_all)
                for j1 in range(1, 16):
                    upd(slice(j1, 256, 16), slice(j1 - 1, 256, 16), 1, gh, 16, j1 % 2)
                for k_, p_ in enumerate((16, 17, 18, 19)):
                    sft = 1 << k_
                    upd(slice(16 * sft + 15, 256, 16), slice(15, 256 - 16 * sft, 16), p_, gh, 16 - sft, k_ % 2)
                for j1 in range(15):
                    upd(slice(16 + j1, 256, 16), slice(15, 240, 16), j1 + 1, gh, 15, j1 % 2)

            def s5_fs(gh):
                FST = ["Fs", "wu", "UJ"]
                S.op("pool", "memset", dict(ap=Fs[H0, :, :, 0:1], constant=0.0), (), FST)
                S.op("pool", "memset", dict(ap=Fs[H1, :, :, 255:256], constant=0.0), (), FST)
                ACT(Fs[H0, :, :, 1:256], Sx[H0, :, :, 0:255], AF.Copy, SXT, FST)
                CP("dve", Fs[H1, :, :, 0:255], Sx[H1, :, :, 254::-1], SXT, FST)

            def s5_out(gh):
                for jb in range(2):
                    jsl = slice(jb * 128, (jb + 1) * 128)
                    k_ = jb
                    for gl4 in range(4):
                        pk = 6 + gl4 % 2
                        for gg in range(4):
                            gl = gl4 * 4 + gg
                            g = gh * 16 + gl
                            osl = slice(gg * 128, (gg + 1) * 128)
                            MM(pbf(pk)[:, osl], Ub[:, g, jsl], WT[:, g, :], True, False, [("Ub", g // 8), "WT"], [("pb", pk)])
                            MM(pbf(pk)[:, osl], Fs[:, 0, gl, jsl], WFr[:, 0, g, :], False, False, ["Fs", "WFr"], [("pb", pk)])
                            MM(pbf(pk)[:, osl], Fs[:, 1, gl, jsl], WFr[:, 1, g, :], False, True, ["Fs", "WFr"], [("pb", pk)])
                        if gl4 % 2 == 0:
                            ACT(YJ[k_][:, :, gl4 * 64:(gl4 + 1) * 64].rearrange("p t (g c) -> p g t c", g=4),
                                pbf(pk).rearrange("p (g t c) -> p g t c", g=4, t=8), AF.Copy, [("pb", pk)], [("YJ", k_)])
                        else:
                            CP("dve", YJ[k_][:, :, gl4 * 64:(gl4 + 1) * 64].rearrange("p t (g c) -> p g t c", g=4),
                               pbf(pk).rearrange("p (g t c) -> p g t c", g=4, t=8), [("pb", pk)], [("YJ", k_)])
                    for cth in range(2):
                        ct = gh * 2 + cth
                        pk = 2 + cth
                        for t in range(8):
                            TR(pbb(pk)[:, t * 128:(t + 1) * 128], YJ[k_][:, t, cth * 128:(cth + 1) * 128], [("YJ", k_)], [("pb", pk)])
                        ACT(yg[:, ct, jb * 1024:(jb + 1) * 1024], pbb(pk), AF.Gelu_apprx_tanh, [("pb", pk)], [("yg", ct)])

            s5_scan(0)
            s5_fs(0)
            s5_scan(1)
            s5_out(0)
            s5_fs(1)
            s5_out(1)
            for ct in range(4):
                for tb in range(4):
                    pk = [0, 1, 4, 5][(ct * 4 + tb) % 4]
                    gi = (ct * 4 + tb) % 2
                    tsl = slice(tb * 512, (tb + 1) * 512)
                    for kt in range(4):
                        MM(pbf(pk), gluw[:, kt, ct * 128:(ct + 1) * 128], yg[:, kt, tsl], kt == 0, kt == 3, [("yg", k_) for k_ in range(4)] + ["gluw"], [("pb", pk)])
                    ACT(gt[gi][:, 0:512], pbf(pk), AF.Sigmoid, [("pb", pk), "glub"], [("gt", gi), ("tA", 0, "dve"), ("tB", 0, "dve"), ("tA", 0, "pool"), ("tB", 0, "pool")], bias=glub[:, ct:ct + 1])
                    jb_, t0_ = tb // 2, 4 * (tb % 2)
                    TT("pool", mixT[:, 4 + ct, jb_ * 1024:(jb_ + 1) * 1024].rearrange("p (j t) -> p t j", t=8)[:, t0_:t0_ + 4, :],
                       gt[gi][:, 0:512].rearrange("p (t j) -> p t j", t=4), yg[:, ct, tsl].rearrange("p (t j) -> p t j", t=4), ALU.mult,
                       [("gt", gi), ("yg", ct)], [("mixT", 4 + ct, jb_ * 2), ("mixT", 4 + ct, jb_ * 2 + 1)])
            if debug == "p2":
                DEBUG["yg"] = (yg.rearrange("p k t -> p (k t)"), [128, 4 * 2048], BF16, [("yg", c_) for c_ in range(4)])
                DEBUG["mix"] = (mixT.rearrange("p k t -> p (k t)"), [128, 8 * 2048], BF16, [("mixT", k_, b_) for k_ in range(8) for b_ in range(4)])
                DEBUG["Ub"] = (Ub.rearrange("p g j -> p (g j)"), [128, 32 * 256], BF16, [("Ub", g_) for g_ in range(4)])
                DEBUG["Sx"] = (Sx.rearrange("p r g j -> p (r g j)"), [128, 2 * 16 * 256], F32, SXT)
                break

            S.barrier()
            M.top = PBASE
            h2T = M.alloc(32768, BF16, "p (k t) -> p k t", k=8)
            x1 = M.alloc(65536, F32, "p (i c) -> p i c", i=16)
            P3 = M.top
            xt = [M.alloc(4096, F32) for _ in range(2)]
            hbs = [M.alloc(2048) for _ in range(2)]
            junk = M.alloc(2048)
            wo = M.alloc(16384, BF16, "p (k n) -> p k n", k=8)
            DMA("wo", wo, wO.rearrange("(k p) c -> p k c", p=128), wO_toks, ["wo"])
            mix_all = [("mixT", k_, b_) for k_ in range(8) for b_ in range(4)]
            pend_ev = None
            pend_tr = None
            ev_box = [None]

            def tr3a(i_, b_):
                pk_ = 4 + (i_ % 2)
                for kt in range(8):
                    TR(pbb(pk_)[:, kt * 128:(kt + 1) * 128], hbs[b_][:, kt * 128:(kt + 1) * 128], [("hb", b_)], [("pb", pk_)])
                if ev_box[0] is not None:
                    pi_, pp_ = ev_box[0]
                    ACT(h2T[:, :, pi_ * 128:(pi_ + 1) * 128], pbb(pp_).rearrange("p (k t) -> p k t", k=8), AF.Copy, [("pb", pp_)], [("h2T", pi_)])
                ev_box[0] = (i_, pk_)

            for i in range(16):
                b = i % 2
                DMA("xt%d" % b, xt[b], x_d[s, i * 128:(i + 1) * 128, :], (), [("xt", b)])
                for hf in range(2):
                    pk = (i % 2) * 2 + hf
                    hsl = slice(hf * 512, (hf + 1) * 512)
                    for kt in range(8):
                        MM(pbf(pk), mixT[:, kt, i * 128:(i + 1) * 128], wo[:, kt, hsl], kt == 0, kt == 7, mix_all + ["wo"], [("pb", pk)])
                    TT("dve", x1[:, i, hsl], pbf(pk), xt[b][:, hsl], ALU.add, [("pb", pk), ("xt", b)], [("x1", i, hf)])
                ACT(junk, x1[:, i, :], AF.Square, [("x1", i, 0), ("x1", i, 1)], ["junk", ("ss2", i)], accum_out=stat[:, 3, i:i + 1])
                ACT(stat[:, 4, i:i + 1], stat[:, 3, i:i + 1], AF.Sqrt, [("ss2", i), ("cst", CE)], [("sd2", i)], scale=1.0 / DM, bias=cst[:, CE:CE + 1])
                RCP(stat[:, 5, i:i + 1], stat[:, 4, i:i + 1], [("sd2", i)], [("rs2", i)])
                hb = hbs[b]
                TS("dve", hb, x1[:, i, :], stat[:, 5, i:i + 1], None, ALU.mult, None, [("x1", i, 0), ("x1", i, 1), ("rs2", i)], [("hb", b)])
                if pend_tr is not None:
                    tr3a(*pend_tr)
                pend_tr = (i, b)
            tr3a(*pend_tr)
            ACT(h2T[:, :, ev_box[0][0] * 128:(ev_box[0][0] + 1) * 128], pbb(ev_box[0][1]).rearrange("p (k t) -> p k t", k=8), AF.Copy, [("pb", ev_box[0][1])], [("h2T", ev_box[0][0])])
            h2T_all = [("h2T", i) for i in range(16)]
            if debug == "p3a":
                DEBUG["x1"] = (x1.rearrange("p i c -> p (i c)"), [128, 16 * 1024], F32, [("x1", i, hf) for i in range(16) for hf in range(2)])
                break

            S.barrier()
            M.top = P3
            aT = mixT_flat[:, 0:NCT * 512].rearrange("p (c t) -> p c t", c=NCT)
            accf = mixT_flat[:, NCT * 512:NCT * 512 + 4096].bitcast(F32)
            acc = [[accf[:, (sl * 2 + vg) * 512:(sl * 2 + vg + 1) * 512] for vg in range(2)] for sl in range(2)]
            NWU = 3
            wup = [M.alloc(4096, BF16, "p (k n) -> p k n", k=8) for _ in range(NWU)]
            wdn = M.alloc(NCT * 1024, BF16, "p (c n) -> p c n", c=NCT)
            dcnt = 0
            glb = [M.alloc(2048, F32) for _ in range(2)]
            yo = [M.alloc(4096, F32)]
            yo.append(yo[0])
            junk = glb[1].bitcast(BF16)
            oi = 0
            wuc = 0
            def ffn_tail(sl, c):
                ACT(glb[sl][:, 0:512], acc[sl][1], AF.Gelu_apprx_tanh, [("acc", sl, 1)], [("glb", sl)])
                TT("pool", aT[:, c, :], glb[sl][:, 0:512], acc[sl][0], ALU.mult, [("glb", sl), ("acc", sl, 0)], [("aT", c)])

            ffn_pend = None
            for q4 in range(4):
                t0 = q4 * 512
                tsl = slice(t0, t0 + 512)
                for c in range(NCT):
                    sl = c % 2
                    ws = wuc % NWU
                    wuc += 1
                    DMA("wup%d" % ws, wup[ws], wU[:, c * 2048:(c + 1) * 2048].rearrange("p (k n) -> p k n", k=8), wU_toks, [("wup", ws)])
                    hb_ = 6 + sl
                    for vg in range(2):
                        pk = sl * 2 + vg
                        wsl = slice(vg * 128, (vg + 1) * 128)
                        for kt in range(8):
                            MM(pbf(pk), wup[ws][:, kt, wsl], h2T[:, kt, tsl], kt == 0, kt == 7, h2T_all + [("wup", ws)], [("pb", pk)])
                    for vg in range(2):
                        wsl = slice(vg * 128, (vg + 1) * 128)
                        hoff = vg * 2
                        if 0 < q4 < 3:
                            for kt in range(8):
                                MM(pbf(hb_)[:, hoff:hoff + 2], wup[ws][:, kt, wsl], h2T[:, kt, t0 - 1:t0 + 513:513], kt == 0, kt == 7, h2T_all + [("wup", ws)], [("pb", hb_)])
                        elif q4 > 0:
                            for kt in range(8):
                                MM(pbf(hb_)[:, hoff:hoff + 1], wup[ws][:, kt, wsl], h2T[:, kt, t0 - 1:t0], kt == 0, kt == 7, h2T_all + [("wup", ws)], [("pb", hb_)])
                        else:
                            for kt in range(8):
                                MM(pbf(hb_)[:, hoff + 1:hoff + 2], wup[ws][:, kt, wsl], h2T[:, kt, t0 + 512:t0 + 513], kt == 0, kt == 7, h2T_all + [("wup", ws)], [("pb", hb_)])
                    for vg in range(2):
                        pk = sl * 2 + vg
                        hoff = vg * 2
                        ch = vg * NCT + c
                        A_ = acc[sl][vg]
                        atok = ("acc", sl, vg)
                        ACT(A_, pbf(pk), AF.Identity, [("pb", pk), "cw", "cb"], [atok], scale=cw[:, ch, 1:2], bias=cb[:, ch:ch + 1])
                        if q4 > 0:
                            ACT(A_[:, 0:1], pbf(hb_)[:, hoff:hoff + 1], AF.Identity, [("pb", hb_), atok, "cw"], [atok], scale=cw[:, ch, 0:1], bias=A_[:, 0:1])
                        if q4 < 3:
                            ACT(A_[:, 511:512], pbf(hb_)[:, hoff + 1:hoff + 2], AF.Identity, [("pb", hb_), atok, "cw"], [atok], scale=cw[:, ch, 2:3], bias=A_[:, 511:512])
                    for vg in range(2):
                        pk = sl * 2 + vg
                        ch = vg * NCT + c
                        A_ = acc[sl][vg]
                        atok = ("acc", sl, vg)
                        STT(A_[:, 1:512], pbf(pk)[:, 0:511], cw[:, ch, 0:1], A_[:, 1:512], ALU.mult, ALU.add, [("pb", pk), atok, "cw"], [atok])
                        STT(A_[:, 0:511], pbf(pk)[:, 1:512], cw[:, ch, 2:3], A_[:, 0:511], ALU.mult, ALU.add, [("pb", pk), atok, "cw"], [atok])
                    if ffn_pend is not None:
                        ffn_tail(*ffn_pend)
                    ffn_pend = (sl, c)
                ffn_tail(*ffn_pend)
                ffn_pend = None
                for hf in range(2):
                    hsl = slice(hf * 512, (hf + 1) * 512)
                    for c in range(NCT):
                        DMA("wdn%d" % c, wdn[:, c, :], wD[c * 128:(c + 1) * 128, hsl], wD_toks, [("wdn", c)])
                    for tt in range(4):
                        i = q4 * 4 + tt
                        pk = 4 + dcnt % 2
                        dcnt += 1
                        for c in range(NCT):
                            MM(pbf(pk), aT[:, c, tt * 128:(tt + 1) * 128], wdn[:, c, :], c == 0, c == NCT - 1, [("aT", c), ("wdn", c)], [("pb", pk)])
                        TT("dve", x1[:, i, hsl], pbf(pk), x1[:, i, hsl], ALU.add, [("pb", pk), ("x1", i, hf)], [("x1", i, hf)])
                        if hf == 1:
                            o_ = oi % 2
                            oi += 1
                            ACT(junk, x1[:, i, :], AF.Square, [("x1", i, 0), ("x1", i, 1)], ["junk", "ss3", ("glb", 1)], accum_out=stat3[:, 0:1])
                            ACT(stat3[:, 1:2], stat3[:, 0:1], AF.Sqrt, ["ss3", ("cst", CE)], ["sd3"], scale=1.0 / DM, bias=cst[:, CE:CE + 1])
                            RCP(stat3[:, 2:3], stat3[:, 1:2], ["sd3"], ["rs3"])
                            STT(yo[o_], x1[:, i, :], stat3[:, 2:3], nfbc, ALU.mult, ALU.mult, [("x1", i, 0), ("x1", i, 1), "rs3", "nfbc"], [("yo", 0)])
                            DMA("yo0", y_d[s, i * 128:(i + 1) * 128, :], yo[o_], [("yo", 0)], [("y", s, i)])

        fin = []
        if debug == "p3b":
            fin = [("y", 0, i) for i in range(16)]
        elif debug:
            for nm, (ap, shape, dt, toks) in DEBUG.items():
                dd = nc.dram_tensor("dbg_" + nm, list(shape), dt, kind="ExternalOutput").ap()
                dbg_out[nm] = (shape, dt)
                DMA("dbg_" + nm, dd, ap, toks, [("dbg", nm)])
                fin.append(("dbg", nm))
        else:
            fin = [("y", s, i) for s in range(NSEQ) for i in range(16)]
        NOP("sp", fin)
        S.emit()
        print("SBUF high water", M.hw, "instr counts", {e: len(v) for e, v in S.streams.items()})
    return nc, dbg_out


def _prep_weights(inp):
    f = np.float32
    w_in = np.asarray(inp["w_in"], f)[0]
    q, k, v, g, u = (w_in[:, i * 512:(i + 1) * 512] for i in range(5))

    def swap(w):
        w4 = w.reshape(DM, 4, 2, 64)
        return np.ascontiguousarray(w4[:, :, ::-1, :]).reshape(DM, 512)

    def inter(w):
        a = w.reshape(DM, 4, 1, 128)
        b = swap(w).reshape(DM, 4, 1, 128)
        return np.concatenate([a, b], axis=2).reshape(DM, 1024)

    w_in_a = np.ascontiguousarray(np.concatenate([inter(q), inter(k), g, v, u], axis=1))

    def col(vec, n):
        return np.ascontiguousarray(np.asarray(vec, f).reshape(n, 128).T)

    lre = np.asarray(inp["s5_lambda_re"], f)[0]
    lim = np.asarray(inp["s5_lambda_im"], f)[0]
    ldt = np.asarray(inp["s5_log_dt"], f)[0]

    def st2(a):
        return np.ascontiguousarray(a.transpose(0, 2, 1).reshape(128, 32))

    ldt_e = np.ascontiguousarray(np.broadcast_to(ldt[:, :, None], (2, 32, 64)))
    Bre = np.asarray(inp["s5_B_re"], f)[0]
    Bim = np.asarray(inp["s5_B_im"], f)[0]
    Cre = np.asarray(inp["s5_C_re"], f)[0]
    Cim = np.asarray(inp["s5_C_im"], f)[0]

    def b2(B):
        return np.ascontiguousarray(B.transpose(0, 2, 1, 3).reshape(128, 512))

    def c2(C):
        return np.ascontiguousarray(C.transpose(0, 3, 1, 2).reshape(128, 512))

    dsk = np.asarray(inp["s5_D"], f)[0]
    drep = np.ascontiguousarray(np.tile(dsk.reshape(32, 16).T, (8, 1)))

    conv_w = np.asarray(inp["conv_w"], f)[0]
    cwc = np.ascontiguousarray(conv_w.reshape(3, 44, 128).transpose(2, 1, 0).reshape(128, 132))
    d = {
        "w_in_a": w_in_a,
        "w_out": np.ascontiguousarray(np.asarray(inp["w_out"], f)[0]),
        "w_up": np.ascontiguousarray(np.asarray(inp["w_up"], f)[0]),
        "w_down": np.ascontiguousarray(np.asarray(inp["w_down"], f)[0]),
        "glu_w": np.ascontiguousarray(np.asarray(inp["s5_glu_w"], f)[0]),
        "nmix_c": col(np.asarray(inp["norm_mix"])[0], 8),
        "nffn_c": col(np.asarray(inp["norm_ffn"])[0], 8),
        "nfin_r": np.ascontiguousarray(np.asarray(inp["norm_final"], f).reshape(1, DM)),
        "gain_c": col(np.asarray(inp["ret_gn_gain"])[0], 4),
        "glub_c": col(np.asarray(inp["s5_glu_b"])[0], 4),
        "cw_c": cwc,
        "cb_c": col(np.asarray(inp["conv_b"])[0], 44),
        "lre2": st2(lre), "lim2": st2(lim), "ldt2": st2(ldt_e),
        "bre2": b2(Bre), "bim2": b2(Bim), "cre2": c2(Cre), "cim2": c2(Cim),
        "drep": drep,
    }
    return d


_CACHE = {}


def kernel(**inputs):
    xp = np.asarray(inputs["x_prompt"], np.float32)
    xs = np.asarray(inputs["x_sample"], np.float32)
    xall = np.concatenate([xp, xs], axis=0)
    wd = _prep_weights(inputs)
    if "nc" not in _CACHE:
        _CACHE["nc"] = build_program(False)[0]
    nc = _CACHE["nc"]
    in_maps = []
    for c in range(NCORES):
        m = dict(wd)
        m["x"] = np.ascontiguousarray(xall[c * NSEQ:(c + 1) * NSEQ])
        in_maps.append(m)
    res = run_bass_kernel_spmd(nc, in_maps, core_ids=list(range(NCORES)))
    yall = np.concatenate([np.asarray(r["y"], np.float32) for r in res.results], axis=0)
    return (np.ascontiguousarray(yall[0:8]), np.ascontiguousarray(yall[8:24]))
```

```python
import math
import contextlib
import numpy as np
import concourse.bass as bass
import concourse.mybir as mybir
from concourse.alu_op_type import AluOpType as ALU
from concourse.bass_utils import run_bass_kernel_spmd

F32 = mybir.dt.float32
BF16 = mybir.dt.bfloat16
AF = mybir.ActivationFunctionType

EPS = 1e-6
L = 2048
DM = 1024
NSEQ = 3
NCORES = 8
DFF = 2816
NCT = 22
LN_G = [math.log(1.0 - 2.0 ** (-5.0 - h)) for h in range(4)]
G128 = [math.exp(128.0 * lg) for lg in LN_G]
SC = 128.0 ** -0.5
TWO_PI = 2.0 * math.pi
MAGIC = 12582912.0
SIN_SCALE = 6.283185
WIN_COLS = 3584


class Sched:
    ENG = ("pe", "act", "dve", "pool", "sp")

    def __init__(self, nc):
        self.nc = nc
        self.streams = {e: [] for e in self.ENG}
        self.cnt = {}
        self.lastw = {}
        self.readers = {}
        self.waited = {e: {} for e in self.ENG}
        self.last_marker = {}
        self.pending = {e: [] for e in self.ENG}
        self.LIM = 32000

    def barrier(self):
        ms = list(self.last_marker.values())
        for e in self.ENG:
            self.pending[e] = list(ms)

    def op(self, eng, meth, kwargs, reads=(), writes=(), chan=None):
        base = chan if chan is not None else eng
        step = 16 if chan is not None else 1
        c = self.cnt.get(base, 0)
        epoch = c // self.LIM
        newc = c + step
        if newc > (epoch + 1) * self.LIM:
            epoch += 1
            c = epoch * self.LIM
            newc = c + step
        self.cnt[base] = newc
        sk = (base, epoch)
        marker = (sk, newc - epoch * self.LIM, eng if chan is None else None)
        deps = list(self.pending[eng])
        self.pending[eng] = []
        for t in reads:
            if t in self.lastw:
                deps.append(self.lastw[t])
        for t in writes:
            if t in self.lastw:
                deps.append(self.lastw[t])
            deps.extend(self.readers.get(t, ()))
        wd = self.waited[eng]
        m = {}
        for (k, v, de) in deps:
            if de == "pe" and eng == "pe" and chan is None:
                continue
            if wd.get(k, 0) >= v:
                continue
            m[k] = max(m.get(k, 0), v)
        for k, v in m.items():
            wd[k] = v
        self.streams[eng].append((meth, kwargs, list(m.items()), sk, step))
        for t in writes:
            self.lastw[t] = marker
            self.readers[t] = []
        for t in reads:
            self.readers.setdefault(t, []).append(marker)
        self.last_marker[base] = marker
        return marker

    def emit(self):
        nc = self.nc
        semkeys = set()
        for e in self.ENG:
            for (meth, kwargs, waits, sk, step) in self.streams[e]:
                semkeys.add(sk)
                for (k, v) in waits:
                    semkeys.add(k)
        semkeys = sorted(semkeys, key=str)
        with contextlib.ExitStack() as st:
            sems = {}
            for i, k in enumerate(semkeys):
                sems[k] = st.enter_context(nc.semaphore("s%d" % i))
            block = st.enter_context(nc.Block())
            engmap = {"pe": "tensor", "act": "scalar", "dve": "vector", "pool": "gpsimd", "sp": "sync"}

            def mk(e):
                def body(eng):
                    for (meth, kwargs, waits, sk, step) in self.streams[e]:
                        for (k, v) in waits:
                            eng.wait_ge(sems[k], v)
                        inst = getattr(eng, meth)(**kwargs)
                        inst.then_inc(sems[sk], step)
                return body

            for e in self.ENG:
                getattr(block, engmap[e])(mk(e))


class Mem:
    def __init__(self, big, cap):
        self.big = big
        self.cap = cap
        self.top = 0
        self.hw = 0

    def alloc(self, nbytes, dt=BF16, pattern=None, **kw):
        nbytes = (nbytes + 63) // 64 * 64
        off = self.top
        self.top += nbytes
        self.hw = max(self.hw, self.top)
        assert self.top <= self.cap, ("SBUF overflow", self.top, self.cap)
        ap = self.big[:, off // 2:(off + nbytes) // 2]
        if dt is F32:
            ap = ap.bitcast(F32)
        if pattern is not None:
            ap = ap.rearrange(pattern, **kw)
        return ap


def build_program(debug=None):
    nc = bass.Bass("TRN2", target_bir_lowering=False)
    DEBUG = {}

    def din(name, shape, dt=F32):
        return nc.dram_tensor(name, list(shape), dt, kind="ExternalInput").ap()

    x_d = din("x", [NSEQ, L, DM])
    y_d = nc.dram_tensor("y", [NSEQ, L, DM], F32, kind="ExternalOutput").ap()
    w_in_d = din("w_in_a", [DM, WIN_COLS])
    w_out_d = din("w_out", [DM, DM])
    w_up_d = din("w_up", [DM, 2 * DFF])
    w_down_d = din("w_down", [DFF, DM])
    glu_w_d = din("glu_w", [512, 512])
    nmix_d = din("nmix_c", [128, 8])
    nffn_d = din("nffn_c", [128, 8])
    nfin_d = din("nfin_r", [1, DM])
    gain_d = din("gain_c", [128, 4])
    glub_d = din("glub_c", [128, 4])
    cw_d = din("cw_c", [128, 44 * 3])
    cb_d = din("cb_c", [128, 44])
    lre2_d = din("lre2", [128, 32])
    lim2_d = din("lim2", [128, 32])
    ldt2_d = din("ldt2", [128, 32])
    bre2_d = din("bre2", [128, 512])
    bim2_d = din("bim2", [128, 512])
    cre2_d = din("cre2", [128, 512])
    cim2_d = din("cim2", [128, 512])
    drep_d = din("drep", [128, 32])

    wI = nc.dram_tensor("wI_s", [128, 8 * WIN_COLS], BF16).ap()
    wO = nc.dram_tensor("wO_s", [DM, DM], BF16).ap()
    wU = nc.dram_tensor("wU_s", [128, NCT * 2048], BF16).ap()
    wD = nc.dram_tensor("wD_s", [DFF, DM], BF16).ap()
    s5Wtoep = nc.dram_tensor("s5Wtoep_s", [128, 4096], BF16).ap()
    s5Wsum = nc.dram_tensor("s5Wsum_s", [128, 8192], BF16).ap()
    s5Wfar = nc.dram_tensor("s5Wfar_s", [128, 8192], BF16).ap()

    dbg_out = {}
    CAP = 207 * 1024
    with contextlib.ExitStack() as st:
        big = st.enter_context(nc.sbuf_tensor("big", [128, CAP // 2], BF16))
        pbs = [st.enter_context(nc.psum_tensor("pb%d" % i, [128, 512], F32)) for i in range(8)]
        S = Sched(nc)
        M = Mem(big, CAP)

        def pbf(i):
            return pbs[i][:]

        def pbb(i):
            return pbs[i][:].bitcast(BF16)

        def DMA(chan, out, in_, reads=(), writes=(), eng="sp"):
            return S.op(eng, "dma_start", dict(out=out, in_=in_), reads, writes, chan=chan)

        def ACT(out, in_, func, reads, writes, **kw):
            return S.op("act", "activation", dict(out=out, in_=in_, func=func, **kw), reads, writes)

        def STT(out, in0, scalar, in1, op0, op1, reads, writes):
            return S.op("dve", "scalar_tensor_tensor", dict(out=out, in0=in0, scalar=scalar, in1=in1, op0=op0, op1=op1), reads, writes)

        def TT(eng, out, in0, in1, op, reads, writes):
            return S.op(eng, "tensor_tensor", dict(out=out, in0=in0, in1=in1, op=op), reads, writes)

        def TS(eng, out, in0, s1, s2, op0, op1, reads, writes):
            if s2 is None:
                return S.op(eng, "tensor_scalar", dict(out=out, in0=in0, scalar1=s1, scalar2=None, op0=op0), reads, writes)
            return S.op(eng, "tensor_scalar", dict(out=out, in0=in0, scalar1=s1, scalar2=s2, op0=op0, op1=op1), reads, writes)

        def TSS(eng, out, in_, scalar, op, reads, writes):
            return S.op(eng, "tensor_single_scalar", dict(out=out, in_=in_, scalar=scalar, op=op), reads, writes)

        def CP(eng, out, in_, reads, writes):
            return S.op(eng, "tensor_copy", dict(out=out, in_=in_), reads, writes)

        def MM(out, lhsT, rhs, start, stop, reads, writes):
            return S.op("pe", "matmul", dict(out=out, lhsT=lhsT, rhs=rhs, start=start, stop=stop), reads, writes)

        def TR(out, in_, reads, writes):
            return S.op("pe", "transpose", dict(out=out, in_=in_, identity=ident), list(reads) + ["ident"], writes)

        def RCP(out, in_, reads, writes):
            return S.op("dve", "reciprocal", dict(out=out, in_=in_), reads, writes)

        def NOP(eng, reads, writes=()):
            return S.op(eng, "nop", dict(), reads, writes)

        ident = M.alloc(256)
        onesm = M.alloc(256)
        Pm = M.alloc(256)
        COS = M.alloc(4096)
        SINS = M.alloc(4096)
        Dm = M.alloc(2048, F32, "p (h n) -> p h n", h=4)
        XIF = M.alloc(2048, F32, "p (h n) -> p h n", h=4)
        XIB = M.alloc(2048, F32, "p (h n) -> p h n", h=4)
        ZF = M.alloc(64, F32)
        ZB = M.alloc(64, F32)
        cst = M.alloc(128, F32)
        gain = M.alloc(64, F32)
        glub = M.alloc(64, F32)
        gluw = M.alloc(4096, BF16, "p (k n) -> p k n", k=4)
        cw = M.alloc(44 * 3 * 4, F32)[:, 0:132].rearrange("p (c k) -> p c k", k=3)
        cb = M.alloc(44 * 4, F32)
        nfbc = M.alloc(4096, F32)
        MPr = M.alloc(20 * 128, F32, "p (q g) -> p q g", q=20)
        MPi = M.alloc(20 * 128, F32, "p (q g) -> p q g", q=20)
        MPB = M.alloc(20 * 256, F32, "p (q r g) -> p q r g", q=20, r=2)
        mixT_flat = M.alloc(32768, BF16)
        mixT = mixT_flat.rearrange("p (k t) -> p k t", k=8)
        stat = M.alloc(6 * 16 * 4, F32, "p (a i) -> p a i", a=6)
        stat3 = M.alloc(64, F32)
        PBASE = M.top

        CE, CLSC = 0, 1
        cvals = {CE: EPS, CLSC: math.log(SC)}
        for h in range(4):
            cvals[4 + h] = LN_G[h]
            cvals[8 + h] = 128.0 * LN_G[h]
            cvals[12 + h] = 127.0 * LN_G[h] + math.log(SC)
        for k, v in cvals.items():
            S.op("pool", "memset", dict(ap=cst[:, k:k + 1], constant=float(v)), (), [("cst", k)])

        M.top = PBASE
        identf = M.alloc(512, F32)
        pidx = M.alloc(64, F32)
        hi = M.alloc(64, F32)
        imod = M.alloc(64, F32)
        invf = M.alloc(64, F32)
        sgn = M.alloc(64, F32)
        nd = M.alloc(512, F32)
        nidx = M.alloc(512, F32)
        SMALL_END = M.top
        lidx = M.alloc(8192, F32)
        ta = M.alloc(8192, F32)
        tb_ = M.alloc(8192, F32)
        tc_ = M.alloc(8192, F32)

        S.op("pool", "memset", dict(ap=identf, constant=0.0), (), ["identf"])
        S.op("pool", "affine_select", dict(out=identf, in_=identf, pattern=[[-1, 128]], compare_op=ALU.not_equal,
                                           fill=1.0, base=0, channel_multiplier=1), ["identf"], ["identf"])
        CP("dve", ident, identf, ["identf"], ["ident"])
        Pmf = M.alloc(512, F32)
        S.op("pool", "memset", dict(ap=Pmf, constant=0.0), (), ["Pmf"])
        S.op("pool", "affine_select", dict(out=Pmf, in_=Pmf, pattern=[[-1, 128]], compare_op=ALU.not_equal,
                                           fill=1.0, base=-64, channel_multiplier=1), ["Pmf"], ["Pmf"])
        S.op("pool", "affine_select", dict(out=Pmf, in_=Pmf, pattern=[[-1, 128]], compare_op=ALU.not_equal,
                                           fill=1.0, base=64, channel_multiplier=1), ["Pmf"], ["Pmf"])
        CP("dve", Pm, Pmf, ["Pmf"], ["Pm"])
        S.op("pool", "memset", dict(ap=onesm, constant=1.0 / 128.0), (), ["onesm"])
        S.op("pool", "iota", dict(out=pidx[:, 0:1], pattern=[[0, 1]], base=0, channel_multiplier=1, allow_small_or_imprecise_dtypes=True), (), ["pidx"])
        S.op("pool", "iota", dict(out=lidx, pattern=[[1, 2048]], base=0, channel_multiplier=0, allow_small_or_imprecise_dtypes=True), (), ["lidx"])
        S.op("pool", "iota", dict(out=nd, pattern=[[1, 128]], base=0, channel_multiplier=-1, allow_small_or_imprecise_dtypes=True), (), ["nd"])
        S.op("pool", "iota", dict(out=nidx, pattern=[[1, 128]], base=0, channel_multiplier=0, allow_small_or_imprecise_dtypes=True), (), ["nidx"])
        TSS("dve", hi[:, 0:1], pidx[:, 0:1], 64.0, ALU.is_ge, ["pidx"], ["hi"])
        STT(imod[:, 0:1], hi[:, 0:1], -64.0, pidx[:, 0:1], ALU.mult, ALU.add, ["hi", "pidx"], ["imod"])
        ACT(invf[:, 0:1], imod[:, 0:1], AF.Exp, ["imod"], ["invf"], scale=-math.log(10000.0) / 64.0)
        TS("dve", sgn[:, 0:1], hi[:, 0:1], 2.0, -1.0, ALU.mult, ALU.add, ["hi"], ["sgn"])

        def sincos(a_ap, tA, tB, out_ap, tok_a, tok_A, tok_B, tok_out, quarter, post_scalar=None, deps=()):
            if quarter != 0.0:
                TSS("dve", tA, a_ap, quarter, ALU.add, [tok_a], [tok_A])
                src, stok = tA, tok_A
            else:
                src, stok = a_ap, tok_a
            TSS("dve", tB, src, MAGIC, ALU.add, [stok], [tok_B])
            TSS("dve", tB, tB, MAGIC, ALU.subtract, [tok_B], [tok_B])
            TT("dve", tA, src, tB, ALU.subtract, [stok, tok_B], [tok_A])
            ACT(tA, tA, AF.Sin, [tok_A], [tok_A], scale=SIN_SCALE)
            if post_scalar is None:
                CP("dve", out_ap, tA, [tok_A] + list(deps), [tok_out])
            else:
                TS("dve", out_ap, tA, post_scalar, None, ALU.mult, None, [tok_A] + list(deps), [tok_out])

        TS("dve", ta, lidx, invf[:, 0:1], 1.0 / TWO_PI, ALU.mult, ALU.mult, ["lidx", "invf"], ["ta"])
        sincos(ta, tb_, tc_, SINS, "ta", "tb", "tc", "SINS", 0.0, post_scalar=sgn[:, 0:1], deps=["sgn"])
        sincos(ta, tb_, tc_, COS, "ta", "tb", "tc", "COS", 0.25)
        ACT(nd, nd, AF.Abs, ["nd"], ["nd"])
        for h in range(4):
            ACT(Dm[:, h, :], nd, AF.Exp, ["nd", ("cst", CLSC)], [("Dm", h)], scale=LN_G[h], bias=cst[:, CLSC:CLSC + 1])
            ACT(XIF[:, h, :], nidx, AF.Exp, ["nidx", ("cst", 4 + h)], [("XIF", h)], scale=LN_G[h], bias=cst[:, 4 + h:5 + h])
            ACT(XIB[:, h, :], nidx, AF.Exp, ["nidx", ("cst", 8 + h)], [("XIB", h)], scale=-LN_G[h], bias=cst[:, 8 + h:9 + h])
            ACT(ZF[:, h:h + 1], pidx[:, 0:1], AF.Exp, ["pidx", ("cst", 12 + h)], [("ZF", h)], scale=-LN_G[h], bias=cst[:, 12 + h:13 + h])
            ACT(ZB[:, h:h + 1], pidx[:, 0:1], AF.Exp, ["pidx", ("cst", CLSC)], [("ZB", h)], scale=LN_G[h], bias=cst[:, CLSC:CLSC + 1])

        DMA("ld0", gain[:, 0:4], gain_d, (), ["gain"])
        DMA("ld1", glub[:, 0:4], glub_d, (), ["glub"])
        DMA("ld3", cw.rearrange("p c k -> p (c k)"), cw_d, (), ["cw"])
        DMA("ld4", cb[:, 0:44], cb_d, (), ["cb"])
        DMA("ld5", nfbc, nfin_d.partition_broadcast(128), (), ["nfbc"])
        DMA("ldc0", gluw, glu_w_d.rearrange("(k p) n -> p k n", p=128), (), ["gluw"], eng="pool")
        S.barrier()
        SBASE = M.top
        M.top = SMALL_END
        sm = {}
        for nm in ["lr", "li", "dt", "t0", "t1", "t2", "AR", "AI", "CR", "CI", "IR", "II", "u0", "u1", "u2", "u3"]:
            sm[nm] = M.alloc(128, F32)
        POSr = M.alloc(9 * 128, F32, "p (q g) -> p q g", q=9)
        POSi = M.alloc(9 * 128, F32, "p (q g) -> p q g", q=9)
        NEGr = M.alloc(8 * 128, F32, "p (q g) -> p q g", q=8)
        NEGi = M.alloc(8 * 128, F32, "p (q g) -> p q g", q=8)
        Et = {}
        for nm in ["q", "k", "s", "w"]:
            Et[nm] = (M.alloc(1024, F32, "p (g t) -> p g t", g=32), M.alloc(1024, F32, "p (g t) -> p g t", g=32))
        Braw_r = M.alloc(2048, F32, "p (g c) -> p g c", g=32)
        Braw_i = M.alloc(2048, F32, "p (g c) -> p g c", g=32)
        BBr = M.alloc(2048, F32, "p (g c) -> p g c", g=32)
        BBi = M.alloc(2048, F32, "p (g c) -> p g c", g=32)
        Cr_ = M.alloc(2048, F32, "p (g c) -> p g c", g=32)
        Ci_ = M.alloc(2048, F32, "p (g c) -> p g c", g=32)
        drep = M.alloc(128, F32)
        pq = M.alloc(64, F32)
        fq = M.alloc(512, F32)
        MF = M.alloc(512, F32)
        MB = M.alloc(512, F32)
        bigs = [M.alloc(16384, F32) for _ in range(5)]
        stg = [M.alloc(8192, BF16)]
        stg.append(stg[0])
        tp_f = [M.alloc(512, F32) for _ in range(2)]

        DMA("ld6", sm["lr"], lre2_d, (), ["v_lr"])
        DMA("ld7", sm["li"], lim2_d, (), ["v_li"])
        DMA("ld8", sm["dt"], ldt2_d, (), ["v_dt"])
        DMA("ld9", Braw_r.rearrange("p g c -> p (g c)"), bre2_d, (), ["Braw_r"])
        DMA("ld10", Braw_i.rearrange("p g c -> p (g c)"), bim2_d, (), ["Braw_i"])
        DMA("ld11", Cr_.rearrange("p g c -> p (g c)"), cre2_d, (), ["Cr"])
        DMA("ld12", Ci_.rearrange("p g c -> p (g c)"), cim2_d, (), ["Ci"])
        DMA("ld13", drep[:, 0:32], drep_d, (), ["drep"])

        def abar(lr, li, dt, t0, t1, t2, out_r, out_i, pfx):
            T = lambda n: pfx + n
            TSS("dve", lr, lr, -1e-4, ALU.min, [T("lr")], [T("lr")])
            ACT(dt, dt, AF.Exp, [T("dt")], [T("dt")])
            TT("dve", t0, lr, dt, ALU.mult, [T("lr"), T("dt")], [T("t0")])
            TS("dve", t0, t0, -8.0, 1.0 / 16.0, ALU.max, ALU.mult, [T("t0")], [T("t0")])
            TSS("dve", out_r, t0, 1.0 / math.factorial(8), ALU.mult, [T("t0")], [T("or")])
            for k in range(7, 0, -1):
                STT(out_r, out_r, 1.0 / math.factorial(k), t0, ALU.add, ALU.mult, [T("or"), T("t0")], [T("or")])
            TSS("dve", t0, out_r, 1.0, ALU.add, [T("or")], [T("t0")])
            for _ in range(4):
                TT("dve", t0, t0, t0, ALU.mult, [T("t0")], [T("t0")])
            STT(t1, li, 1.0 / TWO_PI, dt, ALU.mult, ALU.mult, [T("li"), T("dt")], [T("t1")])
            TSS("dve", t2, t1, MAGIC, ALU.add, [T("t1")], [T("t2")])
            TSS("dve", t2, t2, MAGIC, ALU.subtract, [T("t2")], [T("t2")])
            TT("dve", t1, t1, t2, ALU.subtract, [T("t1"), T("t2")], [T("t1")])
            TSS("dve", t1, t1, math.pi, ALU.mult, [T("t1")], [T("t1")])
            TT("dve", t2, t1, t1, ALU.mult, [T("t1")], [T("t2")])
            TSS("dve", out_i, t2, 1.0 / math.factorial(13), ALU.mult, [T("t2")], [T("oi")])
            for k in range(5, 0, -1):
                STT(out_i, out_i, ((-1.0) ** k) / math.factorial(2 * k + 1), t2, ALU.add, ALU.mult, [T("oi"), T("t2")], [T("oi")])
            STT(out_i, out_i, 1.0, t1, ALU.add, ALU.mult, [T("oi"), T("t1")], [T("oi")])
            TSS("dve", out_r, t2, -1.0 / math.factorial(14), ALU.mult, [T("t2"), T("t0")], [T("or")])
            for k in range(6, 0, -1):
                STT(out_r, out_r, ((-1.0) ** k) / math.factorial(2 * k), t2, ALU.add, ALU.mult, [T("or"), T("t2")], [T("or")])
            TSS("dve", out_r, out_r, 1.0, ALU.add, [T("or")], [T("or")])
            TT("dve", t2, out_i, out_i, ALU.mult, [T("oi"), T("or")], [T("t2")])
            TS("dve", t2, t2, -2.0, 1.0, ALU.mult, ALU.add, [T("t2")], [T("t2")])
            STT(out_i, out_i, 2.0, out_r, ALU.mult, ALU.mult, [T("oi"), T("or")], [T("oi")])
            TT("dve", out_r, t2, t0, ALU.mult, [T("t2"), T("t0"), T("oi")], [T("or")])
            TT("dve", out_i, out_i, t0, ALU.mult, [T("oi"), T("t0")], [T("oi")])

        AR, AI = sm["AR"], sm["AI"]
        abar(sm["lr"], sm["li"], sm["dt"], sm["t0"], sm["t1"], sm["t2"], AR, AI, "v_")
        nmix = M.alloc(64, F32)
        nffn = M.alloc(64, F32)
        PCH = 1024
        NPS = 3
        wtmp = [M.alloc(PCH * 4, F32) for _ in range(NPS)]
        wob = [M.alloc(PCH * 2, BF16) for _ in range(NPS)]
        DMA("ld0", nmix[:, 0:8], nmix_d, (), ["nmix"])
        DMA("ld1", nffn[:, 0:8], nffn_d, (), ["nffn"])
        prep_i = [0]

        def prep(src_, r0, c0, ncols, scale_ap, scale_tok, outs):
            i = prep_i[0] % NPS
            prep_i[0] += 1
            DMA("wpi%d" % i, wtmp[i][:, 0:ncols], src_[r0:r0 + 128, c0:c0 + ncols], (), [("wtmp", i)])
            if scale_ap is None:
                ACT(wob[i][:, 0:ncols], wtmp[i][:, 0:ncols], AF.Copy, [("wtmp", i)], [("wob", i)])
            else:
                ACT(wob[i][:, 0:ncols], wtmp[i][:, 0:ncols], AF.Copy, [("wtmp", i), scale_tok], [("wob", i)], scale=scale_ap)
            for n_, (dst_ap, vf, tok) in enumerate(outs):
                DMA("wpo%d%s" % (i, "abc"[n_]), dst_ap, vf(wob[i]), [("wob", i)], [tok], eng="act")

        wI_toks, wO_toks, wU_toks, wD_toks = [], [], [], []
        wI_A = wI[:, 0:16384].rearrange("p (u k c) -> p u k c", u=8, k=8)
        wI_B = wI[:, 16384:20480].rearrange("p (u k c) -> p u k c", u=4, k=8)
        wI_C = wI[:, 20480:28672].rearrange("p (u k c) -> p u k c", u=2, k=8)
        for kt in range(8):
            sc_, st_ = nmix[:, kt:kt + 1], "nmix"
            prep(w_in_d, kt * 128, 0, 1024, sc_, st_,
                 [(wI_A[:, 0:4, kt, :], lambda w: w[:, 0:1024].rearrange("p (u c) -> p u c", u=4), ("wI", kt, 0))])
            prep(w_in_d, kt * 128, 1024, 1024, sc_, st_,
                 [(wI_A[:, 4:8, kt, :], lambda w: w[:, 0:1024].rearrange("p (u c) -> p u c", u=4), ("wI", kt, 1))])
            prep(w_in_d, kt * 128, 2048, 1024, sc_, st_,
                 [(wI_B[:, :, kt, :], lambda w: w[:, 0:512].rearrange("p (u c) -> p u c", u=4), ("wI", kt, 2)),
                  (wI_C[:, 0, kt, :], lambda w: w[:, 512:1024], ("wI", kt, 3))])
            prep(w_in_d, kt * 128, 3072, 512, sc_, st_,
                 [(wI_C[:, 1, kt, :], lambda w: w[:, 0:512], ("wI", kt, 4))])
            wI_toks += [("wI", kt, n_) for n_ in range(5)]
        for kt in range(8):
            prep(w_out_d, kt * 128, 0, DM, None, None, [(wO[kt * 128:(kt + 1) * 128, :], lambda w: w[:, 0:DM], ("wO", kt))])
            wO_toks.append(("wO", kt))
        wU_v = wU.rearrange("p (c k v n) -> p c k v n", c=NCT, k=8, v=2)
        for kt in range(8):
            for hf in range(2):
                for (ca, cb_) in ((0, 8), (8, 16), (16, 22)):
                    nc_ = cb_ - ca
                    prep(w_up_d, kt * 128, hf * DFF + ca * 128, nc_ * 128, nffn[:, kt:kt + 1], "nffn",
                         [(wU_v[:, ca:cb_, kt, hf, :], lambda w, nc_=nc_: w[:, 0:nc_ * 128].rearrange("p (c n) -> p c n", c=nc_), ("wU", kt, hf, ca))])
                    wU_toks.append(("wU", kt, hf, ca))
        for c in range(NCT):
            prep(w_down_d, c * 128, 0, DM, None, None, [(wD[c * 128:(c + 1) * 128, :], lambda w: w[:, 0:DM], ("wD", c))])
            wD_toks.append(("wD", c))

        ATOK = ["v_or", "v_oi"]
        lr, li = sm["lr"], sm["li"]
        u0, u1, u2, u3 = sm["u0"], sm["u1"], sm["u2"], sm["u3"]
        TT("dve", u0, lr, lr, ALU.mult, ["v_lr"] + ATOK, ["u0"])
        TT("dve", u1, li, li, ALU.mult, ["v_li"], ["u1"])
        TT("dve", u0, u0, u1, ALU.add, ["u0", "u1"], ["u0"])
        RCP(u0, u0, ["u0"], ["u0"])
        TSS("dve", u1, AR, -1.0, ALU.add, ATOK + ["u1"], ["u1"])
        TT("dve", u2, u1, lr, ALU.mult, ["u1", "v_lr"], ["u2"])
        TT("dve", u3, AI, li, ALU.mult, ATOK + ["v_li"], ["u3"])
        TT("dve", u2, u2, u3, ALU.add, ["u2", "u3"], ["u2"])
        TT("dve", sm["CR"], u2, u0, ALU.mult, ["u2", "u0"], ["CR"])
        TT("dve", u2, AI, lr, ALU.mult, ATOK + ["v_lr", "CR"], ["u2"])
        TT("dve", u3, u1, li, ALU.mult, ["u1", "v_li", "CR"], ["u3"])
        TT("dve", u2, u2, u3, ALU.subtract, ["u2", "u3"], ["u2"])
        TT("dve", sm["CI"], u2, u0, ALU.mult, ["u2", "u0"], ["CI"])
        TT("dve", u0, AR, AR, ALU.mult, ATOK + ["CI", "u0"], ["u0"])
        TT("dve", u1, AI, AI, ALU.mult, ATOK + ["CI", "u1"], ["u1"])
        TT("dve", u0, u0, u1, ALU.add, ["u0", "u1"], ["u0"])
        RCP(u0, u0, ["u0"], ["u0"])
        TT("dve", sm["IR"], AR, u0, ALU.mult, ATOK + ["u0"], ["IR"])
        STT(sm["II"], AI, -1.0, u0, ALU.mult, ALU.mult, ATOK + ["u0"], ["II"])

        def cmul(orr, oii, ar, ai, br, bi, toks_in, tok_out):
            TT("dve", u2, ar, br, ALU.mult, toks_in + ["u2"], ["u2"])
            TT("dve", u3, ai, bi, ALU.mult, toks_in + ["u3"], ["u3"])
            TT("dve", orr, u2, u3, ALU.subtract, ["u2", "u3"], [tok_out + "r"])
            TT("dve", u2, ar, bi, ALU.mult, toks_in + [tok_out + "r"], ["u2"])
            TT("dve", u3, ai, br, ALU.mult, toks_in + [tok_out + "r"], ["u3"])
            TT("dve", oii, u2, u3, ALU.add, ["u2", "u3"], [tok_out + "i"])

        S.op("dve", "memset", dict(ap=POSr[:, 0, :], constant=1.0), (), ["POS0r"])
        S.op("dve", "memset", dict(ap=POSi[:, 0, :], constant=0.0), (), ["POS0i"])
        S.op("dve", "memset", dict(ap=NEGr[:, 0, :], constant=1.0), (), ["NEG0r"])
        S.op("dve", "memset", dict(ap=NEGi[:, 0, :], constant=0.0), (), ["NEG0i"])
        CP("dve", POSr[:, 1, :], AR, ATOK, ["POS1r"])
        CP("dve", POSi[:, 1, :], AI, ATOK, ["POS1i"])
        CP("dve", NEGr[:, 1, :], sm["IR"], ["IR"], ["NEG1r"])
        CP("dve", NEGi[:, 1, :], sm["II"], ["II"], ["NEG1i"])
        for p in range(2, 9):
            cmul(POSr[:, p, :], POSi[:, p, :], POSr[:, p - 1, :], POSi[:, p - 1, :], POSr[:, 1, :], POSi[:, 1, :],
                 ["POS%dr" % (p - 1), "POS%di" % (p - 1), "POS1r", "POS1i"], "POS%d" % p)
        for p in range(2, 8):
            cmul(NEGr[:, p, :], NEGi[:, p, :], NEGr[:, p - 1, :], NEGi[:, p - 1, :], NEGr[:, 1, :], NEGi[:, 1, :],
                 ["NEG%dr" % (p - 1), "NEG%di" % (p - 1), "NEG1r", "NEG1i"], "NEG%d" % p)
        POST = [("POS%d" % p) + x for p in range(9) for x in "ri"]
        NEGT = [("NEG%d" % p) + x for p in range(8) for x in "ri"]
        CP("dve", MPr[:, 1, :], POSr[:, 8, :], POST, ["MP1r"])
        CP("dve", MPi[:, 1, :], POSi[:, 8, :], POST, ["MP1i"])
        for p in range(2, 17):
            cmul(MPr[:, p, :], MPi[:, p, :], MPr[:, p - 1, :], MPi[:, p - 1, :], MPr[:, 1, :], MPi[:, 1, :],
                 ["MP%dr" % (p - 1), "MP%di" % (p - 1), "MP1r", "MP1i"], "MP%d" % p)
        for p, q_ in ((17, 16), (18, 17), (19, 18)):
            cmul(MPr[:, p, :], MPi[:, p, :], MPr[:, q_, :], MPi[:, q_, :], MPr[:, q_, :], MPi[:, q_, :],
                 ["MP%dr" % q_, "MP%di" % q_], "MP%d" % p)
        MPT0 = [("MP%d" % p) + x for p in range(1, 20) for x in "ri"]
        TSS("dve", MPB[:, 1:20, 0, :], MPi[:, 1:20, :], -1.0, ALU.mult, MPT0, ["MPB0"])
        CP("dve", MPB[:, 1:20, 1, :], MPi[:, 1:20, :], MPT0, ["MPB1"])
        MPT = MPT0 + ["MPB0", "MPB1"]
        H0, H1 = slice(0, 64), slice(64, 128)
        for t in range(8):
            for (nm, src0, i0, src1, i1) in (("q", "POS", t, "NEG", t), ("k", "NEG", t, "POS", t), ("s", "POS", 7 - t, "POS", t), ("w", "POS", t + 1, "POS", 8 - t)):
                for ri in range(2):
                    tabs = {"POS": (POSr, POSi), "NEG": (NEGr, NEGi)}
                    CP("pool", Et[nm][ri][H0, :, t], tabs[src0][ri][H0, i0, :], POST + NEGT, [("E", nm, ri)])
                    CP("pool", Et[nm][ri][H1, :, t], tabs[src1][ri][H1, i1, :], POST + NEGT, [("E", nm, ri)])
        def bc_g(ap):
            return ap[:, 0:32].unsqueeze(2).to_broadcast([128, 32, 16])
        TT("dve", BBr, Braw_r, bc_g(sm["CR"]), ALU.mult, ["Braw_r", "CR"], ["BBr"])
        TT("dve", BBi, Braw_i, bc_g(sm["CI"]), ALU.mult, ["Braw_i", "CI"], ["BBi"])
        TT("dve", BBr, BBr, BBi, ALU.subtract, ["BBr", "BBi"], ["BBr"])
        TT("dve", BBi, Braw_i, bc_g(sm["CR"]), ALU.mult, ["Braw_i", "CR", "BBr"], ["BBi"])
        TT("dve", Braw_r, Braw_r, bc_g(sm["CI"]), ALU.mult, ["Braw_r", "CI", "BBr"], ["Braw_r"])
        TT("dve", BBi, BBi, Braw_r, ALU.add, ["BBi", "Braw_r"], ["BBi"])

        def v4(ap):
            return ap.rearrange("p (g t c) -> p g t c", g=32, t=8)

        def bX(ap):
            return ap.unsqueeze(2).to_broadcast([128, 32, 8, 16])

        def bE(ap):
            return ap.unsqueeze(3).to_broadcast([128, 32, 8, 16])

        def BG(k):
            return ("big", k)

        def cprod(ko_r, ko_i, Xr, Xi, xtoks, nm, k1, neg_im=False):
            Er, Ei = Et[nm]
            et = [("E", nm, 0), ("E", nm, 1)]
            outr, outi, t1_ = bigs[ko_r], bigs[ko_i], bigs[k1]
            TT("dve", v4(outr), bX(Xr), bE(Er), ALU.mult, xtoks + et, [BG(ko_r)])
            TT("dve", v4(t1_), bX(Xi), bE(Ei), ALU.mult, xtoks + et, [BG(k1)])
            TT("dve", outr, outr, t1_, ALU.subtract, [BG(ko_r), BG(k1)], [BG(ko_r)])
            TT("dve", v4(outi), bX(Xr), bE(Ei), ALU.mult, xtoks + et, [BG(ko_i)])
            TT("dve", v4(t1_), bX(Xi), bE(Er), ALU.mult, xtoks + et, [BG(k1)])
            if neg_im:
                STT(outi, outi, -1.0, t1_, ALU.mult, ALU.subtract, [BG(ko_i), BG(k1)], [BG(ko_i)])
            else:
                TT("dve", outi, outi, t1_, ALU.add, [BG(ko_i), BG(k1)], [BG(ko_i)])

        def TRF(out, in_, reads, writes):
            return S.op("pe", "transpose", dict(out=out, in_=in_, identity=identf), list(reads) + ["identf"], writes)

        CT = ["Cr", "Ci"]
        BT = ["BBr", "BBi"]
        cprod(0, 1, Cr_, Ci_, CT, "w", 2, neg_im=True)
        CP("dve", stg[0], bigs[0], [BG(0)], [("stg", 0)])
        DMA("st0", s5Wfar[:, 0:4096], stg[0], [("stg", 0)], ["s5Wfar0"], eng="pool")
        CP("dve", stg[1], bigs[1], [BG(1)], [("stg", 0)])
        DMA("st0", s5Wfar[:, 4096:8192], stg[1], [("stg", 0)], ["s5Wfar1"], eng="pool")
        cprod(0, 1, BBr, BBi, BT, "s", 2)
        nb = 0
        for ri in range(2):
            stv = stg[ri].rearrange("p (g m) -> p g m", g=32)
            for g4 in range(8):
                bk = 2 + nb % 4
                nb += 1
                for gg in range(4):
                    g = g4 * 4 + gg
                    TRF(pbf(bk)[:, gg * 128:(gg + 1) * 128], bigs[ri][:, g * 128:(g + 1) * 128], [BG(ri)], [("pb", bk)])
                CP("dve", stv[:, g4 * 4:(g4 + 1) * 4, :], pbf(bk).rearrange("p (g m) -> p g m", g=4), [("pb", bk)], [("stg", 0)])
            DMA("st0", s5Wsum[:, ri * 4096:(ri + 1) * 4096], stg[ri], [("stg", 0)], ["s5Wsum%d" % ri], eng="pool")
        cprod(0, 1, Cr_, Ci_, CT, "q", 4, neg_im=True)
        cprod(2, 3, BBr, BBi, BT, "k", 4)
        TS("dve", pq[:, 0:1], pidx[:, 0:1], 1.0 / 16.0, -15.0 / 32.0, ALU.mult, ALU.add, ["pidx"], ["pq"])
        TSS("dve", pq[:, 0:1], pq[:, 0:1], MAGIC, ALU.add, ["pq"], ["pq"])
        TSS("dve", pq[:, 0:1], pq[:, 0:1], MAGIC, ALU.subtract, ["pq"], ["pq"])
        TS("dve", fq[:, 0:128], nidx[:, 0:128], 1.0 / 16.0, -15.0 / 32.0, ALU.mult, ALU.add, ["nidx"], ["fq"])
        TSS("dve", fq[:, 0:128], fq[:, 0:128], MAGIC, ALU.add, ["fq"], ["fq"])
        TSS("dve", fq[:, 0:128], fq[:, 0:128], MAGIC, ALU.subtract, ["fq"], ["fq"])
        TS("dve", MF[:, 0:128], fq[:, 0:128], pq[:, 0:1], None, ALU.is_ge, None, ["fq", "pq"], ["MF"])
        TS("dve", MB[:, 0:128], fq[:, 0:128], pq[:, 0:1], None, ALU.is_le, None, ["fq", "pq"], ["MB"])
        stgT = stg[0].rearrange("p (g m) -> p g m", g=32)
        QK = [BG(0), BG(1), BG(2), BG(3)]
        for g in range(32):
            gsl = slice(g * 128, (g + 1) * 128)
            bf_, bb_ = (0, 1) if g % 2 == 0 else (6, 7)
            k_ = g % 2
            MM(pbf(bf_)[:, 0:128], bigs[2][H0, gsl], bigs[0][H0, gsl], True, False, QK, [("pb", bf_)])
            MM(pbf(bf_)[:, 0:128], bigs[3][H0, gsl], bigs[1][H0, gsl], False, True, QK, [("pb", bf_)])
            MM(pbf(bb_)[:, 0:128], bigs[2][H1, gsl], bigs[0][H1, gsl], True, False, QK, [("pb", bb_)])
            MM(pbf(bb_)[:, 0:128], bigs[3][H1, gsl], bigs[1][H1, gsl], False, True, QK, [("pb", bb_)])
            TT("dve", tp_f[k_][:, 0:128], pbf(bf_)[:, 0:128], MF[:, 0:128], ALU.mult, [("pb", bf_), "MF"], [("tpf", k_)])
            TT("dve", bigs[4][:, gsl], pbf(bb_)[:, 0:128], MB[:, 0:128], ALU.mult, [("pb", bb_), "MB"], [BG(4)])
            TT("dve", tp_f[k_][:, 0:128], tp_f[k_][:, 0:128], bigs[4][:, gsl], ALU.add, [("tpf", k_), BG(4)], [("tpf", k_)])
            STT(stgT[:, g, :], identf, drep[:, g:g + 1], tp_f[k_][:, 0:128], ALU.mult, ALU.add, ["identf", "drep", ("tpf", k_)], [("stg", 0)])
        DMA("st0", s5Wtoep, stg[0], [("stg", 0)], ["s5Wtoep"], eng="pool")
        S5W = ["s5Wfar0", "s5Wfar1", "s5Wsum0", "s5Wsum1", "s5Wtoep"]

        if debug == "setup":
            S.barrier()
            M.top = PBASE
            d1_ = M.alloc(8192)
            d2_ = M.alloc(16384)
            d3_ = M.alloc(16384)
            DMA("lb", d1_, s5Wtoep, S5W, ["d1"])
            DMA("lc", d2_, s5Wsum, S5W, ["d2"])
            DMA("ld", d3_, s5Wfar, S5W, ["d3"])
            DEBUG["Wtoep"] = (d1_, [128, 4096], BF16, ["d1"])
            DEBUG["Wsum"] = (d2_, [128, 8192], BF16, ["d2"])
            DEBUG["Wfar"] = (d3_, [128, 8192], BF16, ["d3"])
            DEBUG["MPr"] = (MPr[:, 1:17, :].rearrange("p q g -> p (q g)"), [128, 16 * 32], F32, MPT)
            DEBUG["MPi"] = (MPi[:, 1:17, :].rearrange("p q g -> p (q g)"), [128, 16 * 32], F32, MPT)

        nseq_run = 0 if debug == "setup" else (1 if debug else NSEQ)
        for s in range(nseq_run):
            S.barrier()
            M.top = PBASE
            hT = M.alloc(32768, BF16, "p (k t) -> p k t", k=8)
            V = M.alloc(16384, BF16, "p (i c) -> p i c", i=16)
            qT = M.alloc(4096)
            kT = M.alloc(4096)
            qf = M.alloc(4096)
            qb = M.alloc(4096)
            Kf = M.alloc(4096, BF16, "p (j d) -> p j d", j=16)
            Kb = M.alloc(4096, BF16, "p (j d) -> p j d", j=16)
            Rbf = M.alloc(8192, BF16, "p (a j e) -> p a j e", a=2, j=16)
            R32 = M.alloc(1024, F32, "p (a e) -> p a e", a=2)
            gs = M.alloc(4096)
            xt = [M.alloc(4096, F32) for _ in range(2)]
            hbs = [M.alloc(2048) for _ in range(2)]
            junk = M.alloc(2048)
            wv = M.alloc(8192, BF16, "p (k n) -> p k n", k=8)
            wq = [M.alloc(4096, BF16, "p (k n) -> p k n", k=8) for _ in range(2)]
            wg = M.alloc(2048, BF16, "p (k n) -> p k n", k=8)
            qs = [M.alloc(1024) for _ in range(2)]
            r1 = [M.alloc(2048, F32) for _ in range(2)]
            r2 = [M.alloc(2048, F32) for _ in range(2)]
            Sm = M.alloc(1024, BF16, "p (j n) -> p j n", j=4)
            sq = [M.alloc(1024) for _ in range(2)]
            sd = [M.alloc(2048, F32) for _ in range(2)]
            on = [M.alloc(2048, F32) for _ in range(2)]
            def wI_unit(off, ncols):
                return wI[:, off:off + 8 * ncols].rearrange("p (k c) -> p k c", k=8)

            pend_ev = None
            for i in range(16):
                b = i % 2
                DMA("xt%d" % b, xt[b], x_d[s, i * 128:(i + 1) * 128, :], (), [("xt", b)])
                ACT(junk, xt[b], AF.Square, [("xt", b)], ["junk", ("ss", i)], accum_out=stat[:, 0, i:i + 1])
                ACT(stat[:, 1, i:i + 1], stat[:, 0, i:i + 1], AF.Sqrt, [("ss", i), ("cst", CE)], [("sd", i)], scale=1.0 / DM, bias=cst[:, CE:CE + 1])
                RCP(stat[:, 2, i:i + 1], stat[:, 1, i:i + 1], [("sd", i)], [("rs", i)])
                hb = hbs[b]
                TS("dve", hb, xt[b], stat[:, 2, i:i + 1], None, ALU.mult, None, [("xt", b), ("rs", i)], [("hb", b)])
                pk = i % 2
                for kt in range(8):
                    TR(pbb(pk)[:, kt * 128:(kt + 1) * 128], hb[:, kt * 128:(kt + 1) * 128], [("hb", b)], [("pb", pk)])
                if pend_ev is not None:
                    ACT(hT[:, :, pend_ev[0] * 128:(pend_ev[0] + 1) * 128], pbb(pend_ev[1]).rearrange("p (k t) -> p k t", k=8), AF.Copy, [("pb", pend_ev[1])], [("hT", pend_ev[0])])
                pend_ev = (i, pk)
            ACT(hT[:, :, pend_ev[0] * 128:(pend_ev[0] + 1) * 128], pbb(pend_ev[1]).rearrange("p (k t) -> p k t", k=8), AF.Copy, [("pb", pend_ev[1])], [("hT", pend_ev[0])])
            hT_all = [("hT", i) for i in range(16)]

            DMA("wv", wv, wI_unit(20480, 512), wI_toks, ["wv"])
            for i in range(16):
                pk = 2 + (i % 2)
                for kt in range(8):
                    MM(pbf(pk), hT[:, kt, i * 128:(i + 1) * 128], wv[:, kt, :], kt == 0, kt == 7, [("hT", i), "wv"], [("pb", pk)])
                ACT(V[:, i, :], pbf(pk), AF.Copy, [("pb", pk)], [("V", i)])

            for h in range(4):
                hs = slice(h * 128, (h + 1) * 128)
                DMA("wq0", wq[0], wI_unit(h * 2048, 256), wI_toks, [("wq", 0, 0), ("wq", 0, 1)])
                DMA("wq1", wq[1], wI_unit((4 + h) * 2048, 256), wI_toks, [("wq", 1, 0), ("wq", 1, 1)])
                DMA("wg", wg, wI_unit(16384 + h * 1024, 128), wI_toks, ["wg"])
                ui = 0
                for which in range(2):
                    dst = qT if which == 0 else kT
                    dtok = "qT" if which == 0 else "kT"
                    for tb in range(4):
                        pa = (ui % 2) * 2
                        pbk = pa + 1
                        rr = ui % 2
                        ui += 1
                        tsl = slice(tb * 512, (tb + 1) * 512)
                        for kt in range(8):
                            MM(pbf(pa), wq[which][:, kt, 0:128], hT[:, kt, tsl], kt == 0, kt == 7, hT_all + [("wq", which, 0)], [("pb", pa)])
                        for kt in range(8):
                            MM(pbf(pbk), wq[which][:, kt, 128:256], hT[:, kt, tsl], kt == 0, kt == 7, hT_all + [("wq", which, 1)], [("pb", pbk)])
                        TT("dve", r1[rr][:, 0:512], pbf(pa), COS[:, tsl], ALU.mult, [("pb", pa), "COS"], [("r1", rr)])
                        TT("dve", r2[rr][:, 0:512], pbf(pbk), SINS[:, tsl], ALU.mult, [("pb", pbk), "SINS"], [("r2", rr)])
                        TT("pool", dst[:, tsl], r1[rr][:, 0:512], r2[rr][:, 0:512], ALU.add, [("r1", rr), ("r2", rr)], [(dtok, tb)])
                        if which == 0:
                            for (dd, XI, nm, xn) in ((qf, XIF, "qf", "XIF"), (qb, XIB, "qb", "XIB")):
                                TT("pool", dd[:, tsl].rearrange("p (j n) -> p j n", j=4), qT[:, tsl].rearrange("p (j n) -> p j n", j=4),
                                   XI[:, h, :].unsqueeze(1).to_broadcast([128, 4, 128]), ALU.mult, [("qT", tb), (xn, h)], [(nm, tb)])
                for tb in range(4):
                    pk = 4 + (tb % 2)
                    tsl = slice(tb * 512, (tb + 1) * 512)
                    for kt in range(8):
                        MM(pbf(pk), wg[:, kt, :], hT[:, kt, tsl], kt == 0, kt == 7, hT_all + ["wg"], [("pb", pk)])
                    ACT(gs[:, tsl], pbf(pk), AF.Silu, [("pb", pk)], [("gs", tb)])
                for g4 in range(4):
                    for jj in range(4):
                        j = g4 * 4 + jj
                        TR(pbb(6)[:, jj * 128:(jj + 1) * 128], kT[:, j * 128:(j + 1) * 128], [("kT", g4)], [("pb", 6)])
                    ACT(Kf[:, g4 * 4:(g4 + 1) * 4, :].rearrange("p j d -> p (j d)"), pbb(6)[:, 0:512], AF.Copy, [("pb", 6), ("ZF", h)], [("Kf", g4)], scale=ZF[:, h:h + 1])
                    ACT(Kb[:, g4 * 4:(g4 + 1) * 4, :].rearrange("p j d -> p (j d)"), pbb(6)[:, 0:512], AF.Copy, [("pb", 6), ("ZB", h)], [("Kb", g4)], scale=ZB[:, h:h + 1])
                ring = [6, 7, 2, 3]
                slot = 0
                for n_ in range(16):
                    for a in range(2):
                        j = n_ if a == 0 else 15 - n_
                        KK = Kf if a == 0 else Kb
                        ktok = "Kf" if a == 0 else "Kb"
                        bk = ring[slot % 4]
                        slot += 1
                        MM(pbf(bk)[:, 0:128], KK[:, j, :], V[:, j, hs], True, True, [(ktok, j // 4), ("V", j)], [("pb", bk)])
                        if n_ == 0:
                            CP("dve", R32[:, a, :], pbf(bk)[:, 0:128], [("pb", bk)], [("R32", a)])
                        else:
                            STT(R32[:, a, :], R32[:, a, :], G128[h], pbf(bk)[:, 0:128], ALU.mult, ALU.add, [("pb", bk), ("R32", a)], [("R32", a)])
                        if a == 0:
                            ACT(Rbf[:, a, j, :], R32[:, a, :], AF.Copy, [("R32", a)], [("Rbf", a, j)])
                        else:
                            CP("pool", Rbf[:, a, j, :], R32[:, a, :], [("R32", a)], [("Rbf", a, j)])

                def scores(j):
                    bk = 6 + (j % 2)
                    jsl = slice(j * 128, (j + 1) * 128)
                    MM(pbf(bk)[:, 0:128], kT[:, jsl], qT[:, jsl], True, True, [("kT", j // 4), ("qT", j // 4)], [("pb", bk)])

                def norm_tail(b4):
                    po = 4 + (b4 % 2)
                    pm = b4 % 2
                    k_ = b4 % 2
                    MM(pbf(pm), onesm, sq[k_][:, 0:512], True, True, [("sq", k_), "onesm"], [("pb", pm)])
                    ACT(sd[k_][:, 0:512], pbf(pm), AF.Ln, [("pb", pm), ("cst", CE)], [("sd", k_)], bias=cst[:, CE:CE + 1])
                    ACT(sd[k_][:, 0:512], sd[k_][:, 0:512], AF.Exp, [("sd", k_)], [("sd", k_)], scale=-0.5)
                    STT(on[k_][:, 0:512], pbf(po), gain[:, h:h + 1], sd[k_][:, 0:512], ALU.mult, ALU.mult, [("pb", po), ("sd", k_), "gain"], [("on", k_)])
                    TT("pool", mixT[:, h, b4 * 512:(b4 + 1) * 512], on[k_][:, 0:512], gs[:, b4 * 512:(b4 + 1) * 512], ALU.mult, [("on", k_), ("gs", b4)], [("mixT", h, b4)])

                scores(0)
                pend = None
                for j in range(16):
                    b4, jj = divmod(j, 4)
                    po = 4 + (b4 % 2)
                    sl = j % 4
                    bk = 6 + (j % 2)
                    jsl = slice(j * 128, (j + 1) * 128)
                    osl = slice(jj * 128, (jj + 1) * 128)
                    if j < 15:
                        scores(j + 1)
                    TT("dve", Sm[:, sl, :], pbf(bk)[:, 0:128], Dm[:, h, :], ALU.mult, [("pb", bk), ("Dm", h)], [("Sm", sl)])
                    nmm = 1 + (1 if j > 0 else 0) + (1 if j < 15 else 0)
                    MM(pbf(po)[:, osl], V[:, j, hs], Sm[:, sl, :], True, nmm == 1, [("V", j), ("Sm", sl)], [("pb", po)])
                    if j > 0:
                        MM(pbf(po)[:, osl], Rbf[:, 0, j - 1, :], qf[:, jsl], False, j == 15, [("Rbf", 0, j - 1), ("qf", b4)], [("pb", po)])
                    if j < 15:
                        MM(pbf(po)[:, osl], Rbf[:, 1, j + 1, :], qb[:, jsl], False, True, [("Rbf", 1, j + 1), ("qb", b4)], [("pb", po)])
                    if pend is not None and j == pend * 4 + 5:
                        norm_tail(pend)
                        pend = None
                    if jj == 3:
                        ACT(sq[b4 % 2][:, 0:512], pbf(po), AF.Square, [("pb", po)], [("sq", b4 % 2)])
                        if pend is not None:
                            norm_tail(pend)
                        pend = b4
                norm_tail(pend)
                if debug == "p1" and h == 0:
                    DEBUG["qT"] = (qT, [128, 2048], BF16, [("qT", t_) for t_ in range(4)])
                    DEBUG["kT"] = (kT, [128, 2048], BF16, [("kT", t_) for t_ in range(4)])
                    DEBUG["Kf"] = (Kf.rearrange("p j d -> p (j d)"), [128, 2048], BF16, [("Kf", t_) for t_ in range(4)])
                    DEBUG["Rbf"] = (Rbf.rearrange("p a j e -> p (a j e)"), [128, 4096], BF16, [("Rbf", a_, j_) for a_ in range(2) for j_ in range(16)])
                    DEBUG["gs"] = (gs, [128, 2048], BF16, [("gs", t_) for t_ in range(4)])

            if debug == "p1":
                DEBUG["ret"] = (mixT[:, 0:4, :].rearrange("p k t -> p (k t)"), [128, 4 * 2048], BF16, [("mixT", h, b) for h in range(4) for b in range(4)])
                DEBUG["hT"] = (hT.rearrange("p k t -> p (k t)"), [128, 8 * 2048], BF16, hT_all)
                DEBUG["V"] = (V.rearrange("p i c -> p (i c)"), [128, 16 * 512], BF16, [("V", i) for i in range(16)])
                break

            S.barrier()
            M.top = PBASE + 32768
            Sx = big[:, PBASE // 2:(PBASE + 32768) // 2].bitcast(F32).rearrange("p (r g j) -> p r g j", r=2, g=16)
            Ub = M.alloc(16384, BF16, "p (g j) -> p g j", g=32)
            WT = M.alloc(8192, BF16, "p (g m) -> p g m", g=32)
            WSm = M.alloc(16384, BF16, "p (r g m) -> p r g m", r=2, g=32)
            WFr = M.alloc(16384, BF16, "p (r g m) -> p r g m", r=2, g=32)
            yg = M.alloc(16384, BF16, "p (k t) -> p k t", k=4)
            YJ = [M.alloc(4096, BF16, "p (t c) -> p t c", t=8) for _ in range(2)]
            R1 = M.top
            wu = M.alloc(8192, BF16, "p (k n) -> p k n", k=8)
            UJ = M.alloc(8192, BF16, "p (g m) -> p g m", g=32)
            M.top = R1
            Fs = M.alloc(16384, BF16, "p (r g j) -> p r g j", r=2, g=16)
            sctA0, sctA1, sctB0, sctB1 = (M.alloc(2048, F32, "p (g j) -> p g j", g=32) for _ in range(4))
            gt = [sctA0.rearrange("p g j -> p (g j)"), sctB0.rearrange("p g j -> p (g j)")]

            DMA("wu", wu, wI_unit(24576, 512), wI_toks, ["wu"])
            DMA("lws", WSm.rearrange("p r g m -> p (r g m)"), s5Wsum, S5W, ["WSm"])
            DMA("lwt", WT.rearrange("p g m -> p (g m)"), s5Wtoep, S5W, ["WT"])
            DMA("lwf", WFr.rearrange("p r g m -> p (r g m)"), s5Wfar, S5W, ["WFr"])
            ev = 0
            for jb in range(2):
                for t in range(8):
                    pk = t % 2
                    base = jb * 1024 + t
                    for kt in range(8):
                        MM(pbf(pk), hT[:, kt, base:(jb + 1) * 1024:8], wu[:, kt, :], kt == 0, kt == 7, hT_all + ["wu"], [("pb", pk)])
                    if ev % 2 == 0:
                        ACT(UJ[:, :, t * 16:(t + 1) * 16], pbf(pk).rearrange("p (g c) -> p g c", g=32), AF.Copy, [("pb", pk)], ["UJ"])
                    else:
                        CP("dve", UJ[:, :, t * 16:(t + 1) * 16], pbf(pk).rearrange("p (g c) -> p g c", g=32), [("pb", pk)], ["UJ"])
                    ev += 1
                for g8 in range(4):
                    pk = 2 + g8 % 2
                    for gg in range(8):
                        TR(pbb(pk)[:, gg * 128:(gg + 1) * 128], UJ[:, g8 * 8 + gg, :], ["UJ"], [("pb", pk)])
                    if g8 % 2 == 0:
                        ACT(Ub[:, g8 * 8:(g8 + 1) * 8, jb * 128:(jb + 1) * 128], pbb(pk).rearrange("p (g j) -> p g j", g=8), AF.Copy, [("pb", pk)], [("Ub", g8)])
                    else:
                        CP("dve", Ub[:, g8 * 8:(g8 + 1) * 8, jb * 128:(jb + 1) * 128], pbb(pk).rearrange("p (g j) -> p g j", g=8), [("pb", pk)], [("Ub", g8)])

            H0, H1 = slice(0, 64), slice(64, 128)
            SXT = [("Sx", "dve"), ("Sx", "pool")]
            tmpA = [x.rearrange("p (r g) j -> p r g j", r=2) for x in (sctA0, sctA1)]
            tmpB = [x.rearrange("p (r g) j -> p r g j", r=2) for x in (sctB0, sctB1)]

            LANES = (("dve", 0, 10), ("pool", 10, 16))

            def upd(jd, js, p_, gh, n, ts):
                for (eng, g0, g1) in LANES:
                    ng = g1 - g0
                    gsl = slice(gh * 16 + g0, gh * 16 + g1)
                    lsl = slice(g0, g1)
                    dst = Sx[:, :, lsl, jd]
                    srcv = Sx[:, :, lsl, js]
                    swp = Sx[:, ::-1, lsl, js]
                    if n is None:
                        Ka = MPr[:, p_, gsl].unsqueeze(1).to_broadcast([128, 2, ng])
                        Kb = MPB[:, p_, :, gsl]
                        t0 = tmpA[ts][:, :, lsl, 0]
                        t1 = tmpB[ts][:, :, lsl, 0]
                    else:
                        Ka = MPr[:, p_, gsl].unsqueeze(1).unsqueeze(3).to_broadcast([128, 2, ng, n])
                        Kb = MPB[:, p_, :, gsl].unsqueeze(3).to_broadcast([128, 2, ng, n])
                        t0 = tmpA[ts][:, :, lsl, 0:n]
                        t1 = tmpB[ts][:, :, lsl, 0:n]
                    sx = [("Sx", eng)]
                    TT(eng, t0, srcv, Ka, ALU.mult, sx + MPT, [("tA", ts, eng)])
                    TT(eng, t1, swp, Kb, ALU.mult, sx + MPT, [("tB", ts, eng)])
                    TT(eng, dst, dst, t0, ALU.add, [("tA", ts, eng)], sx)
                    TT(eng, dst, dst, t1, ALU.add, [("tB", ts, eng)], sx)

            def s5_scan(gh):
                for gl in range(16):
                    g = gh * 16 + gl
                    pk = 4 + gl % 2
                    MM(pbf(pk)[:, 0:256], WSm[:, 0, g, :], Ub[:, g, :], True, True, [("Ub", g // 8), "WSm"], [("pb", pk)])
                    MM(pbf(pk)[:, 256:512], WSm[:, 1, g, :], Ub[:, g, :], True, True, [("Ub", g // 8), "WSm"], [("pb", pk)])
                    ACT(Sx[H0, :, gl, :], pbf(pk)[H0, :].rearrange("p (r j) -> p r j", r=2), AF.Copy, [("pb", pk)], SXT + hT_all)
                    CP("dve", Sx[H1, :, gl, ::-1], pbf(pk)[H1, :].rearrange("p (r j) -> p r j", r=2), [("pb", pk)], SXT + hT_all)
                for j1 in range(1, 16):
                    upd(slice(j1, 256, 16), slice(j1 - 1, 256, 16), 1, gh, 16, j1 % 2)
                for k_, p_ in enumerate((16, 17, 18, 19)):
                    sft = 1 << k_
                    upd(slice(16 * sft + 15, 256, 16), slice(15, 256 - 16 * sft, 16), p_, gh, 16 - sft, k_ % 2)
                for j1 in range(15):
                    upd(slice(16 + j1, 256, 16), slice(15, 240, 16), j1 + 1, gh, 15, j1 % 2)

            def s5_fs(gh):
                FST = ["Fs", "wu", "UJ"]
                S.op("pool", "memset", dict(ap=Fs[H0, :, :, 0:1], constant=0.0), (), FST)
                S.op("pool", "memset", dict(ap=Fs[H1, :, :, 255:256], constant=0.0), (), FST)
                ACT(Fs[H0, :, :, 1:256], Sx[H0, :, :, 0:255], AF.Copy, SXT, FST)
                CP("dve", Fs[H1, :, :, 0:255], Sx[H1, :, :, 254::-1], SXT, FST)

            def s5_out(gh):
                for jb in range(2):
                    jsl = slice(jb * 128, (jb + 1) * 128)
                    k_ = jb
                    for gl4 in range(4):
                        pk = 6 + gl4 % 2
                        for gg in range(4):
                            gl = gl4 * 4 + gg
                            g = gh * 16 + gl
                            osl = slice(gg * 128, (gg + 1) * 128)
                            MM(pbf(pk)[:, osl], Ub[:, g, jsl], WT[:, g, :], True, False, [("Ub", g // 8), "WT"], [("pb", pk)])
                            MM(pbf(pk)[:, osl], Fs[:, 0, gl, jsl], WFr[:, 0, g, :], False, False, ["Fs", "WFr"], [("pb", pk)])
                            MM(pbf(pk)[:, osl], Fs[:, 1, gl, jsl], WFr[:, 1, g, :], False, True, ["Fs", "WFr"], [("pb", pk)])
                        if gl4 % 2 == 0:
                            ACT(YJ[k_][:, :, gl4 * 64:(gl4 + 1) * 64].rearrange("p t (g c) -> p g t c", g=4),
                                pbf(pk).rearrange("p (g t c) -> p g t c", g=4, t=8), AF.Copy, [("pb", pk)], [("YJ", k_)])
                        else:
                            CP("dve", YJ[k_][:, :, gl4 * 64:(gl4 + 1) * 64].rearrange("p t (g c) -> p g t c", g=4),
                               pbf(pk).rearrange("p (g t c) -> p g t c", g=4, t=8), [("pb", pk)], [("YJ", k_)])
                    for cth in range(2):
                        ct = gh * 2 + cth
                        pk = 2 + cth
                        for t in range(8):
                            TR(pbb(pk)[:, t * 128:(t + 1) * 128], YJ[k_][:, t, cth * 128:(cth + 1) * 128], [("YJ", k_)], [("pb", pk)])
                        ACT(yg[:, ct, jb * 1024:(jb + 1) * 1024], pbb(pk), AF.Gelu_apprx_tanh, [("pb", pk)], [("yg", ct)])

            s5_scan(0)
            s5_fs(0)
            s5_scan(1)
            s5_out(0)
            s5_fs(1)
            s5_out(1)
            for ct in range(4):
                for tb in range(4):
                    pk = [0, 1, 4, 5][(ct * 4 + tb) % 4]
                    gi = (ct * 4 + tb) % 2
                    tsl = slice(tb * 512, (tb + 1) * 512)
                    for kt in range(4):
                        MM(pbf(pk), gluw[:, kt, ct * 128:(ct + 1) * 128], yg[:, kt, tsl], kt == 0, kt == 3, [("yg", k_) for k_ in range(4)] + ["gluw"], [("pb", pk)])
                    ACT(gt[gi][:, 0:512], pbf(pk), AF.Sigmoid, [("pb", pk), "glub"], [("gt", gi), ("tA", 0, "dve"), ("tB", 0, "dve"), ("tA", 0, "pool"), ("tB", 0, "pool")], bias=glub[:, ct:ct + 1])
                    jb_, t0_ = tb // 2, 4 * (tb % 2)
                    TT("pool", mixT[:, 4 + ct, jb_ * 1024:(jb_ + 1) * 1024].rearrange("p (j t) -> p t j", t=8)[:, t0_:t0_ + 4, :],
                       gt[gi][:, 0:512].rearrange("p (t j) -> p t j", t=4), yg[:, ct, tsl].rearrange("p (t j) -> p t j", t=4), ALU.mult,
                       [("gt", gi), ("yg", ct)], [("mixT", 4 + ct, jb_ * 2), ("mixT", 4 + ct, jb_ * 2 + 1)])
            if debug == "p2":
                DEBUG["yg"] = (yg.rearrange("p k t -> p (k t)"), [128, 4 * 2048], BF16, [("yg", c_) for c_ in range(4)])
                DEBUG["mix"] = (mixT.rearrange("p k t -> p (k t)"), [128, 8 * 2048], BF16, [("mixT", k_, b_) for k_ in range(8) for b_ in range(4)])
                DEBUG["Ub"] = (Ub.rearrange("p g j -> p (g j)"), [128, 32 * 256], BF16, [("Ub", g_) for g_ in range(4)])
                DEBUG["Sx"] = (Sx.rearrange("p r g j -> p (r g j)"), [128, 2 * 16 * 256], F32, SXT)
                break

            S.barrier()
            M.top = PBASE
            h2T = M.alloc(32768, BF16, "p (k t) -> p k t", k=8)
            x1 = M.alloc(65536, F32, "p (i c) -> p i c", i=16)
            P3 = M.top
            xt = [M.alloc(4096, F32) for _ in range(2)]
            hbs = [M.alloc(2048) for _ in range(2)]
            junk = M.alloc(2048)
            wo = M.alloc(16384, BF16, "p (k n) -> p k n", k=8)
            DMA("wo", wo, wO.rearrange("(k p) c -> p k c", p=128), wO_toks, ["wo"])
            mix_all = [("mixT", k_, b_) for k_ in range(8) for b_ in range(4)]
            pend_ev = None
            pend_tr = None
            ev_box = [None]

            def tr3a(i_, b_):
                pk_ = 4 + (i_ % 2)
                for kt in range(8):
                    TR(pbb(pk_)[:, kt * 128:(kt + 1) * 128], hbs[b_][:, kt * 128:(kt + 1) * 128], [("hb", b_)], [("pb", pk_)])
                if ev_box[0] is not None:
                    pi_, pp_ = ev_box[0]
                    ACT(h2T[:, :, pi_ * 128:(pi_ + 1) * 128], pbb(pp_).rearrange("p (k t) -> p k t", k=8), AF.Copy, [("pb", pp_)], [("h2T", pi_)])
                ev_box[0] = (i_, pk_)

            for i in range(16):
                b = i % 2
                DMA("xt%d" % b, xt[b], x_d[s, i * 128:(i + 1) * 128, :], (), [("xt", b)])
                for hf in range(2):
                    pk = (i % 2) * 2 + hf
                    hsl = slice(hf * 512, (hf + 1) * 512)
                    for kt in range(8):
                        MM(pbf(pk), mixT[:, kt, i * 128:(i + 1) * 128], wo[:, kt, hsl], kt == 0, kt == 7, mix_all + ["wo"], [("pb", pk)])
                    TT("dve", x1[:, i, hsl], pbf(pk), xt[b][:, hsl], ALU.add, [("pb", pk), ("xt", b)], [("x1", i, hf)])
                ACT(junk, x1[:, i, :], AF.Square, [("x1", i, 0), ("x1", i, 1)], ["junk", ("ss2", i)], accum_out=stat[:, 3, i:i + 1])
                ACT(stat[:, 4, i:i + 1], stat[:, 3, i:i + 1], AF.Sqrt, [("ss2", i), ("cst", CE)], [("sd2", i)], scale=1.0 / DM, bias=cst[:, CE:CE + 1])
                RCP(stat[:, 5, i:i + 1], stat[:, 4, i:i + 1], [("sd2", i)], [("rs2", i)])
                hb = hbs[b]
                TS("dve", hb, x1[:, i, :], stat[:, 5, i:i + 1], None, ALU.mult, None, [("x1", i, 0), ("x1", i, 1), ("rs2", i)], [("hb", b)])
                if pend_tr is not None:
                    tr3a(*pend_tr)
                pend_tr = (i, b)
            tr3a(*pend_tr)
            ACT(h2T[:, :, ev_box[0][0] * 128:(ev_box[0][0] + 1) * 128], pbb(ev_box[0][1]).rearrange("p (k t) -> p k t", k=8), AF.Copy, [("pb", ev_box[0][1])], [("h2T", ev_box[0][0])])
            h2T_all = [("h2T", i) for i in range(16)]
            if debug == "p3a":
                DEBUG["x1"] = (x1.rearrange("p i c -> p (i c)"), [128, 16 * 1024], F32, [("x1", i, hf) for i in range(16) for hf in range(2)])
                break

            S.barrier()
            M.top = P3
            aT = mixT_flat[:, 0:NCT * 512].rearrange("p (c t) -> p c t", c=NCT)
            accf = mixT_flat[:, NCT * 512:NCT * 512 + 4096].bitcast(F32)
            acc = [[accf[:, (sl * 2 + vg) * 512:(sl * 2 + vg + 1) * 512] for vg in range(2)] for sl in range(2)]
            NWU = 3
            wup = [M.alloc(4096, BF16, "p (k n) -> p k n", k=8) for _ in range(NWU)]
            wdn = M.alloc(NCT * 1024, BF16, "p (c n) -> p c n", c=NCT)
            dcnt = 0
            glb = [M.alloc(2048, F32) for _ in range(2)]
            yo = [M.alloc(4096, F32)]
            yo.append(yo[0])
            junk = glb[1].bitcast(BF16)
            oi = 0
            wuc = 0
            def ffn_tail(sl, c):
                ACT(glb[sl][:, 0:512], acc[sl][1], AF.Gelu_apprx_tanh, [("acc", sl, 1)], [("glb", sl)])
                TT("pool", aT[:, c, :], glb[sl][:, 0:512], acc[sl][0], ALU.mult, [("glb", sl), ("acc", sl, 0)], [("aT", c)])

            ffn_pend = None
            for q4 in range(4):
                t0 = q4 * 512
                tsl = slice(t0, t0 + 512)
                for c in range(NCT):
                    sl = c % 2
                    ws = wuc % NWU
                    wuc += 1
                    DMA("wup%d" % ws, wup[ws], wU[:, c * 2048:(c + 1) * 2048].rearrange("p (k n) -> p k n", k=8), wU_toks, [("wup", ws)])
                    hb_ = 6 + sl
                    for vg in range(2):
                        pk = sl * 2 + vg
                        wsl = slice(vg * 128, (vg + 1) * 128)
                        for kt in range(8):
                            MM(pbf(pk), wup[ws][:, kt, wsl], h2T[:, kt, tsl], kt == 0, kt == 7, h2T_all + [("wup", ws)], [("pb", pk)])
                    for vg in range(2):
                        wsl = slice(vg * 128, (vg + 1) * 128)
                        hoff = vg * 2
                        if 0 < q4 < 3:
                            for kt in range(8):
                                MM(pbf(hb_)[:, hoff:hoff + 2], wup[ws][:, kt, wsl], h2T[:, kt, t0 - 1:t0 + 513:513], kt == 0, kt == 7, h2T_all + [("wup", ws)], [("pb", hb_)])
                        elif q4 > 0:
                            for kt in range(8):
                                MM(pbf(hb_)[:, hoff:hoff + 1], wup[ws][:, kt, wsl], h2T[:, kt, t0 - 1:t0], kt == 0, kt == 7, h2T_all + [("wup", ws)], [("pb", hb_)])
                        else:
                            for kt in range(8):
                                MM(pbf(hb_)[:, hoff + 1:hoff + 2], wup[ws][:, kt, wsl], h2T[:, kt, t0 + 512:t0 + 513], kt == 0, kt == 7, h2T_all + [("wup", ws)], [("pb", hb_)])
                    for vg in range(2):
                        pk = sl * 2 + vg
                        hoff = vg * 2
                        ch = vg * NCT + c
                        A_ = acc[sl][vg]
                        atok = ("acc", sl, vg)
                        ACT(A_, pbf(pk), AF.Identity, [("pb", pk), "cw", "cb"], [atok], scale=cw[:, ch, 1:2], bias=cb[:, ch:ch + 1])
                        if q4 > 0:
                            ACT(A_[:, 0:1], pbf(hb_)[:, hoff:hoff + 1], AF.Identity, [("pb", hb_), atok, "cw"], [atok], scale=cw[:, ch, 0:1], bias=A_[:, 0:1])
                        if q4 < 3:
                            ACT(A_[:, 511:512], pbf(hb_)[:, hoff + 1:hoff + 2], AF.Identity, [("pb", hb_), atok, "cw"], [atok], scale=cw[:, ch, 2:3], bias=A_[:, 511:512])
                    for vg in range(2):
                        pk = sl * 2 + vg
                        ch = vg * NCT + c
                        A_ = acc[sl][vg]
                        atok = ("acc", sl, vg)
                        STT(A_[:, 1:512], pbf(pk)[:, 0:511], cw[:, ch, 0:1], A_[:, 1:512], ALU.mult, ALU.add, [("pb", pk), atok, "cw"], [atok])
                        STT(A_[:, 0:511], pbf(pk)[:, 1:512], cw[:, ch, 2:3], A_[:, 0:511], ALU.mult, ALU.add, [("pb", pk), atok, "cw"], [atok])
                    if ffn_pend is not None:
                        ffn_tail(*ffn_pend)
                    ffn_pend = (sl, c)
                ffn_tail(*ffn_pend)
                ffn_pend = None
                for hf in range(2):
                    hsl = slice(hf * 512, (hf + 1) * 512)
                    for c in range(NCT):
                        DMA("wdn%d" % c, wdn[:, c, :], wD[c * 128:(c + 1) * 128, hsl], wD_toks, [("wdn", c)])
                    for tt in range(4):
                        i = q4 * 4 + tt
                        pk = 4 + dcnt % 2
                        dcnt += 1
                        for c in range(NCT):
                            MM(pbf(pk), aT[:, c, tt * 128:(tt + 1) * 128], wdn[:, c, :], c == 0, c == NCT - 1, [("aT", c), ("wdn", c)], [("pb", pk)])
                        TT("dve", x1[:, i, hsl], pbf(pk), x1[:, i, hsl], ALU.add, [("pb", pk), ("x1", i, hf)], [("x1", i, hf)])
                        if hf == 1:
                            o_ = oi % 2
                            oi += 1
                            ACT(junk, x1[:, i, :], AF.Square, [("x1", i, 0), ("x1", i, 1)], ["junk", "ss3", ("glb", 1)], accum_out=stat3[:, 0:1])
                            ACT(stat3[:, 1:2], stat3[:, 0:1], AF.Sqrt, ["ss3", ("cst", CE)], ["sd3"], scale=1.0 / DM, bias=cst[:, CE:CE + 1])
                            RCP(stat3[:, 2:3], stat3[:, 1:2], ["sd3"], ["rs3"])
                            STT(yo[o_], x1[:, i, :], stat3[:, 2:3], nfbc, ALU.mult, ALU.mult, [("x1", i, 0), ("x1", i, 1), "rs3", "nfbc"], [("yo", 0)])
                            DMA("yo0", y_d[s, i * 128:(i + 1) * 128, :], yo[o_], [("yo", 0)], [("y", s, i)])

        fin = []
        if debug == "p3b":
            fin = [("y", 0, i) for i in range(16)]
        elif debug:
            for nm, (ap, shape, dt, toks) in DEBUG.items():
                dd = nc.dram_tensor("dbg_" + nm, list(shape), dt, kind="ExternalOutput").ap()
                dbg_out[nm] = (shape, dt)
                DMA("dbg_" + nm, dd, ap, toks, [("dbg", nm)])
                fin.append(("dbg", nm))
        else:
            fin = [("y", s, i) for s in range(NSEQ) for i in range(16)]
        NOP("sp", fin)
        S.emit()
        print("SBUF high water", M.hw, "instr counts", {e: len(v) for e, v in S.streams.items()})
    return nc, dbg_out


def _prep_weights(inp):
    f = np.float32
    w_in = np.asarray(inp["w_in"], f)[0]
    q, k, v, g, u = (w_in[:, i * 512:(i + 1) * 512] for i in range(5))

    def swap(w):
        w4 = w.reshape(DM, 4, 2, 64)
        return np.ascontiguousarray(w4[:, :, ::-1, :]).reshape(DM, 512)

    def inter(w):
        a = w.reshape(DM, 4, 1, 128)
        b = swap(w).reshape(DM, 4, 1, 128)
        return np.concatenate([a, b], axis=2).reshape(DM, 1024)

    w_in_a = np.ascontiguousarray(np.concatenate([inter(q), inter(k), g, v, u], axis=1))

    def col(vec, n):
        return np.ascontiguousarray(np.asarray(vec, f).reshape(n, 128).T)

    lre = np.asarray(inp["s5_lambda_re"], f)[0]
    lim = np.asarray(inp["s5_lambda_im"], f)[0]
    ldt = np.asarray(inp["s5_log_dt"], f)[0]

    def st2(a):
        return np.ascontiguousarray(a.transpose(0, 2, 1).reshape(128, 32))

    ldt_e = np.ascontiguousarray(np.broadcast_to(ldt[:, :, None], (2, 32, 64)))
    Bre = np.asarray(inp["s5_B_re"], f)[0]
    Bim = np.asarray(inp["s5_B_im"], f)[0]
    Cre = np.asarray(inp["s5_C_re"], f)[0]
    Cim = np.asarray(inp["s5_C_im"], f)[0]

    def b2(B):
        return np.ascontiguousarray(B.transpose(0, 2, 1, 3).reshape(128, 512))

    def c2(C):
        return np.ascontiguousarray(C.transpose(0, 3, 1, 2).reshape(128, 512))

    dsk = np.asarray(inp["s5_D"], f)[0]
    drep = np.ascontiguousarray(np.tile(dsk.reshape(32, 16).T, (8, 1)))

    conv_w = np.asarray(inp["conv_w"], f)[0]
    cwc = np.ascontiguousarray(conv_w.reshape(3, 44, 128).transpose(2, 1, 0).reshape(128, 132))
    d = {
        "w_in_a": w_in_a,
        "w_out": np.ascontiguousarray(np.asarray(inp["w_out"], f)[0]),
        "w_up": np.ascontiguousarray(np.asarray(inp["w_up"], f)[0]),
        "w_down": np.ascontiguousarray(np.asarray(inp["w_down"], f)[0]),
        "glu_w": np.ascontiguousarray(np.asarray(inp["s5_glu_w"], f)[0]),
        "nmix_c": col(np.asarray(inp["norm_mix"])[0], 8),
        "nffn_c": col(np.asarray(inp["norm_ffn"])[0], 8),
        "nfin_r": np.ascontiguousarray(np.asarray(inp["norm_final"], f).reshape(1, DM)),
        "gain_c": col(np.asarray(inp["ret_gn_gain"])[0], 4),
        "glub_c": col(np.asarray(inp["s5_glu_b"])[0], 4),
        "cw_c": cwc,
        "cb_c": col(np.asarray(inp["conv_b"])[0], 44),
        "lre2": st2(lre), "lim2": st2(lim), "ldt2": st2(ldt_e),
        "bre2": b2(Bre), "bim2": b2(Bim), "cre2": c2(Cre), "cim2": c2(Cim),
        "drep": drep,
    }
    return d


_CACHE = {}


def kernel(**inputs):
    xp = np.asarray(inputs["x_prompt"], np.float32)
    xs = np.asarray(inputs["x_sample"], np.float32)
    xall = np.concatenate([xp, xs], axis=0)
    wd = _prep_weights(inputs)
    if "nc" not in _CACHE:
        _CACHE["nc"] = build_program(False)[0]
    nc = _CACHE["nc"]
    in_maps = []
    for c in range(NCORES):
        m = dict(wd)
        m["x"] = np.ascontiguousarray(xall[c * NSEQ:(c + 1) * NSEQ])
        in_maps.append(m)
    res = run_bass_kernel_spmd(nc, in_maps, core_ids=list(range(NCORES)))
    yall = np.concatenate([np.asarray(r["y"], np.float32) for r in res.results], axis=0)
    return (np.ascontiguousarray(yall[0:8]), np.ascontiguousarray(yall[8:24]))
```
